# Optimizing a Trainium2 kernel written in Bass

```python
import math
import jax, jax.numpy as jnp
from jax import lax
import numpy as np

D_MODEL = 1024
BATCH = 2
SEQ = 8192
DEPTH = 2
DEC_BATCH = 16
DEC_SEQ = 64
PAST_LEN = 2048

CHUNK = 64
GLA_HEADS = 4
GLA_DK = 32
GLA_DV = 64
GLA_RANK = 16
GLA_TAU = 16.0
S5_GROUPS = 16
S5_CH = 16
S5_STATE = 64
S5_DT_MIN = 0.001
S5_DT_MAX = 0.1
FOX_HEADS = 8
FOX_HD = 64
FOX_BIAS_INIT = 4.0
Q_BLOCK = 128
MEM_TOKENS = 256
MEM_HEADS = 4
MEM_HD = 256
D_FF = 2816
CONV_W = 3
LN_EPS = 1e-5
DN_ALPHA = (2 * DEPTH) ** 0.25
DN_BETA = (8 * DEPTH) ** -0.25

GLA_W = GLA_HEADS * GLA_DV
S5_W = S5_GROUPS * S5_CH
FOX_W = FOX_HEADS * FOX_HD
MIX_W = GLA_W + S5_W + FOX_W
IN_SPLITS = (GLA_HEADS * GLA_DK, GLA_HEADS * GLA_DK, GLA_W, GLA_W, GLA_RANK, S5_W, FOX_W, FOX_W, FOX_W, FOX_HEADS)
IN_COL_SCALE = (1.0, 1.0, DN_BETA, 1.0, 1.0, 1.0, 1.0, 1.0, DN_BETA, 1.0)
N_IN = sum(IN_SPLITS)

kernel_name = 'hybrid_gla_s5_fox_streaming_encoder_step'


def _layer_norm(x, g, b):
    xf = x.astype(jnp.float32)
    mu = jnp.mean(xf, axis=-1, keepdims=True)
    var = jnp.mean(jnp.square(xf - mu), axis=-1, keepdims=True)
    y = (xf - mu) * lax.rsqrt(var + LN_EPS) * g.astype(jnp.float32) + b.astype(jnp.float32)
    return y.astype(x.dtype)


def _gla_mix(q, k, v, g_out, a_lr, w_a2, b_a, norm_g, s0):
    f32 = jnp.float32
    bn, t, _ = q.shape
    q = q.reshape(bn, t, GLA_HEADS, GLA_DK).astype(f32) * (GLA_DK ** -0.5)
    k = k.reshape(bn, t, GLA_HEADS, GLA_DK).astype(f32)
    v = v.reshape(bn, t, GLA_HEADS, GLA_DV).astype(f32)
    log_a = jax.nn.log_sigmoid((a_lr @ w_a2 + b_a).astype(f32)) / GLA_TAU
    log_a = log_a.reshape(bn, t, GLA_HEADS, GLA_DK)
    pad = (-t) % CHUNK
    nc = (t + pad) // CHUNK

    def blocks(a):
        a = jnp.pad(a, ((0, 0), (0, pad), (0, 0), (0, 0)))
        return a.reshape(bn, nc, CHUNK, a.shape[2], a.shape[3])

    qc, kc, vc, gc = blocks(q), blocks(k), blocks(v), blocks(log_a)
    cum = jnp.cumsum(gc, axis=2)
    cum_end = cum[:, :, -1:]
    q_dec = qc * jnp.exp(cum)
    k_inv = kc * jnp.exp(-cum)
    k_end = kc * jnp.exp(cum_end - cum)
    causal = jnp.tril(jnp.ones((CHUNK, CHUNK), dtype=bool))
    scores = jnp.where(causal, jnp.einsum('bclhd,bcshd->bchls', q_dec, k_inv), 0.0)
    o = jnp.einsum('bchls,bcshe->bclhe', scores, vc)
    ds = jnp.einsum('bclhd,bclhe->bchde', k_end, vc)
    decay = jnp.exp(cum_end[:, :, 0])

    def step(s, inp):
        dec, d_s = inp
        return dec[..., None] * s + d_s, s

    s_fin, s_prev = lax.scan(step, s0.astype(f32), (jnp.moveaxis(decay, 1, 0), jnp.moveaxis(ds, 1, 0)))
    o = o + jnp.einsum('bclhd,cbhde->bclhe', q_dec, s_prev)
    o = o.reshape(bn, nc * CHUNK, GLA_HEADS, GLA_DV)[:, :t]
    o = o * lax.rsqrt(jnp.mean(jnp.square(o), axis=-1, keepdims=True) + LN_EPS) * norm_g.astype(f32)
    out = o.reshape(bn, t, GLA_W) * jax.nn.silu(g_out.astype(f32))
    return out.astype(g_out.dtype), s_fin


def _complex_affine_combine(e1, e2):
    a1r, a1i, b1r, b1i = e1
    a2r, a2i, b2r, b2i = e2
    return (a2r * a1r - a2i * a1i,
            a2r * a1i + a2i * a1r,
            a2r * b1r - a2i * b1i + b2r,
            a2r * b1i + a2i * b1r + b2i)


def _s5_mix(u, lam_re, lam_im, log_dt, b_re, b_im, c_re, c_im, d_skip, w_glu, b_glu, h0_re, h0_im):
    f32 = jnp.float32
    bn, t, _ = u.shape
    lam_re, lam_im = lam_re.astype(f32), lam_im.astype(f32)
    dt = jnp.exp(log_dt.astype(f32))
    mag = jnp.exp(lam_re * dt)
    ab_re = mag * jnp.cos(lam_im * dt)
    ab_im = mag * jnp.sin(lam_im * dt)
    den = jnp.square(lam_re) + jnp.square(lam_im)
    f_re = ((ab_re - 1.0) * lam_re + ab_im * lam_im) / den
    f_im = (ab_im * lam_re - (ab_re - 1.0) * lam_im) / den
    b_re, b_im = b_re.astype(f32), b_im.astype(f32)
    bb_re = f_re[..., None] * b_re - f_im[..., None] * b_im
    bb_im = f_re[..., None] * b_im + f_im[..., None] * b_re
    uf = u.astype(f32).reshape(bn, t, S5_GROUPS, S5_CH)
    bu_re = jnp.einsum('gph,btgh->btgp', bb_re, uf)
    bu_im = jnp.einsum('gph,btgh->btgp', bb_im, uf)
    a_re = jnp.broadcast_to(ab_re, bu_re.shape)
    a_im = jnp.broadcast_to(ab_im, bu_im.shape)
    pr, pi, xr, xi = lax.associative_scan(_complex_affine_combine, (a_re, a_im, bu_re, bu_im), axis=1)
    h0r = h0_re.astype(f32)[:, None]
    h0i = h0_im.astype(f32)[:, None]
    xr = xr + pr * h0r - pi * h0i
    xi = xi + pr * h0i + pi * h0r
    y = jnp.einsum('ghp,btgp->btgh', c_re.astype(f32), xr) - jnp.einsum('ghp,btgp->btgh', c_im.astype(f32), xi)
    y = y.reshape(bn, t, S5_W) + d_skip.astype(f32) * uf.reshape(bn, t, S5_W)
    z = jax.nn.gelu(y)
    out = z * jax.nn.sigmoid(z @ w_glu.astype(f32) + b_glu.astype(f32))
    return out.astype(u.dtype), xr[:, -1], xi[:, -1]


def _fox_attend(q, k, v, cf_q, cf_k, q_pos, k_pos):
    bn, tq = q.shape[0], q.shape[1]
    blk = Q_BLOCK if tq % Q_BLOCK == 0 else tq
    nb = tq // blk
    qb = q.reshape(bn, nb, blk, FOX_HEADS, FOX_HD).swapaxes(0, 1)
    fqb = cf_q.reshape(bn, nb, blk, FOX_HEADS).swapaxes(0, 1)
    pb = q_pos.reshape(nb, blk)
    fk_t = cf_k.transpose(0, 2, 1)[:, :, None, :]

    def one_block(args):
        qi, fi, pi = args
        s = jnp.einsum('bqhd,bkhd->bhqk', qi, k, preferred_element_type=jnp.float32) * (FOX_HD ** -0.5)
        s = s + fi.transpose(0, 2, 1)[..., None] - fk_t
        s = jnp.where(k_pos[None, :] <= pi[:, None], s, -jnp.inf)
        p = jax.nn.softmax(s, axis=-1)
        return jnp.einsum('bhqk,bkhd->bqhd', p.astype(v.dtype), v)

    o = lax.map(one_block, (qb, fqb, pb))
    return o.swapaxes(0, 1).reshape(bn, tq, FOX_W)


def _mem_attend(x, mem_k, mem_v, w_q, w_o):
    bn, t, _ = x.shape
    q = (x @ w_q).reshape(bn, t, MEM_HEADS, MEM_HD)
    s = jnp.einsum('bthd,bmhd->bhtm', q, mem_k.astype(q.dtype), preferred_element_type=jnp.float32) * (MEM_HD ** -0.5)
    p = jax.nn.softmax(s, axis=-1)
    o = jnp.einsum('bhtm,bmhd->bthd', p.astype(x.dtype), mem_v.astype(x.dtype))
    return o.reshape(bn, t, MEM_HEADS * MEM_HD) @ w_o


def _conv_ffn(x, w_up, conv_w, conv_b, w_down, prev):
    t = x.shape[1]
    u = x @ w_up
    ext = jnp.concatenate([prev.astype(u.dtype), u], axis=1)
    y = conv_b
    for j in range(CONV_W):
        y = y + conv_w[j] * ext[:, j:j + t]
    a, g = jnp.split(y, 2, axis=-1)
    h = jax.nn.gelu(a) * g
    return h @ w_down, ext[:, t:]


def _hybrid_layer(x, mem_k, mem_v, gla_s0, s5_h0_re, s5_h0_im, past_k, past_v, past_logf, conv_prev, p):
    f32 = jnp.float32
    bn, t, _ = x.shape
    proj = x @ p['w_in']
    offsets = np.cumsum(IN_SPLITS)[:-1].tolist()
    g_q, g_k, g_v, g_o, g_a, s_u, f_q, f_k, f_v, f_f = jnp.split(proj, offsets, axis=-1)
    o_gla, gla_s = _gla_mix(g_q, g_k, g_v, g_o, g_a, p['gla_w_a2'], p['gla_b_a'], p['gla_norm_g'], gla_s0)
    o_s5, s5_re, s5_im = _s5_mix(s_u, p['s5_lam_re'], p['s5_lam_im'], p['s5_log_dt'], p['s5_b_re'], p['s5_b_im'],
                                 p['s5_c_re'], p['s5_c_im'], p['s5_d'], p['s5_w_glu'], p['s5_b_glu'], s5_h0_re, s5_h0_im)
    q = f_q.reshape(bn, t, FOX_HEADS, FOX_HD)
    k_new = f_k.reshape(bn, t, FOX_HEADS, FOX_HD)
    v_new = f_v.reshape(bn, t, FOX_HEADS, FOX_HD)
    logf = jax.nn.log_sigmoid((f_f + p['fox_b_f']).astype(f32))
    if past_k is None:
        k_all, v_all, logf_all = k_new, v_new, logf
    else:
        k_all = jnp.concatenate([past_k.astype(k_new.dtype), k_new], axis=1)
        v_all = jnp.concatenate([past_v.astype(v_new.dtype), v_new], axis=1)
        logf_all = jnp.concatenate([past_logf.astype(f32), logf], axis=1)
    past = k_all.shape[1] - t
    cum_f = jnp.cumsum(logf_all, axis=1)
    o_fox = _fox_attend(q, k_all, v_all, cum_f[:, past:], cum_f, past + jnp.arange(t), jnp.arange(past + t))
    mixed = jnp.concatenate([o_gla, o_s5, o_fox.astype(x.dtype)], axis=-1) @ p['w_mix_out']
    x = _layer_norm(DN_ALPHA * x + mixed, p['ln1_g'], p['ln1_b'])
    x = _layer_norm(DN_ALPHA * x + _mem_attend(x, mem_k, mem_v, p['mem_w_q'], p['mem_w_o']), p['ln2_g'], p['ln2_b'])
    f_out, conv_new = _conv_ffn(x, p['ffn_w_up'], p['ffn_conv_w'], p['ffn_conv_b'], p['ffn_w_down'], conv_prev)
    x = _layer_norm(DN_ALPHA * x + f_out, p['ln3_g'], p['ln3_b'])
    return x, (gla_s, s5_re, s5_im, k_new, v_new, logf, conv_new)


def setup_inputs(seed: int = 0) -> dict:
    key = jax.random.key(seed)
    keys = jax.random.split(key, 64)
    cnt = [0]
    f32 = jnp.float32

    def nk():
        kk = keys[cnt[0]]
        cnt[0] += 1
        return kk

    def nrm(shape, scale=1.0):
        return jax.random.normal(nk(), shape, f32) * scale

    L = DEPTH
    col_scale = jnp.concatenate([jnp.full((n,), s, f32) for n, s in zip(IN_SPLITS, IN_COL_SCALE)])
    return {
        'x_prompt': nrm((BATCH, SEQ, D_MODEL)),
        'x_sample': nrm((DEC_BATCH, DEC_SEQ, D_MODEL)),
        'mem_prompt': nrm((BATCH, MEM_TOKENS, D_MODEL)),
        'state_gla': nrm((L, DEC_BATCH, GLA_HEADS, GLA_DK, GLA_DV), 0.3),
        'state_s5_re': nrm((L, DEC_BATCH, S5_GROUPS, S5_STATE), 0.1),
        'state_s5_im': nrm((L, DEC_BATCH, S5_GROUPS, S5_STATE), 0.1),
        'cache_fox_k': nrm((L, DEC_BATCH, PAST_LEN, FOX_HEADS, FOX_HD)),
        'cache_fox_v': nrm((L, DEC_BATCH, PAST_LEN, FOX_HEADS, FOX_HD), DN_BETA),
        'cache_fox_logf': jax.nn.log_sigmoid(FOX_BIAS_INIT + nrm((L, DEC_BATCH, PAST_LEN, FOX_HEADS))),
        'cache_mem_k': nrm((L, DEC_BATCH, MEM_TOKENS, MEM_HEADS, MEM_HD)),
        'cache_mem_v': nrm((L, DEC_BATCH, MEM_TOKENS, MEM_HEADS, MEM_HD), DN_BETA),
        'state_ffn_conv': nrm((L, DEC_BATCH, CONV_W - 1, 2 * D_FF), DN_BETA),
        'ln_in_g': 1.0 + nrm((D_MODEL,), 0.02),
        'ln_in_b': nrm((D_MODEL,), 0.02),
        'w_in': nrm((L, D_MODEL, N_IN), D_MODEL ** -0.5) * col_scale,
        'gla_w_a2': nrm((L, GLA_RANK, GLA_HEADS * GLA_DK), GLA_RANK ** -0.5),
        'gla_b_a': nrm((L, GLA_HEADS * GLA_DK), 0.1),
        'gla_norm_g': 1.0 + nrm((L, GLA_DV), 0.02),
        's5_lam_re': -0.5 + nrm((L, S5_GROUPS, S5_STATE), 0.01),
        's5_lam_im': math.pi * jnp.arange(S5_STATE, dtype=f32) + nrm((L, S5_GROUPS, S5_STATE), 0.01),
        's5_log_dt': jax.random.uniform(nk(), (L, S5_GROUPS, S5_STATE), f32, math.log(S5_DT_MIN), math.log(S5_DT_MAX)),
        's5_b_re': nrm((L, S5_GROUPS, S5_STATE, S5_CH), (2 * S5_CH) ** -0.5),
        's5_b_im': nrm((L, S5_GROUPS, S5_STATE, S5_CH), (2 * S5_CH) ** -0.5),
        's5_c_re': nrm((L, S5_GROUPS, S5_CH, S5_STATE), S5_STATE ** -0.5),
        's5_c_im': nrm((L, S5_GROUPS, S5_CH, S5_STATE), S5_STATE ** -0.5),
        's5_d': nrm((L, S5_W)),
        's5_w_glu': nrm((L, S5_W, S5_W), S5_W ** -0.5),
        's5_b_glu': nrm((L, S5_W), 0.02),
        'fox_b_f': FOX_BIAS_INIT + nrm((L, FOX_HEADS), 0.1),
        'w_mix_out': nrm((L, MIX_W, D_MODEL), MIX_W ** -0.5 * DN_BETA),
        'ln1_g': 1.0 + nrm((L, D_MODEL), 0.02),
        'ln1_b': nrm((L, D_MODEL), 0.02),
        'mem_w_q': nrm((L, D_MODEL, MEM_HEADS * MEM_HD), D_MODEL ** -0.5),
        'mem_w_k': nrm((L, D_MODEL, MEM_HEADS * MEM_HD), D_MODEL ** -0.5),
        'mem_w_v': nrm((L, D_MODEL, MEM_HEADS * MEM_HD), D_MODEL ** -0.5 * DN_BETA),
        'mem_w_o': nrm((L, MEM_HEADS * MEM_HD, D_MODEL), (MEM_HEADS * MEM_HD) ** -0.5 * DN_BETA),
        'ln2_g': 1.0 + nrm((L, D_MODEL), 0.02),
        'ln2_b': nrm((L, D_MODEL), 0.02),
        'ffn_w_up': nrm((L, D_MODEL, 2 * D_FF), D_MODEL ** -0.5 * DN_BETA),
        'ffn_conv_w': nrm((L, CONV_W, 2 * D_FF), CONV_W ** -0.5),
        'ffn_conv_b': nrm((L, 2 * D_FF), 0.02),
        'ffn_w_down': nrm((L, D_FF, D_MODEL), D_FF ** -0.5 * DN_BETA),
        'ln3_g': 1.0 + nrm((L, D_MODEL), 0.02),
        'ln3_b': nrm((L, D_MODEL), 0.02),
    }


def reference(x_prompt, x_sample, mem_prompt, state_gla, state_s5_re, state_s5_im, cache_fox_k, cache_fox_v,
              cache_fox_logf, cache_mem_k, cache_mem_v, state_ffn_conv, ln_in_g, ln_in_b, w_in, gla_w_a2, gla_b_a,
              gla_norm_g, s5_lam_re, s5_lam_im, s5_log_dt, s5_b_re, s5_b_im, s5_c_re, s5_c_im, s5_d, s5_w_glu,
              s5_b_glu, fox_b_f, w_mix_out, ln1_g, ln1_b, mem_w_q, mem_w_k, mem_w_v, mem_w_o, ln2_g, ln2_b,
              ffn_w_up, ffn_conv_w, ffn_conv_b, ffn_w_down, ln3_g, ln3_b):
    f32 = jnp.float32
    bp = x_prompt.shape[0]
    hp = _layer_norm(x_prompt, ln_in_g, ln_in_b)
    hs = _layer_norm(x_sample, ln_in_g, ln_in_b)
    zero_gla = jnp.zeros((bp, GLA_HEADS, GLA_DK, GLA_DV), f32)
    zero_s5 = jnp.zeros((bp, S5_GROUPS, S5_STATE), f32)
    zero_conv = jnp.zeros((bp, CONV_W - 1, 2 * D_FF), x_prompt.dtype)
    prompt_states = []
    sample_states = []
    for l in range(DEPTH):
        prm = dict(w_in=w_in[l], gla_w_a2=gla_w_a2[l], gla_b_a=gla_b_a[l], gla_norm_g=gla_norm_g[l],
                   s5_lam_re=s5_lam_re[l], s5_lam_im=s5_lam_im[l], s5_log_dt=s5_log_dt[l], s5_b_re=s5_b_re[l],
                   s5_b_im=s5_b_im[l], s5_c_re=s5_c_re[l], s5_c_im=s5_c_im[l], s5_d=s5_d[l], s5_w_glu=s5_w_glu[l],
                   s5_b_glu=s5_b_glu[l], fox_b_f=fox_b_f[l], w_mix_out=w_mix_out[l], ln1_g=ln1_g[l], ln1_b=ln1_b[l],
                   mem_w_q=mem_w_q[l], mem_w_o=mem_w_o[l], ln2_g=ln2_g[l], ln2_b=ln2_b[l], ffn_w_up=ffn_w_up[l],
                   ffn_conv_w=ffn_conv_w[l], ffn_conv_b=ffn_conv_b[l], ffn_w_down=ffn_w_down[l],
                   ln3_g=ln3_g[l], ln3_b=ln3_b[l])
        mk_p = (mem_prompt @ mem_w_k[l]).reshape(bp, MEM_TOKENS, MEM_HEADS, MEM_HD)
        mv_p = (mem_prompt @ mem_w_v[l]).reshape(bp, MEM_TOKENS, MEM_HEADS, MEM_HD)
        hp, st_p = _hybrid_layer(hp, mk_p, mv_p, zero_gla, zero_s5, zero_s5, None, None, None, zero_conv, prm)
        prompt_states.append(st_p + (mk_p, mv_p))
        hs, st_s = _hybrid_layer(hs, cache_mem_k[l], cache_mem_v[l], state_gla[l], state_s5_re[l], state_s5_im[l],
                                 cache_fox_k[l], cache_fox_v[l], cache_fox_logf[l], state_ffn_conv[l], prm)
        sample_states.append(st_s)
    gla_p, s5_re_p, s5_im_p, fox_k_p, fox_v_p, fox_logf_p, ffn_conv_p, mem_k_p, mem_v_p = [jnp.stack(z) for z in zip(*prompt_states)]
    gla_s, s5_re_s, s5_im_s, fox_k_s, fox_v_s, fox_logf_s, ffn_conv_s = [jnp.stack(z) for z in zip(*sample_states)]
    return (hp, hs, gla_p, s5_re_p, s5_im_p, fox_k_p, fox_v_p, fox_logf_p, ffn_conv_p, mem_k_p, mem_v_p,
            gla_s, s5_re_s, s5_im_s, fox_k_s, fox_v_s, fox_logf_s, ffn_conv_s)
```

```python
import math
from contextlib import ExitStack

import numpy as np
import concourse.bass as bass
import concourse.mybir as mybir
from concourse.bass_utils import run_bass_kernel_spmd

F32 = mybir.dt.float32
BF16 = mybir.dt.bfloat16
ALU = mybir.AluOpType
AF = mybir.ActivationFunctionType
AX = mybir.AxisListType

D = 1024
L = 2
DFF = 2816
NCH_FF = 44
LN_EPS = 1e-5
ALPHA = float((2 * L) ** 0.25)
NCORES = 8


class Buf:
    __slots__ = ("name", "excl", "last_w", "readers")

    def __init__(self, name, excl=False):
        self.name = name
        self.excl = excl
        self.last_w = None
        self.readers = []


class Chan:
    __slots__ = ("sem", "count")

    def __init__(self, sem):
        self.sem = sem
        self.count = 0


class Op:
    __slots__ = ("engine", "fn", "deps", "signal", "sigidx", "chan", "chanval")

    def __init__(self, engine, fn, chan):
        self.engine = engine
        self.fn = fn
        self.deps = []
        self.signal = False
        self.sigidx = 0
        self.chan = chan
        self.chanval = 0


class Sched:
    ENGS = ("pe", "act", "dve", "pool", "sp")

    def __init__(self, nc, stack):
        self.nc = nc
        self.stack = stack
        self.ops = {e: [] for e in self.ENGS}
        self.esem = {e: stack.enter_context(nc.semaphore("es_" + e)) for e in ("pe", "act", "dve", "pool")}
        self.chans = []
        self.nops = 0

    def chan(self, name):
        c = Chan(self.stack.enter_context(self.nc.semaphore("ch_" + name)))
        self.chans.append(c)
        return c

    def add(self, eng, fn, reads=(), writes=(), chan=None):
        op = Op(eng, fn, chan)
        if chan is not None:
            chan.count += 16
            op.chanval = chan.count
        deps = {}
        for b in reads:
            if b.last_w is not None:
                deps[id(b.last_w)] = b.last_w
            if b.excl:
                for r in b.readers:
                    deps[id(r)] = r
        for b in writes:
            if b.last_w is not None:
                deps[id(b.last_w)] = b.last_w
            for r in b.readers:
                deps[id(r)] = r
        for d in deps.values():
            if d is op:
                continue
            if d.chan is None:
                if d.engine == eng and eng == "pe":
                    continue
                d.signal = True
            op.deps.append(d)
        for b in reads:
            if b.excl:
                b.last_w = op
                b.readers = []
            else:
                b.readers.append(op)
        for b in writes:
            b.last_w = op
            b.readers = []
        self.ops[eng].append(op)
        self.nops += 1
        return op

    def emit(self):
        nc = self.nc
        for e in ("pe", "act", "dve", "pool"):
            n = 0
            for op in self.ops[e]:
                if op.signal:
                    n += 1
                    op.sigidx = n
        chans = self.chans
        esem = self.esem

        def run(ename, e):
            waited = {}
            for op in self.ops[ename]:
                need = {}
                for d in op.deps:
                    if d.chan is not None:
                        sem, val = d.chan.sem, d.chanval
                    else:
                        sem, val = esem[d.engine], d.sigidx
                    k = id(sem)
                    if k not in need or need[k][1] < val:
                        need[k] = (sem, val)
                for k, (sem, val) in need.items():
                    if waited.get(k, 0) >= val:
                        continue
                    waited[k] = val
                    e.wait_ge(sem, val)
                inst = op.fn(e)
                if op.chan is not None:
                    inst.then_inc(op.chan.sem, 16)
                elif op.signal:
                    inst.then_inc(esem[ename], 1)
            if ename == "sp":
                for c in chans:
                    if c.count > 0 and waited.get(id(c.sem), 0) < c.count:
                        e.wait_ge(c.sem, c.count)

        with nc.Block() as block:
            @block.tensor
            def _(e):
                run("pe", e)

            @block.scalar
            def _(e):
                run("act", e)

            @block.vector
            def _(e):
                run("dve", e)

            @block.gpsimd
            def _(e):
                run("pool", e)

            @block.sync
            def _(e):
                run("sp", e)


OFF = dict(gq=0, gk=128, gv=256, go=512, ga=768, su=784, fq=1040, fk=1552, fv=2064, ff=2576)

W_STREAM = [("wfm", 8, 568), ("wfq", 8, 544), ("wfk", 8, 544), ("wtv", 8, 512), ("wtk", 8, 512), ("wtf", 8, 520),
            ("s5b", 8, 256), ("s5c", 8, 256), ("s5g", 2, 256),
            ("wmx0", 8, 512), ("wmx1", 8, 512), ("wq0", 8, 512), ("wq1", 8, 512), ("wo0", 8, 512), ("wo1", 8, 512)]
W_STREAM += [("wup%d" % j, 8, 512) for j in range(11)]
W_STREAM += [("wdn%d%d" % (h, g), (8, 8, 6)[g], 512) for h in range(2) for g in range(3)]
W_PACK = [t for t in W_STREAM if t[0] != "s5c"] + [("wk0", 8, 512), ("wk1", 8, 512), ("wv0", 8, 512), ("wv1", 8, 512),
                                                    ("s5cr", 8, 256)]
W_OFF = {}
_o = 0
for _n, _k, _c in W_PACK:
    W_OFF[_n] = (_o, _k, _c)
    _o += _k * _c
W_TOT = _o
SLOT_COLS = 8 * 568

SP = dict(gba=0, fbf=1, s5d=2, s5bg=4, cw0=6, cw1=50, cw2=94, cb=138, lre=182, lim=190, ldt=198, hm=206, hre=210, him=218)
NSP = 226
TP = dict(gng=0, fbf=256)
NTP = 264


def ffn_feat(ci):
    j, r = divmod(ci, 4)
    if r < 2:
        return (2 * j + r) * 128
    return DFF + (2 * j + r - 2) * 128


class Cfg:
    def __init__(self, SEQ=8192, PAST=2048, T=256):
        self.SEQ, self.PAST, self.T = SEQ, PAST, T
        self.NSEG = SEQ // T
        self.NPS = PAST // T
        assert SEQ % T == 0 and PAST % T == 0 and T % 128 == 0


def build(cfg):
    SEQ, PAST, T = cfg.SEQ, cfg.PAST, cfg.T
    NSEG, NPS = cfg.NSEG, cfg.NPS
    SB = 64
    nc = bass.Bass("TRN2", target_bir_lowering=False)
    st = ExitStack()
    sch = Sched(nc, st)
    add = sch.add

    def din(name, shape):
        return nc.dram_tensor(name, list(shape), F32, kind="ExternalInput").ap()

    def dout(name, shape):
        return nc.dram_tensor(name, list(shape), F32, kind="ExternalOutput").ap()

    def dscr(name, shape, dt=BF16):
        return nc.dram_tensor(name, list(shape), dt, kind="Internal").ap()

    def sb(name, shape, dt=F32):
        return st.enter_context(nc.sbuf_tensor("s_" + name, list(shape), dt))

    i_xp = din("xp", [SEQ, D])
    i_xs = din("xs", [2, 64, D])
    i_memT = din("memT", [128, 8 * 256])
    i_stgla = din("stgla", [L, 2, 128, 64])
    i_sts5 = din("sts5", [L, 2, 2, 128, 8])
    i_cfk = din("cfk", [L, 2, 64, 8 * PAST])
    i_cfv = din("cfv", [L, 2, 128, (PAST // 128) * 512])
    i_cflf = din("cflf", [L, 2, 8, PAST])
    i_cmkT = din("cmkT", [L, 2, 128, 8 * 256])
    i_cmv = din("cmv", [L, 2, 128, 2 * 1024])
    i_cvst = din("cvst", [L, 2, 128, NCH_FF * 2])
    i_wpk = din("wpk", [L, 128, W_TOT])
    i_lnp = din("lnp", [1 + 3 * L, 128, 2 * D])
    i_sp = din("sp", [L, 128, NSP])
    i_tp = din("tp", [L, 128, NTP])
    i_wa2 = din("wa2", [L, 16, 128])
    i_cst = din("cst", [128, 2048])
    i_sel = din("sel", [65, 2 * 8 * 68])

    o_yp = dout("yp", [SEQ, D])
    o_ys = dout("ys", [2, 64, D])
    o_glap = dout("glap", [L, 128, 64])
    o_s5p = dout("s5p", [L, 2, 128, 8])
    o_fkp = dout("fkp", [L, SEQ, 512])
    o_fvp = dout("fvp", [L, SEQ, 512])
    o_flp = dout("flp", [L, SEQ, 8])
    o_cvp = dout("cvp", [L, 128, NCH_FF * 2])
    o_mkp = dout("mkp", [L, 256, D])
    o_mvp = dout("mvp", [L, 256, D])
    o_glas = dout("glas", [L, 2, 128, 64])
    o_s5s = dout("s5s", [L, 2, 2, 128, 8])
    o_fks = dout("fks", [L, 2, 64, 512])
    o_fvs = dout("fvs", [L, 2, 64, 512])
    o_fls = dout("fls", [L, 2, 64, 8])
    o_cvs = dout("cvs", [L, 2, 128, NCH_FF * 2])

    wscr = {}
    for l in range(L):
        for n, k, c in W_PACK + [("s5c", 8, 256)]:
            if n == "s5cr":
                continue
            wscr[(l, n)] = dscr("w_%d_%s" % (l, n), [128, k * c])
    NKS = max(NSEG, NPS + 1)
    ktscr = [dscr("kts%d" % l, [NKS, 68, 8 * T]) for l in range(L)]
    vscr = [dscr("vs%d" % l, [NKS, 128, (T // 128) * 8 * 128]) for l in range(L)]
    KTB = [[Buf("kts%d_%d" % (l, s)) for s in range(NKS)] for l in range(L)]
    VSB = [[Buf("vs%d_%d" % (l, s)) for s in range(NKS)] for l in range(L)]
    WSB = {k: Buf("wscr") for k in wscr}

    pbank = [st.enter_context(nc.psum_tensor("pb%d" % i, [128, 512], F32)) for i in range(8)]
    PB = [Buf("pb%d" % i, excl=True) for i in range(8)]

    def pbf(i):
        return pbank[i][:].bitcast(BF16)

    NT = T // 128
    NCHK = T // 64
    cst = sb("cst", [128, 2048])
    cstb = sb("cstb", [128, 1024], BF16)
    selb = sb("selb", [65, 2 * 8 * 68], BF16)
    self32 = sb("self32", [65, 2 * 8 * 68])
    C_ID, C_PD, C_PU, C_TRI, C_ONE = 0, 128, 256, 384, 512
    C_CM4, C_BM, C_I64, C_CHM = 640, 896, 1152, 1280
    ident = cstb[:, 0:128]
    pdown = cstb[:, 128:256]
    pup = cstb[:, 256:384]
    trib = cstb[:, 384:512]
    onesb = cstb[:, 512:640]
    i64pad = cstb[0:64, 640:708]
    cmask4 = cst[0:64, C_CM4:C_CM4 + 256]
    blockmask = cst[:, C_BM:C_BM + 256]
    chunkmask = cst[:, C_CHM:C_CHM + T]
    ones32 = cst[:, C_ONE:C_ONE + 128]
    BC = Buf("consts")

    spt = sb("spt", [128, L, NSP])
    tpt = sb("tpt", [128, L, NTP])
    wa2b = sb("wa2b", [16, L, 128], BF16)
    wa2f = sb("wa2f", [16, L, 128])
    s5tab = sb("s5tab", [128, L, 2, 8, SB])
    s5r = sb("s5r", [128, L, 8])
    s5f = sb("s5f", [128, L, 4, 8])
    BP = BC

    xres = sb("xres", [128, NT, D])
    BXR = Buf("xres")
    xbt = sb("xbt", [128, 2, D], BF16)
    BXB = [Buf("xbt%d" % i) for i in range(2)]
    xTp = sb("xTp", [128, 8, T + 2], BF16)
    xT = xTp[:, :, 2:T + 2]
    BXT = Buf("xT")
    xhc = sb("xhc", [128, L, 8, 2], BF16)
    BXH = [Buf("xhc%d" % l) for l in range(L)]
    NWS = 3
    wring = [sb("wring%d" % i, [128, SLOT_COLS], BF16) for i in range(NWS)]
    BW = [Buf("wring%d" % i) for i in range(NWS)]
    CW = [sch.chan("w%d" % i) for i in range(NWS)]
    lnt = sb("lnt", [128, 2 * D])
    BLN = Buf("lnt")
    CLN = sch.chan("ln")
    s5bt = sb("s5bt", [128, 2, 8 * T])
    BS5H = [Buf("s5bt%d" % i) for i in range(2)]
    stg = [s5bt[:, i, 0:1024] for i in range(2)]
    BSTG = BS5H
    CSTG = [sch.chan("stg%d" % i) for i in range(2)]
    stgb = [sb("stgb%d" % i, [128, 1024], BF16) for i in range(2)]
    BSTGB = [Buf("stgb%d" % i) for i in range(2)]
    CSTGB = [sch.chan("stgb%d" % i) for i in range(2)]
    mkT = sb("mkT", [128, L, 8, 256], BF16)
    mvb = sb("mvb", [128, L, 2, D], BF16)
    BMK = [Buf("mk%d" % l) for l in range(L)]
    gS = sb("gS", [128, L, 256])
    gSb = sb("gSb", [128, L, 256], BF16)
    BGS = [Buf("gS%d" % l) for l in range(L)]
    CGS = [sch.chan("gS%d" % l) for l in range(L)]
    CGSL = [sch.chan("gSl%d" % l) for l in range(L)]
    s5x = sb("s5x", [128, L, 2, 8])
    BS5X = [Buf("s5x%d" % l) for l in range(L)]
    s5e = sb("s5e", [128, L, 2, 8])
    CS5X = sch.chan("s5x")
    CS5XL = sch.chan("s5xl")
    halo = sb("halo", [128, L, NCH_FF, 2])
    BHALO = [[Buf("halo%d_%d" % (l, ci)) for ci in range(NCH_FF)] for l in range(L)]
    CHALO = [sch.chan("halo%d" % l) for l in range(L)]
    CHALOL = [sch.chan("halol%d" % l) for l in range(L)]
    cfc = sb("cfc", [40, L])
    BCFC = [Buf("cfc%d" % l) for l in range(L)]

    gqk = sb("gqk", [128, 2, T])
    BGQK = Buf("gqk")
    suT = sb("suT", [128, 2, T])
    BSU = Buf("suT")
    sub = sb("sub", [128, 2, T], BF16)
    gaT = sb("gaT", [16, T], BF16)
    BGA = Buf("gaT")
    ffT = sb("ffT", [40, T])
    BFF = Buf("ffT")
    qTa = sb("qTa", [68, 8, T], BF16)
    BQT = Buf("qTa")
    kTa = sb("kTa", [68, 8, T], BF16)
    BKT = Buf("kTa")
    CKT = sch.chan("kTa")
    vtb = sb("vtb", [64, NCHK, 256], BF16)
    gtk = sb("gtk", [64, NCHK, 256])
    BVT = Buf("vtok")
    ost = [sb("ost%d" % i, [128, 520]) for i in range(2)]
    BOST = [Buf("ost%d" % i) for i in range(2)]
    COST = [sch.chan("ost%d" % i) for i in range(2)]
    vown = sb("vown", [128, NT, 8, 128], BF16)
    BVO = Buf("vown")
    CVO = sch.chan("vown")
    NTMP = 8
    tmp = [sb("tmp%d" % i, [128, T + 2]) for i in range(NTMP)]
    BT = [Buf("tmp%d" % i) for i in range(NTMP)]
    tmpb = [sb("tmpb%d" % i, [128, 4, T], BF16) for i in range(2)]
    BTB = [Buf("tmpb%d" % i) for i in range(2)]
    s5t = sb("s5t", [128, 10, 8])
    BS5T = Buf("s5t")
    sm = sb("sm", [128, 64])
    smt = sb("smt", [128, 32])
    BSMT = [Buf("smt%d" % i) for i in range(2)]
    BSM = Buf("sm")
    augs = sb("augs", [65, T], BF16)
    BAUG = Buf("augs")
    hi2 = sb("hi2", [40, T], BF16)
    big = sb("big", [128, 22, T], BF16)
    BBIG = Buf("big")
    mixT = sb("mixT", [128, 8, T], BF16)
    BMIX = Buf("mixT")
    qmT = sb("qmT", [128, 8, T], BF16)
    BQM = Buf("qmT")
    ktr = [sb("ktr%d" % i, [68, 8, T], BF16) for i in range(2)]
    BKR = [Buf("ktr%d" % i) for i in range(2)]
    CKR = [sch.chan("ktr%d" % i) for i in range(2)]
    vtr = [sb("vtr%d" % i, [128, NT, 8, 128], BF16) for i in range(2)]
    BVR = [Buf("vtr%d" % i) for i in range(2)]
    CVR = [sch.chan("vtr%d" % i) for i in range(2)]
    ptb = [sb("ptb%d" % i, [128, 512], BF16) for i in range(4)]
    BPT = [Buf("ptb%d" % i) for i in range(4)]
    rlb = sb("rlb", [128, 4, T], BF16)
    BRL = Buf("rlb")
    rl1 = sb("rl1", [1, T], BF16)
    BRL1 = Buf("rl1")
    CXIN = sch.chan("xin")
    CXIN2 = sch.chan("xin2")
    COUT = sch.chan("out")
    CPAR = sch.chan("par")

    def MM(out, lhsT, rhs, start, stop, R, W, skip=False):
        if skip:
            add("pe", lambda e: e.matmul(out, lhsT, rhs, start=start, stop=stop, skip_group_check=True), reads=R, writes=W)
        else:
            add("pe", lambda e: e.matmul(out, lhsT, rhs, start=start, stop=stop), reads=R, writes=W)

    def TR(out, in_, idn, R, W):
        add("pe", lambda e: e.transpose(out, in_, idn), reads=R + [BC], writes=W)

    def ACT(out, in_, func, R, W, bias=None, scale=1.0, accum=None):
        kw = {}
        if bias is not None:
            kw["bias"] = bias
        if accum is not None:
            kw["accum_out"] = accum
        add("act", lambda e: e.activation(out=out, in_=in_, func=func, scale=scale, **kw), reads=R, writes=W)

    def ACOPY(out, in_, R, W):
        add("act", lambda e: e.copy(out=out, in_=in_), reads=R, writes=W)

    def VCOPY(out, in_, R, W, eng="dve"):
        add(eng, lambda e: e.tensor_copy(out=out, in_=in_), reads=R, writes=W)

    def TT(out, a, b, op, R, W, eng="dve"):
        add(eng, lambda e: e.tensor_tensor(out=out, in0=a, in1=b, op=op), reads=R, writes=W)

    def TS(out, a, s1, s2, op0, op1, R, W, eng="dve"):
        if op1 is None:
            add(eng, lambda e: e.tensor_scalar(out=out, in0=a, scalar1=s1, scalar2=None, op0=op0), reads=R, writes=W)
        else:
            add(eng, lambda e: e.tensor_scalar(out=out, in0=a, scalar1=s1, scalar2=s2, op0=op0, op1=op1), reads=R, writes=W)

    def STT(out, a, s, b, op0, op1, R, W):
        add("dve", lambda e: e.scalar_tensor_tensor(out=out, in0=a, scalar=s, in1=b, op0=op0, op1=op1), reads=R, writes=W)

    def SCAN(out, d0, d1, init, R, W):
        add("dve", lambda e: e.tensor_tensor_scan(out=out, data0=d0, data1=d1, initial=init, op0=ALU.mult, op1=ALU.add),
            reads=R, writes=W)

    def RECIP(out, in_, R, W):
        add("dve", lambda e: e.reciprocal(out=out, in_=in_), reads=R, writes=W)

    def MSET(ap, v, W, eng="pool"):
        add(eng, lambda e: e.memset(ap, v), writes=W)

    def LOAD(out, in_, W, chan, R=()):
        add("sp", lambda e: e.dma_start(out=out, in_=in_), reads=list(R), writes=W, chan=chan)

    def STORE(out, in_, R, chan, W=()):
        add("pool", lambda e: e.dma_start(out=out, in_=in_), reads=R, writes=list(W), chan=chan)

    wstate = dict(next_load=0, next_use=0)
    wseq = []

    def plan_stream(nstreams):
        for _ in range(nstreams):
            for l in range(L):
                for n, k, c in W_STREAM:
                    wseq.append((l, n, k, c))

    def issue_wload():
        i = wstate["next_load"]
        if i >= len(wseq):
            return
        l, n, k, c = wseq[i]
        slot = i % NWS
        LOAD(wring[slot][:, 0:k * c], wscr[(l, n)], [BW[slot]], CW[slot], R=[WSB[(l, n)]])
        wstate["next_load"] = i + 1

    def getw(l, name):
        i = wstate["next_use"]
        assert wseq[i][0] == l and wseq[i][1] == name, (wseq[i], l, name)
        while wstate["next_load"] <= i:
            issue_wload()
        k, c = wseq[i][2], wseq[i][3]
        slot = i % NWS
        wstate["next_use"] = i + 1
        return wring[slot][:, 0:k * c].rearrange("p (k c) -> p k c", k=k), BW[slot]

    def wdone():
        while wstate["next_load"] < min(len(wseq), wstate["next_use"] + NWS - 1):
            issue_wload()

    LOAD(cst[:], i_cst, [BC], CPAR)
    LOAD(self32[:], i_sel, [BC], CPAR)
    LOAD(spt[:], i_sp.rearrange("l p n -> p l n"), [BP], CPAR)
    LOAD(tpt[:], i_tp.rearrange("l p n -> p l n"), [BP], CPAR)
    LOAD(wa2f[:], i_wa2.rearrange("l p n -> p l n"), [BP], CPAR)
    VCOPY(cstb[:, 0:640], cst[:, 0:640], [BC], [BC])
    VCOPY(cstb[0:64, 640:708], cst[0:64, C_I64:C_I64 + 68], [BC], [BC])
    VCOPY(selb[:], self32[:], [BC], [BC])
    VCOPY(wa2b[:], wa2f[:], [BP], [BP])
    MSET(rlb[:], 0.0, [BRL])
    MSET(augs[:], 0.0, [BAUG])
    MSET(augs[64:65, :], 1.0, [BAUG])
    MSET(hi2[:], 0.0, [BAUG])
    MSET(ffT[:], 0.0, [BFF])
    for i in range(2):
        MSET(vtr[i][:], 1.0, [BVR[i]])
    MSET(vown[:], 1.0, [BVO])

    castn = [0]

    def cast_pieces(src_ap, dst_ap, ncols, dstbuf):
        c0 = 0
        while c0 < ncols:
            cw = min(1024, ncols - c0)
            i = castn[0] % 2
            castn[0] += 1
            LOAD(stg[i][:, 0:cw], src_ap[:, c0:c0 + cw], [BSTG[i]], CSTG[i])
            eng = ("dve", "act", "pool")[castn[0] % 3]
            if eng == "act":
                ACOPY(stgb[i][:, 0:cw], stg[i][:, 0:cw], [BSTG[i]], [BSTGB[i]])
            else:
                VCOPY(stgb[i][:, 0:cw], stg[i][:, 0:cw], [BSTG[i]], [BSTGB[i]], eng=eng)
            STORE(dst_ap[:, c0:c0 + cw], stgb[i][:, 0:cw], [BSTGB[i]], CSTGB[i], W=[dstbuf])
            c0 += cw

    for l in range(L):
        for n, k, c in W_PACK:
            if n == "s5cr":
                continue
            o = W_OFF[n][0]
            cast_pieces(i_wpk[l][:, o:o + k * c], wscr[(l, n)], k * c, WSB[(l, n)])

    def s5_prep(l):
        lre = spt[:, l, SP["lre"]:SP["lre"] + 8]
        lim = spt[:, l, SP["lim"]:SP["lim"] + 8]
        ldt = spt[:, l, SP["ldt"]:SP["ldt"] + 8]
        t = lambda i: s5t[:, i, :]
        R = [BP]
        Wt = [BS5T]
        ACT(t(0), ldt, AF.Exp, R, Wt)
        TT(t(1), lre, t(0), ALU.mult, R + Wt, Wt)
        TT(t(2), lim, t(0), ALU.mult, R + Wt, Wt)
        ACT(s5r[:, l, :], t(1), AF.Exp, Wt, [BP])
        ACT(t(3), t(2), AF.Sin, Wt, Wt, scale=1.0 / 16)
        TS(t(4), t(2), 1.0 / 16, math.pi / 2, ALU.mult, ALU.add, Wt, Wt)
        ACT(t(4), t(4), AF.Sin, Wt, Wt)
        for _ in range(4):
            TT(t(5), t(4), t(4), ALU.mult, Wt, Wt)
            TT(t(6), t(3), t(3), ALU.mult, Wt, Wt)
            TT(t(3), t(3), t(4), ALU.mult, Wt, Wt)
            TS(t(3), t(3), 2.0, None, ALU.mult, None, Wt, Wt)
            TT(t(4), t(5), t(6), ALU.subtract, Wt, Wt)
        cosT = s5tab[:, l, 0]
        sinT = s5tab[:, l, 1]
        MSET(cosT[:, :, 0:1], 1.0, [BP])
        MSET(sinT[:, :, 0:1], 0.0, [BP])
        VCOPY(t(5), t(4), Wt, Wt)
        VCOPY(t(6), t(3), Wt, Wt)
        n = 1
        while n < SB:
            dc = t(5).unsqueeze(2).to_broadcast([128, 8, n])
            ds = t(6).unsqueeze(2).to_broadcast([128, 8, n])
            a = tmp[1][:, 0:8 * n].rearrange("p (c j) -> p c j", c=8)
            b = tmp[2][:, 0:8 * n].rearrange("p (c j) -> p c j", c=8)
            RW = Wt + [BP, BT[1], BT[2]]
            TT(a, cosT[:, :, 0:n], dc, ALU.mult, RW, [BT[1]])
            TT(b, sinT[:, :, 0:n], ds, ALU.mult, RW, [BT[2]])
            TT(cosT[:, :, n:2 * n], a, b, ALU.subtract, RW, [BP])
            TT(a, cosT[:, :, 0:n], ds, ALU.mult, RW, [BT[1]])
            TT(b, sinT[:, :, 0:n], dc, ALU.mult, RW, [BT[2]])
            TT(sinT[:, :, n:2 * n], a, b, ALU.add, RW, [BP])
            TT(t(7), t(5), t(5), ALU.mult, Wt, Wt)
            TT(t(8), t(6), t(6), ALU.mult, Wt, Wt)
            TT(t(6), t(5), t(6), ALU.mult, Wt, Wt)
            TS(t(6), t(6), 2.0, None, ALU.mult, None, Wt, Wt)
            TT(t(5), t(7), t(8), ALU.subtract, Wt, Wt)
            n *= 2
        TS(spt[:, l, SP["hre"]:SP["hre"] + 2], spt[:, l, SP["s5bg"]:SP["s5bg"] + 2], -1.0, None, ALU.mult, None, [BP], [BP])
        VCOPY(s5e[:, l, 0, :], t(5), Wt, [BP])
        VCOPY(s5e[:, l, 1, :], t(6), Wt, [BP])
        mag = s5r[:, l, :]
        u = lambda i: tmp[4][:, 8 * i:8 * i + 8]
        W4 = [BT[4]]
        RR = R + Wt + W4
        TT(u(0), mag, t(4), ALU.mult, RR, W4)
        TS(u(0), u(0), -1.0, None, ALU.add, None, RR, W4)
        TT(u(1), mag, t(3), ALU.mult, RR, W4)
        TT(u(2), lre, lre, ALU.mult, RR, W4)
        TT(u(3), lim, lim, ALU.mult, RR, W4)
        TT(u(2), u(2), u(3), ALU.add, RR, W4)
        RECIP(u(2), u(2), RR, W4)
        TT(u(3), u(0), lre, ALU.mult, RR, W4)
        TT(u(4), u(1), lim, ALU.mult, RR, W4)
        TT(u(3), u(3), u(4), ALU.add, RR, W4)
        TT(s5f[:, l, 0, :], u(3), u(2), ALU.mult, RR, [BP])
        TT(u(3), u(1), lre, ALU.mult, RR, W4)
        TT(u(4), u(0), lim, ALU.mult, RR, W4)
        TT(u(3), u(3), u(4), ALU.subtract, RR, W4)
        TT(s5f[:, l, 1, :], u(3), u(2), ALU.mult, RR, [BP])
        TT(u(5), s5f[:, l, 0, :], s5f[:, l, 0, :], ALU.mult, RR, W4)
        TT(u(6), s5f[:, l, 1, :], s5f[:, l, 1, :], ALU.mult, RR, W4)
        TT(u(5), u(5), u(6), ALU.add, RR, W4)
        RECIP(u(5), u(5), RR, W4)
        TT(s5f[:, l, 2, :], s5f[:, l, 0, :], u(5), ALU.mult, RR, [BP])
        TT(s5f[:, l, 3, :], s5f[:, l, 1, :], u(5), ALU.mult, RR, [BP])
        TS(s5f[:, l, 3, :], s5f[:, l, 3, :], -1.0, None, ALU.mult, None, RR, [BP])
        o = W_OFF["s5cr"][0]
        LOAD(stg[0][:, :], i_wpk[l][:, o:o + 1024], [BSTG[0]], CSTG[0])
        LOAD(stg[1][:, :], i_wpk[l][:, o + 1024:o + 2048], [BSTG[1]], CSTG[1])
        for c in range(8):
            sg = stg[c // 4]
            bs = BSTG[c // 4]
            cre = sg[:, (c % 4) * 256:(c % 4) * 256 + 128]
            cim = sg[:, (c % 4) * 256 + 128:(c % 4) * 256 + 256]
            fre = s5f[:, l, 0, c:c + 1]
            fim = s5f[:, l, 1, c:c + 1]
            a = tmp[5][:, 0:128]
            b = tmp[6][:, 0:128]
            TS(a, cre, fre, None, ALU.mult, None, [bs, BP], [BT[5]])
            TS(b, cim, fim, None, ALU.mult, None, [bs, BP], [BT[6]])
            TT(stgb[0][:, c * 128:c * 128 + 128], a, b, ALU.subtract, [BT[5], BT[6]], [BSTGB[0]])
            TS(a, cre, fim, None, ALU.mult, None, [bs, BP], [BT[5]])
            TS(b, cim, fre, None, ALU.mult, None, [bs, BP], [BT[6]])
            TT(a, a, b, ALU.add, [BT[5], BT[6]], [BT[5]])
            TS(stgb[1][:, c * 128:c * 128 + 128], a, -1.0, None, ALU.mult, None, [BT[5]], [BSTGB[1]])
        dst = wscr[(l, "s5c")].rearrange("p (c x) -> p c x", c=8)
        STORE(dst[:, :, 0:128], stgb[0][:, 0:1024].rearrange("p (c x) -> p c x", c=8), [BSTGB[0]], CSTGB[0], W=[WSB[(l, "s5c")]])
        STORE(dst[:, :, 128:256], stgb[1][:, 0:1024].rearrange("p (c x) -> p c x", c=8), [BSTGB[1]], CSTGB[1], W=[WSB[(l, "s5c")]])

    for l in range(L):
        s5_prep(l)

    def mem_prep_prompt(l):
        mt = big[:, 0:8, 0:256]
        for i in range(2):
            LOAD(stg[i][:, :], i_memT[:, i * 1024:(i + 1) * 1024], [BSTG[i]], CSTG[i])
            VCOPY(mt[:, 4 * i:4 * i + 4, :], stg[i][:, :].rearrange("p (k m) -> p k m", k=4), [BSTG[i]], [BBIG])
        for which, name, od in ((0, "wk", o_mkp), (1, "wv", o_mvp)):
            for hf in range(2):
                i = hf % 2
                LOAD(wring[i][:, 0:4096], wscr[(l, "%s%d" % (name, hf))], [BW[i]], CW[i], R=[WSB[(l, "%s%d" % (name, hf))]])
                w = wring[i][:, 0:4096].rearrange("p (k c) -> p k c", k=8)
                if which == 0:
                    for c4 in range(4):
                        cidx = hf * 4 + c4
                        pb = 2 + (cidx % 2)
                        for kc in range(8):
                            MM(pbank[pb][:, 0:256], w[:, kc, c4 * 128:(c4 + 1) * 128], mt[:, kc, :], kc == 0, kc == 7,
                               [BW[i], BBIG], [PB[pb]])
                        ACOPY(mkT[:, l, cidx, :], pbank[pb][:, 0:256], [PB[pb]], [BMK[l]])
                for m2 in range(2):
                    pb = 4 + m2
                    for kc in range(8):
                        MM(pbank[pb][:, :], mt[:, kc, m2 * 128:(m2 + 1) * 128], w[:, kc, :], kc == 0, kc == 7,
                           [BW[i], BBIG], [PB[pb]])
                    oi = (hf * 2 + m2) % 2
                    ACOPY(ost[oi][:, 0:512], pbank[pb][:, :], [PB[pb]], [BOST[oi]])
                    if which == 1:
                        VCOPY(mvb[:, l, m2, hf * 512:(hf + 1) * 512], ost[oi][:, 0:512], [BOST[oi]], [BMK[l]])
                    STORE(od[l][m2 * 128:(m2 + 1) * 128, hf * 512:(hf + 1) * 512], ost[oi][:, 0:512], [BOST[oi]], COST[oi])

    for l in range(L):
        mem_prep_prompt(l)

    lnstate = dict(cur=None)

    def ln_load(idx):
        LOAD(lnt[:], i_lnp[idx], [BLN], CLN)

    def layer_norm(S, tiles):
        P = S["P"]
        xs_ = {tt: xres[0:P, tt, :] for tt in tiles}
        o = {tt: 16 * (tt % 2) for tt in tiles}
        BS = {tt: BSMT[tt % 2] for tt in tiles}
        st6 = {tt: smt[0:P, o[tt]:o[tt] + 12].rearrange("p (a b) -> p a b", a=2) for tt in tiles}
        for tt in tiles:
            x = xs_[tt]
            add("dve", lambda e, x=x, a=st6[tt]: e.bn_stats(out=a[:, 0, :], in_=x[:, 0:512]), reads=[BXR], writes=[BS[tt]])
            add("dve", lambda e, x=x, a=st6[tt]: e.bn_stats(out=a[:, 1, :], in_=x[:, 512:1024]), reads=[BXR], writes=[BS[tt]])
        for tt in tiles:
            mv = smt[0:P, o[tt] + 12:o[tt] + 14]
            add("dve", lambda e, mv=mv, a=st6[tt]: e.bn_aggr(out=mv, in_=a), reads=[BS[tt]], writes=[BS[tt]])
        for tt in tiles:
            rs = smt[0:P, o[tt] + 14:o[tt] + 15]
            ACT(rs, smt[0:P, o[tt] + 13:o[tt] + 14], AF.Ln, [BS[tt]], [BS[tt]], bias=LN_EPS)
        for tt in tiles:
            rs = smt[0:P, o[tt] + 14:o[tt] + 15]
            ACT(rs, rs, AF.Exp, [BS[tt]], [BS[tt]], scale=-0.5)
        for tt in tiles:
            rs = smt[0:P, o[tt] + 14:o[tt] + 15]
            TS(xs_[tt], xs_[tt], smt[0:P, o[tt] + 12:o[tt] + 13], rs, ALU.subtract, ALU.mult, [BXR, BS[tt]], [BXR])
            TT(xs_[tt], xs_[tt], lnt[0:P, 0:D], ALU.mult, [BXR, BLN], [BXR])
            TT(xs_[tt], xs_[tt], lnt[0:P, D:2 * D], ALU.add, [BXR, BLN], [BXR])
        for tt in tiles:
            ACOPY(xbt[0:P, tt % 2, :], xs_[tt], [BXR], [BXB[tt % 2]])
        for tt in tiles:
            bk = 4 + tt % 2
            pv = pbf(bk)
            for kc in range(8):
                TR(pv[:, kc * 128:kc * 128 + P], xbt[0:P, tt % 2, kc * 128:(kc + 1) * 128], ident[0:P, 0:P], [BXB[tt % 2]], [PB[bk]])
        for tt in tiles:
            bk = 4 + tt % 2
            VCOPY(xT[:, :, tt * 128:tt * 128 + P], pbf(bk)[:, :].rearrange("p (k t) -> p k t", k=8)[:, :, 0:P], [PB[bk]], [BXT])

    def resid_from_psum(S, tt, half, pb):
        P = S["P"]
        x = xres[0:P, tt, half * 512:(half + 1) * 512]
        STT(x, x, ALPHA, pbank[pb][0:P, :], ALU.mult, ALU.add, [BXR, PB[pb]], [BXR])

    def fm_chunk(w, bw, c0, M, pb, Tn):
        for kc in range(8):
            MM(pbank[pb][0:M, 0:Tn], w[:, kc, c0:c0 + M], xT[:, kc, 0:Tn], kc == 0, kc == 7, [bw, BXT], [PB[pb]])

    def logsig_inplace(ap, R, W):
        ACT(ap, ap, AF.Exp, R, W, scale=-1.0)
        ACT(ap, ap, AF.Ln, R, W, bias=1.0)
        TS(ap, ap, -1.0, None, ALU.mult, None, R, W)

    def layer(S, l, seg):
        Tn, P, nt, nchk = S["T"], S["P"], S["nt"], S["nchk"]
        samp = S["samp"]
        sidx = S["sidx"]
        tok0 = seg * Tn

        w, bw = getw(l, "wfm")
        for j in range(2):
            pb = 2 + j
            fm_chunk(w, bw, j * 128, 128, pb, Tn)
            ACOPY(gqk[:, j, 0:Tn], pbank[pb][:, 0:Tn], [PB[pb]], [BGQK])
        for j in range(2):
            pb = 2 + j
            fm_chunk(w, bw, 256 + j * 128, 128, pb, Tn)
            ACOPY(suT[:, j, 0:Tn], pbank[pb][:, 0:Tn], [PB[pb]], [BSU])
        VCOPY(sub[:, :, 0:Tn], suT[:, :, 0:Tn], [BSU], [BSU])
        fm_chunk(w, bw, 512, 16, 2, Tn)
        ACOPY(gaT[:, 0:Tn], pbank[2][0:16, 0:Tn], [PB[2]], [BGA])
        fm_chunk(w, bw, 528, 40, 3, Tn)
        lf = tmp[7][0:40, 0:Tn]
        TS(lf, pbank[3][0:40, 0:Tn], spt[0:40, l, SP["fbf"]:SP["fbf"] + 1], None, ALU.add, None, [PB[3], BP], [BT[7]])
        logsig_inplace(lf, [BT[7]], [BT[7]])
        SCAN(ffT[:, 0:Tn], ones32[0:40, 0:1].to_broadcast([40, Tn]), lf, cfc[:, l:l + 1], [BT[7], BC, BCFC[l]], [BFF])
        VCOPY(cfc[:, l:l + 1], ffT[:, Tn - 1:Tn], [BFF], [BCFC[l]])
        VCOPY(augs[0:8, 0:Tn], ffT[0:8, 0:Tn], [BFF], [BAUG])
        VCOPY(hi2[32:40, 0:Tn], ffT[32:40, 0:Tn], [BFF], [BAUG])
        TT(augs[32:40, 0:Tn], ffT[32:40, 0:Tn], hi2[32:40, 0:Tn], ALU.subtract, [BFF, BAUG], [BAUG])
        wdone()
        w, bw = getw(l, "wfq")
        for h in range(8):
            pb = 2 + (h % 4)
            for kc in range(8):
                MM(pbank[pb][0:68, 0:Tn], w[:, kc, h * 68:(h + 1) * 68], xT[:, kc, 0:Tn], kc == 0, False, [bw, BXT], [PB[pb]])
            MM(pbank[pb][0:68, 0:Tn], selb[:, (8 + h) * 68:(9 + h) * 68], augs[:, 0:Tn], False, True, [BC, BAUG], [PB[pb]])
            ACT(qTa[:, h, 0:Tn], pbank[pb][0:68, 0:Tn], AF.Copy, [PB[pb]], [BQT], scale=0.125)
        wdone()
        w, bw = getw(l, "wfk")
        for h in range(8):
            pb = 2 + (h % 4)
            for kc in range(8):
                MM(pbank[pb][0:68, 0:Tn], w[:, kc, h * 68:(h + 1) * 68], xT[:, kc, 0:Tn], kc == 0, False, [bw, BXT], [PB[pb]])
            MM(pbank[pb][0:68, 0:Tn], selb[:, h * 68:(h + 1) * 68], augs[:, 0:Tn], False, True, [BC, BAUG], [PB[pb]])
            ACOPY(kTa[:, h, 0:Tn], pbank[pb][0:68, 0:Tn], [PB[pb]], [BKT])
        wdone()
        kslot = S["kslot"](seg)
        STORE(ktscr[l][kslot].rearrange("p (h t) -> p h t", h=8)[:, :, 0:Tn], kTa[:, :, 0:Tn], [BKT], CKT, W=[KTB[l][kslot]])
        w, bw = getw(l, "wtv")
        for c in range(nchk):
            pb = 2 + (c % 4)
            for kc in range(8):
                MM(pbank[pb][0:64, :], xT[:, kc, c * 64:(c + 1) * 64], w[:, kc, :], kc == 0, kc == 7, [bw, BXT], [PB[pb]])
            ACOPY(vtb[:, c, :], pbank[pb][0:64, 0:256], [PB[pb]], [BVT])
            VCOPY(gtk[:, c, :], pbank[pb][0:64, 256:512], [PB[pb]], [BVT])
        wdone()
        w, bw = getw(l, "wtk")
        for tt in range(nt):
            pb = 2 + (tt % 2)
            for kc in range(8):
                MM(pbank[pb][0:P, :], xT[:, kc, tt * 128:tt * 128 + P], w[:, kc, :], kc == 0, kc == 7, [bw, BXT], [PB[pb]])
            oi = tt % 2
            ACOPY(ost[oi][0:P, 0:512], pbank[pb][0:P, :], [PB[pb]], [BOST[oi]])
            STORE(S["o_fk"](l)[tok0 + tt * 128:tok0 + tt * 128 + P, :], ost[oi][0:P, 0:512], [BOST[oi]], COST[oi])
        wdone()
        w, bw = getw(l, "wtf")
        for tt in range(nt):
            for (c0, cn, pb) in ((0, 512, 2), (512, 8, 3)):
                for kc in range(8):
                    MM(pbank[pb][0:P, 0:cn], xT[:, kc, tt * 128:tt * 128 + P], w[:, kc, c0:c0 + cn], kc == 0, kc == 7,
                       [bw, BXT], [PB[pb]])
            oi = tt % 2
            ACOPY(ost[oi][0:P, 0:512], pbank[2][0:P, :], [PB[2]], [BOST[oi]])
            TT(ost[oi][0:P, 512:520], pbank[3][0:P, 0:8], tpt[0:P, l, TP["fbf"]:TP["fbf"] + 8], ALU.add, [PB[3], BP], [BOST[oi]])
            logsig_inplace(ost[oi][0:P, 512:520], [BOST[oi]], [BOST[oi]])
            v4 = ost[oi][0:P, 0:512].rearrange("p (a b d) -> p a b d", a=4, b=2)
            vo = vown[0:P, tt].rearrange("p (a b) d -> p a b d", a=4)
            VCOPY(vo[:, :, 0, 0:64], v4[:, :, 0, :], [BOST[oi]], [BVO])
            VCOPY(vo[:, :, 1, 64:128], v4[:, :, 1, :], [BOST[oi]], [BVO])
            STORE(S["o_fv"](l)[tok0 + tt * 128:tok0 + tt * 128 + P, :], ost[oi][0:P, 0:512], [BOST[oi]], COST[oi])
            STORE(S["o_fl"](l)[tok0 + tt * 128:tok0 + tt * 128 + P, :], ost[oi][0:P, 512:520], [BOST[oi]], COST[oi])
        wdone()
        STORE(vscr[l][kslot].rearrange("p (t x) -> p t x", t=NT)[0:P, 0:nt, :],
              vown[0:P, 0:nt].rearrange("p t h d -> p t (h d)"), [BVO], CVO, W=[VSB[l][kslot]])

        q = gqk[:, 0, 0:Tn]
        k = gqk[:, 1, 0:Tn]
        cum = tmp[0][:, 0:Tn]
        la = tmp[7][:, 0:Tn]
        MM(pbank[2][:, 0:Tn], wa2b[:, l, :], gaT[:, 0:Tn], True, True, [BP, BGA], [PB[2]])
        TS(la, pbank[2][:, 0:Tn], spt[:, l, SP["gba"]:SP["gba"] + 1], None, ALU.add, None, [PB[2], BP], [BT[7]])
        logsig_inplace(la, [BT[7]], [BT[7]])
        TS(la, la, 1.0 / 16.0, None, ALU.mult, None, [BT[7]], [BT[7]])
        SCAN(cum, chunkmask[:, 0:Tn], la, 0.0, [BT[7], BC], [BT[0]])
        cend = tmp[1][:, 0:nchk]
        VCOPY(cend, cum.rearrange("p (c j) -> p c j", j=64)[:, :, 63], [BT[0]], [BT[1]])
        dec = tmp[1][:, 8:8 + nchk]
        ACT(dec, cend, AF.Exp, [BT[1]], [BT[1]])
        e1 = tmp[2][:, 0:Tn]
        ACT(e1, cum, AF.Exp, [BT[0]], [BT[2]])
        qd = tmpb[0][:, 0, 0:Tn]
        STT(qd, q, 32.0 ** -0.5, e1, ALU.mult, ALU.mult, [BGQK, BT[2]], [BTB[0]])
        ACT(e1, cum, AF.Exp, [BT[0]], [BT[2]], scale=-1.0)
        kinv = tmp[3][:, 0:Tn]
        TT(kinv, k, e1, ALU.mult, [BGQK, BT[2]], [BT[3]])
        kih = tmpb[1]
        for h in range(4):
            TS(kih[:, h, 0:Tn], kinv, spt[:, l, SP["hm"] + h:SP["hm"] + h + 1], None, ALU.mult, None, [BT[3], BP], [BTB[1]])
        kend = tmpb[0][:, 1, 0:Tn]
        for c in range(nchk):
            ACT(e1[:, c * 64:(c + 1) * 64], cum[:, c * 64:(c + 1) * 64], AF.Exp, [BT[0], BT[1]], [BT[2]],
                bias=cend[:, c:c + 1], scale=-1.0)
        TT(kend, k, e1, ALU.mult, [BGQK, BT[2]], [BTB[0]])
        Sf = gS[:, l, :]
        Sb = gSb[:, l, :]
        for c in range(nchk):
            cs = slice(c * 64, (c + 1) * 64)
            for h in range(4):
                MM(pbank[2][0:64, h * 64:(h + 1) * 64], kih[:, h, cs], qd[:, cs], True, True, [BTB[0], BTB[1]], [PB[2]])
            scm = tmpb[0][0:64, 2, 0:256]
            TT(scm, pbank[2][0:64, 0:256], cmask4, ALU.mult, [PB[2], BC], [BTB[0]])
            MM(pbank[3][0:64, 0:256], qd[:, cs], Sb, True, False, [BTB[0], BGS[l]], [PB[3]])
            for h in range(4):
                MM(pbank[3][0:64, h * 64:(h + 1) * 64], scm[:, h * 64:(h + 1) * 64], vtb[:, c, h * 64:(h + 1) * 64], False, h == 3,
                   [BTB[0], BVT], [PB[3]])
            TR(pbf(4)[0:64, 0:128], kend[:, cs], ident, [BTB[0]], [PB[4]])
            keT = tmpb[0][0:64, 3, 0:128]
            VCOPY(keT, pbf(4)[0:64, 0:128], [PB[4]], [BTB[0]])
            MM(pbank[5][:, 0:256], keT, vtb[:, c, :], True, True, [BTB[0], BVT], [PB[5]])
            dsm = tmp[4][:, 0:256]
            TT(dsm, pbank[5][:, 0:256], blockmask, ALU.mult, [PB[5], BC], [BT[4]])
            STT(Sf, Sf, dec[:, c:c + 1], dsm, ALU.mult, ALU.add, [BGS[l], BT[1], BT[4]], [BGS[l]])
            ACOPY(Sb, Sf, [BGS[l]], [BGS[l]])
            osq = tmp[5][0:64, 0:256]
            ACT(osq, pbank[3][0:64, 0:256], AF.Square, [PB[3]], [BT[5]])
            ss = sm[0:64, 16:20]
            add("dve", lambda e, ss=ss, osq=osq: e.reduce_sum(out=ss, in_=osq.rearrange("p (h d) -> p h d", h=4), axis=AX.X),
                reads=[BT[5]], writes=[BSM])
            TS(ss, ss, 1.0 / 64.0, LN_EPS, ALU.mult, ALU.add, [BSM], [BSM])
            ACT(ss, ss, AF.Ln, [BSM], [BSM])
            ACT(ss, ss, AF.Exp, [BSM], [BSM], scale=-0.5)
            on = tmp[5][0:64, 0:256]
            TT(on.rearrange("p (h d) -> p h d", h=4), pbank[3][0:64, 0:256].rearrange("p (h d) -> p h d", h=4),
               ss.unsqueeze(2).to_broadcast([64, 4, 64]), ALU.mult, [PB[3], BSM], [BT[5]])
            TT(on, on, tpt[0:64, l, TP["gng"]:TP["gng"] + 256], ALU.mult, [BT[5], BP], [BT[5]])
            sg = tmp[6][0:64, 0:256]
            ACT(sg, gtk[:, c, :], AF.Exp, [BVT], [BT[6]], scale=-1.0)
            ACT(sg, sg, AF.Ln, [BT[6]], [BT[6]], bias=1.0)
            ACT(sg, sg, AF.Exp, [BT[6]], [BT[6]], scale=-1.0)
            TT(sg, sg, gtk[:, c, :], ALU.mult, [BT[6], BVT], [BT[6]])
            og = tmpb[0][0:64, 2, 0:256]
            TT(og, on, sg, ALU.mult, [BT[5], BT[6]], [BTB[0]])
            for j in range(2):
                TR(pbf(4)[:, 256 + j * 64:256 + (j + 1) * 64], og[:, j * 128:(j + 1) * 128], ident[0:64, 0:64], [BTB[0]], [PB[4]])
            VCOPY(mixT[:, 0:2, cs], pbf(4)[:, 256:384].rearrange("p (j t) -> p j t", j=2), [PB[4]], [BMIX])

        w, bw = getw(l, "s5b")
        xre = big[:, 0:8, 0:Tn]
        xim = big[:, 8:16, 0:Tn]
        cosT = s5tab[:, l, 0]
        sinT = s5tab[:, l, 1]
        nsb = Tn // SB
        v3 = lambda ap: ap.rearrange("p (s j) -> p s j", j=SB)
        btre = s5bt[:, 0, :].rearrange("p (m t) -> p m t", m=8)
        btim = s5bt[:, 1, :].rearrange("p (m t) -> p m t", m=8)
        t0_, t1_, t2_, t3_ = (tmp[i][:, 0:Tn] for i in range(4))
        for m in range(8):
            kc = m // 4
            pa, pi_ = (2, 3) if m % 2 == 0 else (4, 5)
            MM(pbank[pa][:, 0:Tn], w[:, m, 0:128], sub[:, kc, 0:Tn], True, True, [bw, BSU], [PB[pa]])
            MM(pbank[pi_][:, 0:Tn], w[:, m, 128:256], sub[:, kc, 0:Tn], True, True, [bw, BSU], [PB[pi_]])
            cb = cosT[:, m, :].unsqueeze(1).to_broadcast([128, nsb, SB])
            sbb = sinT[:, m, :].unsqueeze(1).to_broadcast([128, nsb, SB])
            pre = pbank[pa][:, 0:Tn]
            pim = pbank[pi_][:, 0:Tn]
            TT(v3(t0_), v3(pre), cb, ALU.mult, [PB[pa], BP], [BT[0]])
            TT(v3(t1_), v3(pim), sbb, ALU.mult, [PB[pi_], BP], [BT[1]])
            TT(v3(t2_), v3(pim), cb, ALU.mult, [PB[pi_], BP], [BT[2]])
            TT(v3(t3_), v3(pre), sbb, ALU.mult, [PB[pa], BP], [BT[3]])
            TT(btre[:, m, 0:Tn], t0_, t1_, ALU.add, [BT[0], BT[1]], [BS5H[0]])
            TT(btim[:, m, 0:Tn], t2_, t3_, ALU.subtract, [BT[2], BT[3]], [BS5H[1]])
        wdone()
        cS = s5e[:, l, 0, :]
        sS = s5e[:, l, 1, :]
        for sbi in range(nsb):
            js = slice(sbi * SB, (sbi + 1) * SB)
            if sbi == 0:
                wl_re, wl_im, Rw = s5x[:, l, 0, :], s5x[:, l, 1, :], [BS5X[l]]
            else:
                wl_re, wl_im, Rw = btre[:, :, sbi * SB - 1], btim[:, :, sbi * SB - 1], [BS5H[0], BS5H[1]]
            ini = sm[:, 32:48].rearrange("p (a c) -> p a c", a=2)
            ta = sm[:, 48:56]
            tb = sm[:, 56:64]
            TT(ini[:, 0, :], cS, wl_re, ALU.mult, [BP] + Rw, [BSM])
            TT(ta, sS, wl_im, ALU.mult, [BP] + Rw, [BSM])
            TT(ini[:, 0, :], ini[:, 0, :], ta, ALU.subtract, [BSM], [BSM])
            TT(ini[:, 1, :], cS, wl_im, ALU.mult, [BP] + Rw, [BSM])
            TT(tb, sS, wl_re, ALU.mult, [BP] + Rw, [BSM])
            TT(ini[:, 1, :], ini[:, 1, :], tb, ALU.add, [BSM], [BSM])
            for m in range(8):
                rbc = s5r[:, l, m:m + 1].to_broadcast([128, SB])
                SCAN(btre[:, m, js], rbc, btre[:, m, js], ini[:, 0, m:m + 1], [BP, BS5H[0], BSM], [BS5H[0]])
                SCAN(btim[:, m, js], rbc, btim[:, m, js], ini[:, 1, m:m + 1], [BP, BS5H[1], BSM], [BS5H[1]])
        VCOPY(s5x[:, l, 0, :], btre[:, :, Tn - 1], [BS5H[0]], [BS5X[l]])
        VCOPY(s5x[:, l, 1, :], btim[:, :, Tn - 1], [BS5H[1]], [BS5X[l]])
        for m in range(8):
            cb = cosT[:, m, :].unsqueeze(1).to_broadcast([128, nsb, SB])
            sbb = sinT[:, m, :].unsqueeze(1).to_broadcast([128, nsb, SB])
            wre = btre[:, m, 0:Tn]
            wim = btim[:, m, 0:Tn]
            TT(v3(t0_), v3(wre), cb, ALU.mult, [BS5H[0], BP], [BT[0]])
            TT(v3(t1_), v3(wim), sbb, ALU.mult, [BS5H[1], BP], [BT[1]])
            TT(v3(t2_), v3(wim), cb, ALU.mult, [BS5H[1], BP], [BT[2]])
            TT(v3(t3_), v3(wre), sbb, ALU.mult, [BS5H[0], BP], [BT[3]])
            TT(xre[:, m, :], t0_, t1_, ALU.subtract, [BT[0], BT[1]], [BBIG])
            TT(xim[:, m, :], t2_, t3_, ALU.add, [BT[2], BT[3]], [BBIG])
        w, bw = getw(l, "s5c")
        wg, bwg = getw(l, "s5g")
        zT = tmpb[1][:, 0:2, 0:Tn]
        zf = [tmp[0][:, 0:Tn], tmp[1][:, 0:Tn]]
        for oc in range(2):
            pb = 2 + oc
            n = 0
            for c in range(4 * oc, 4 * oc + 4):
                MM(pbank[pb][:, 0:Tn], w[:, c, 0:128], xre[:, c, :], n == 0, False, [bw, BBIG], [PB[pb]])
                MM(pbank[pb][:, 0:Tn], w[:, c, 128:256], xim[:, c, :], False, n == 3, [bw, BBIG], [PB[pb]])
                n += 1
            y = tmp[2][:, 0:Tn]
            STT(y, suT[:, oc, 0:Tn], spt[:, l, SP["s5d"] + oc:SP["s5d"] + oc + 1], pbank[pb][:, 0:Tn], ALU.mult, ALU.add,
                [BSU, BP, PB[pb]], [BT[2]])
            gelu(y, zf[oc], [BT[2]], [BT[oc]], 3, Tn)
            ACOPY(zT[:, oc, :], zf[oc], [BT[oc]], [BTB[1]])
        for oc in range(2):
            pb = 2 + oc
            for kc in range(2):
                MM(pbank[pb][:, 0:Tn], wg[:, kc, oc * 128:(oc + 1) * 128], zT[:, kc, :], kc == 0, kc == 1, [bwg, BTB[1]], [PB[pb]])
            gt = tmp[2][:, 0:Tn]
            ACT(gt, pbank[pb][:, 0:Tn], AF.Exp, [PB[pb], BP], [BT[2]], bias=spt[:, l, SP["hre"] + oc:SP["hre"] + oc + 1], scale=-1.0)
            ACT(gt, gt, AF.Ln, [BT[2]], [BT[2]], bias=1.0)
            ACT(gt, gt, AF.Exp, [BT[2]], [BT[2]], scale=-1.0)
            TT(mixT[:, 2 + oc, 0:Tn], zf[oc], gt, ALU.mult, [BT[oc], BT[2]], [BMIX])
        wdone()

        for b in range(4):
            MSET(pbank[b][:, :], 0.0, [PB[b]], eng="dve")
        ksegs = S["ksegs"](seg)

        def kv_load(si):
            slot, nk, diag = ksegs[si]
            r = si % 2
            LOAD(ktr[r][:, :, 0:nk], ktscr[l][slot].rearrange("p (h t) -> p h t", h=8)[:, :, 0:nk], [BKR[r]], CKR[r], R=[KTB[l][slot]])
            nkb = (nk + 127) // 128
            kp = min(128, nk)
            LOAD(vtr[r][0:kp, 0:nkb].rearrange("p t h d -> p t (h d)"), vscr[l][slot].rearrange("p (t x) -> p t x", t=NT)[0:kp, 0:nkb, :],
                 [BVR[r]], CVR[r], R=[VSB[l][slot]])

        items = []
        firsts = {}
        for si, (slot, nk, diag) in enumerate(ksegs):
            nkb = (nk + 127) // 128
            firsts[len(items)] = si
            for h in range(8):
                if diag or nkb * Tn > 512:
                    for kb in range(nkb):
                        items.append((si, h, [kb]))
                else:
                    items.append((si, h, list(range(nkb))))
        kv_load(0)
        if len(ksegs) > 1:
            kv_load(1)
        DPIPE = 2

        def att_s12(it, n):
            si, h, kbs = it
            slot, nk, diag = ksegs[si]
            r = si % 2
            pb = 4 + n % 4
            pt = ptb[n % 4]
            bpt = BPT[n % 4]
            for jj, kb in enumerate(kbs):
                kn = min(128, nk - kb * 128)
                q0 = kb * 128 if diag else 0
                MM(pbank[pb][0:kn, jj * Tn + q0:jj * Tn + Tn], ktr[r][:, h, kb * 128:kb * 128 + kn], qTa[:, h, q0:Tn], True, True,
                   [BKR[r], BQT], [PB[pb]])
            if diag:
                ACT(pt[0:kn, q0:Tn], pbank[pb][0:kn, q0:Tn], AF.Exp, [PB[pb]], [bpt])
                qe = min(Tn, q0 + 128)
                TT(pt[0:kn, q0:qe], pt[0:kn, q0:qe], trib[0:kn, 0:qe - q0], ALU.mult, [bpt, BC], [bpt])
            else:
                ACT(pt[0:kn, 0:len(kbs) * Tn], pbank[pb][0:kn, 0:len(kbs) * Tn], AF.Exp, [PB[pb]], [bpt])

        def att_s3(it, n):
            si, h, kbs = it
            slot, nk, diag = ksegs[si]
            nkb = (nk + 127) // 128
            r = si % 2
            pt = ptb[n % 4]
            bpt = BPT[n % 4]
            ab = h // 2
            acc = pbank[ab][:, (h % 2) * 256:(h % 2) * 256 + Tn]
            for jj, kb in enumerate(kbs):
                kn = min(128, nk - kb * 128)
                q0 = kb * 128 if diag else 0
                last = (si == len(ksegs) - 1) and (kb == nkb - 1)
                MM(acc[:, q0:Tn], vtr[r][0:kn, kb, h, :], pt[0:kn, jj * Tn + q0:jj * Tn + Tn], False, last, [BVR[r], bpt], [PB[ab]], skip=True)

        for i in range(len(items) + DPIPE):
            if i < len(items):
                att_s12(items[i], i)
            k_ = i - DPIPE
            if k_ >= 0:
                att_s3(items[k_], k_)
                if k_ in firsts and firsts[k_] >= 1 and firsts[k_] + 1 < len(ksegs):
                    kv_load(firsts[k_] + 1)
        for g4 in range(2):
            hs = list(range(4 * g4, 4 * g4 + 4))
            info = {}
            for hh, h in enumerate(hs):
                ab = h // 2
                acc = pbank[ab][:, (h % 2) * 256:(h % 2) * 256 + Tn]
                lo, hi_ = (0, 64) if h % 2 == 0 else (64, 128)
                l0 = 64 if h % 2 == 0 else 0
                info[h] = (ab, acc, lo, hi_, l0)
                ACT(tmp[7][l0:l0 + 1, 0:Tn], acc[l0:l0 + 1, :], AF.Ln, [PB[ab]], [BT[7]])
                ACT(rlb[l0:l0 + 1, hh, 0:Tn], tmp[7][l0:l0 + 1, 0:Tn], AF.Exp, [BT[7]], [BRL], scale=-1.0)
            for hh, h in enumerate(hs):
                MM(pbank[4 + hh][:, 0:Tn], pdown if h % 2 == 0 else pup, rlb[:, hh, 0:Tn], True, True, [BC, BRL], [PB[4 + hh]])
            for hh, h in enumerate(hs):
                ab, acc, lo, hi_, l0 = info[h]
                ACOPY(tmp[hh][lo:hi_, 0:Tn], pbank[4 + hh][lo:hi_, 0:Tn], [PB[4 + hh]], [BT[hh]])
            for hh, h in enumerate(hs):
                ab, acc, lo, hi_, l0 = info[h]
                TT(mixT[lo:hi_, 4 + h // 2, 0:Tn], acc[lo:hi_, :], tmp[hh][lo:hi_, 0:Tn], ALU.mult, [PB[ab], BT[hh]], [BMIX])

        ln_load(1 + 3 * l)
        wh = [getw(l, "wmx%d" % hf) for hf in range(2)]
        for tt in range(nt):
            for hf in range(2):
                w, bw = wh[hf]
                pb = 2 + (2 * tt + hf) % 4
                for kc in range(8):
                    MM(pbank[pb][0:P, :], mixT[:, kc, tt * 128:tt * 128 + P], w[:, kc, :], kc == 0, kc == 7, [bw, BMIX], [PB[pb]])
                resid_from_psum(S, tt, hf, pb)
            if tt == nt - 1:
                wdone()
            layer_norm(S, [tt])

        mk = S["mkT"](l)
        mv_ = S["mv"](l)
        bmk = S["bmk"](l)
        for hf in range(2):
            w, bw = getw(l, "wq%d" % hf)
            for c4 in range(4):
                pb = 2 + c4 % 2
                fm_chunk(w, bw, c4 * 128, 128, pb, Tn)
                ACOPY(qmT[:, hf * 4 + c4, 0:Tn], pbank[pb][:, 0:Tn], [PB[pb]], [BQM])
            wdone()
        def mem_a(h):
            pt = ptb[h]
            for mc in range(2):
                pb = 2 + (h % 2) * 2 + mc
                for dc in range(2):
                    MM(pbank[pb][:, 0:Tn], mk[:, 2 * h + dc, mc * 128:(mc + 1) * 128], qmT[:, 2 * h + dc, 0:Tn], dc == 0, dc == 1,
                       [bmk, BQM], [PB[pb]])
                ACT(pt[:, mc * 256:mc * 256 + Tn], pbank[pb][:, 0:Tn], AF.Exp, [PB[pb]], [BPT[h]], scale=1.0 / 16.0)

        def mem_b(h):
            pt = ptb[h]
            for mc in range(2):
                MM(pbank[6][0:1, 0:Tn], onesb[:, 0:1], pt[:, mc * 256:mc * 256 + Tn], mc == 0, mc == 1, [BC, BPT[h]], [PB[6]])
            ACT(tmp[7][0:1, 0:Tn], pbank[6][0:1, 0:Tn], AF.Ln, [PB[6]], [BT[7]])
            ACT(rl1[0:1, 0:Tn], tmp[7][0:1, 0:Tn], AF.Exp, [BT[7]], [BRL1], scale=-1.0)
            MM(pbank[6][:, 256:256 + Tn], onesb[0:1, :], rl1[0:1, 0:Tn], True, True, [BC, BRL1], [PB[6]])
            rl = tmp[h % 2][:, 0:Tn]
            ACOPY(rl, pbank[6][:, 256:256 + Tn], [PB[6]], [BT[h % 2]])
            for dc in range(2):
                pb = (7, 0)[dc]
                for mc in range(2):
                    MM(pbank[pb][:, 0:Tn], mv_[:, mc, (2 * h + dc) * 128:(2 * h + dc + 1) * 128], pt[:, mc * 256:mc * 256 + Tn],
                       mc == 0, mc == 1, [bmk, BPT[h]], [PB[pb]])
                TT(mixT[:, 2 * h + dc, 0:Tn], pbank[pb][:, 0:Tn], rl, ALU.mult, [PB[pb], BT[h % 2]], [BMIX])

        mem_a(0)
        for h in range(1, 4):
            mem_a(h)
            mem_b(h - 1)
        mem_b(3)
        ln_load(2 + 3 * l)
        wh = [getw(l, "wo%d" % hf) for hf in range(2)]
        for tt in range(nt):
            for hf in range(2):
                w, bw = wh[hf]
                pb = 2 + (2 * tt + hf) % 4
                for kc in range(8):
                    MM(pbank[pb][0:P, :], mixT[:, kc, tt * 128:tt * 128 + P], w[:, kc, :], kc == 0, kc == 7, [bw, BMIX], [PB[pb]])
                resid_from_psum(S, tt, hf, pb)
            if tt == nt - 1:
                wdone()
            layer_norm(S, [tt])

        hl = halo[:, l]
        if not samp:
            VCOPY(xTp[:, :, 0:2], xhc[:, l], [BXH[l]], [BXT])
        for j in range(11):
            w, bw = getw(l, "wup%d" % j)
            bk = [(j % 2) * 4 + r4 for r4 in range(4)]
            ys = [tmp[(j % 2) * 4 + r4][:, 0:Tn] for r4 in range(4)]
            bys = [BT[(j % 2) * 4 + r4] for r4 in range(4)]
            cwp = lambda tap, ci: spt[:, l, SP["cw%d" % tap] + ci:SP["cw%d" % tap] + ci + 1]
            c0 = 0 if samp else 2
            for r4 in range(4):
                pb = bk[r4]
                for kc in range(8):
                    if samp:
                        MM(pbank[pb][:, 0:Tn], w[:, kc, r4 * 128:(r4 + 1) * 128], xT[:, kc, 0:Tn], kc == 0, kc == 7, [bw, BXT], [PB[pb]])
                    else:
                        MM(pbank[pb][:, 0:Tn + 2], w[:, kc, r4 * 128:(r4 + 1) * 128], xTp[:, kc, 0:Tn + 2], kc == 0, kc == 7,
                           [bw, BXT], [PB[pb]])
            for r4 in range(4):
                ci = 4 * j + r4
                ACT(ys[r4], pbank[bk[r4]][:, c0:c0 + Tn], AF.Identity, [PB[bk[r4]], BP], [bys[r4]],
                    bias=spt[:, l, SP["cb"] + ci:SP["cb"] + ci + 1], scale=cwp(2, ci))
            for r4 in range(4):
                ci = 4 * j + r4
                pb = bk[r4]
                ps = pbank[pb]
                y = ys[r4]
                by = bys[r4]
                bh = BHALO[l][ci]
                if samp:
                    STT(y[:, 1:Tn], ps[:, 0:Tn - 1], cwp(1, ci), y[:, 1:Tn], ALU.mult, ALU.add, [PB[pb], BP, by], [by])
                    STT(y[:, 0:1], hl[:, ci, 1:2], cwp(1, ci), y[:, 0:1], ALU.mult, ALU.add, [bh, BP, by], [by])
                    STT(y[:, 2:Tn], ps[:, 0:Tn - 2], cwp(0, ci), y[:, 2:Tn], ALU.mult, ALU.add, [PB[pb], BP, by], [by])
                    STT(y[:, 0:2], hl[:, ci, 0:2], cwp(0, ci), y[:, 0:2], ALU.mult, ALU.add, [bh, BP, by], [by])
                    ACOPY(hl[:, ci, :], ps[:, Tn - 2:Tn], [PB[pb]], [bh])
                else:
                    STT(y, ps[:, 1:Tn + 1], cwp(1, ci), y, ALU.mult, ALU.add, [PB[pb], BP, by], [by])
                    STT(y, ps[:, 0:Tn], cwp(0, ci), y, ALU.mult, ALU.add, [PB[pb], BP, by], [by])
                    if seg == NSEG - 1:
                        ACOPY(hl[:, ci, :], ps[:, Tn:Tn + 2], [PB[pb]], [bh])
                if r4 < 2:
                    ACT(y, y, AF.Gelu_apprx_tanh, [by], [by])
            for r4 in range(2):
                TT(big[:, 2 * j + r4, 0:Tn], ys[r4], ys[2 + r4], ALU.mult, [bys[r4], bys[2 + r4]], [BBIG])
            wdone()
        if not samp:
            VCOPY(xhc[:, l], xTp[:, :, Tn:Tn + 2], [BXT], [BXH[l]])
        ln_load(3 + 3 * l)
        for hf in range(2):
            for g in range(3):
                w, bw = getw(l, "wdn%d%d" % (hf, g))
                nk = (8, 8, 6)[g]
                for tt in range(nt):
                    pb = 2 + tt
                    for kk in range(nk):
                        kc = g * 8 + kk
                        MM(pbank[pb][0:P, :], big[:, kc, tt * 128:tt * 128 + P], w[:, kk, :], kc == 0, kc == 21, [bw, BBIG], [PB[pb]])
                wdone()
            for tt in range(nt):
                resid_from_psum(S, tt, hf, 2 + tt)
        layer_norm(S, list(range(nt)))

    def gelu(y, out, R, W, ti, Tn):
        t = tmp[ti][:, 0:Tn]
        bt = BT[ti]
        ACT(t, y, AF.Square, R, [bt])
        TS(t, t, 0.044715, 1.0, ALU.mult, ALU.add, [bt], [bt])
        TT(t, t, y, ALU.mult, [bt] + R, [bt])
        ACT(t, t, AF.Sigmoid, [bt], [bt], scale=1.5957691216057308)
        TT(out, y, t, ALU.mult, R + [bt], W)

    def cmul(o_re, o_im, a_re, a_im, b_re, b_im, conj_a, R, W):
        ta = sm[:, 48:56]
        tb = sm[:, 56:64]
        TT(ta, a_re, b_re, ALU.mult, R, [BSM])
        TT(tb, a_im, b_im, ALU.mult, R, [BSM])
        TT(o_re, ta, tb, ALU.add if conj_a else ALU.subtract, [BSM], W)
        TT(ta, a_re, b_im, ALU.mult, R, [BSM])
        TT(tb, a_im, b_re, ALU.mult, R, [BSM])
        TT(o_im, ta, tb, ALU.subtract if conj_a else ALU.add, [BSM], W)

    def s5_state_out(l, dst):
        xq = sm[:, 0:16].rearrange("p (a c) -> p a c", a=2)
        so = tmp[0][:, 0:16].rearrange("p (a c) -> p a c", a=2)
        cmul(xq[:, 0, :], xq[:, 1, :], s5tab[:, l, 0, :, SB - 1], s5tab[:, l, 1, :, SB - 1], s5x[:, l, 0, :], s5x[:, l, 1, :], False,
             [BP, BS5X[l], BSM], [BSM])
        cmul(so[:, 0, :], so[:, 1, :], s5f[:, l, 0, :], s5f[:, l, 1, :], xq[:, 0, :], xq[:, 1, :], False, [BP, BSM], [BT[0]])
        STORE(dst.rearrange("a p c -> p a c"), so, [BT[0]], CS5X)

    def s5_state_in(l, src):
        h0 = tmp[0][:, 0:16].rearrange("p (a c) -> p a c", a=2)
        LOAD(h0, src.rearrange("a p c -> p a c"), [BT[0]], CS5XL)
        xq = sm[:, 0:16].rearrange("p (a c) -> p a c", a=2)
        cmul(xq[:, 0, :], xq[:, 1, :], s5f[:, l, 2, :], s5f[:, l, 3, :], h0[:, 0, :], h0[:, 1, :], False, [BP, BT[0], BSM], [BSM])
        cmul(s5x[:, l, 0, :], s5x[:, l, 1, :], s5tab[:, l, 0, :, SB - 1], s5tab[:, l, 1, :, SB - 1], xq[:, 0, :], xq[:, 1, :], True,
             [BP, BSM], [BS5X[l]])

    plan_stream(NSEG + 2)

    SP_ = dict(T=T, P=128, nt=NT, nchk=NCHK, samp=False, sidx=0,
               kslot=lambda seg: seg,
               ksegs=lambda seg: [(s, T, False) for s in range(seg)] + [(seg, T, True)],
               o_fk=lambda l: o_fkp[l], o_fv=lambda l: o_fvp[l], o_fl=lambda l: o_flp[l],
               mkT=lambda l: mkT[:, l], mv=lambda l: mvb[:, l], bmk=lambda l: BMK[l])
    for l in range(L):
        MSET(gS[:, l, :], 0.0, [BGS[l]])
        MSET(gSb[:, l, :], 0.0, [BGS[l]])
        MSET(s5x[:, l], 0.0, [BS5X[l]])
        MSET(halo[:, l], 0.0, BHALO[l])
        MSET(cfc[:, l:l + 1], 0.0, [BCFC[l]])
        MSET(xhc[:, l], 0.0, [BXH[l]])
    for seg in range(NSEG):
        ln_load(0)
        for tt in range(NT):
            LOAD(xres[:, tt, :], i_xp[seg * T + tt * 128:seg * T + (tt + 1) * 128, :], [BXR], CXIN)
        layer_norm(SP_, list(range(NT)))
        for l in range(L):
            layer(SP_, l, seg)
        for tt in range(NT):
            STORE(o_yp[seg * T + tt * 128:seg * T + (tt + 1) * 128, :], xres[:, tt, :], [BXR], COUT)
    for l in range(L):
        for h in range(4):
            STORE(o_glap[l][32 * h:32 * h + 32, :], gS[32 * h:32 * h + 32, l, 64 * h:64 * h + 64], [BGS[l]], CGS[l])
        s5_state_out(l, o_s5p[l])
        STORE(o_cvp[l], halo[:, l].rearrange("p c j -> p (c j)"), BHALO[l], CHALO[l])

    smk = mkT
    for s in range(2):
        SS = dict(T=64, P=64, nt=1, nchk=1, samp=True, sidx=s,
                  kslot=lambda seg: NPS,
                  ksegs=lambda seg: [(ps, T, False) for ps in range(NPS)] + [(NPS, 64, True)],
                  o_fk=lambda l, s=s: o_fks[l][s], o_fv=lambda l, s=s: o_fvs[l][s], o_fl=lambda l, s=s: o_fls[l][s],
                  mkT=lambda l: mkT[:, l], mv=lambda l: mvb[:, l], bmk=lambda l: BMK[l])
        for l in range(L):
            MSET(gS[:, l, :], 0.0, [BGS[l]])
            for h in range(4):
                LOAD(gS[32 * h:32 * h + 32, l, 64 * h:64 * h + 64], i_stgla[l][s][32 * h:32 * h + 32, :], [BGS[l]], CGSL[l])
            ACOPY(gSb[:, l, :], gS[:, l, :], [BGS[l]], [BGS[l]])
            s5_state_in(l, i_sts5[l][s])
            LOAD(halo[:, l].rearrange("p c j -> p (c j)"), i_cvst[l][s], BHALO[l], CHALOL[l])
            for i in range(2):
                LOAD(stg[i][:, :], i_cmkT[l][s][:, i * 1024:(i + 1) * 1024], [BSTG[i]], CSTG[i])
                VCOPY(mkT[:, l, 4 * i:4 * i + 4, :], stg[i][:, :].rearrange("p (k m) -> p k m", k=4), [BSTG[i]], [BMK[l]])
            for i in range(2):
                LOAD(stg[i][:, :], i_cmv[l][s][:, i * 1024:(i + 1) * 1024], [BSTG[i]], CSTG[i])
                VCOPY(mvb[:, l, i, :], stg[i][:, :], [BSTG[i]], [BMK[l]])
            cfp = tmp[6][0:40, 0:T]
            BXIN = BT[6]
            MSET(cfc[:, l:l + 1], 0.0, [BCFC[l]])
            for ps in range(NPS):
                MSET(tmp[6][0:40, 0:T], 0.0, [BXIN])
                LOAD(tmp[6][0:8, 0:T], i_cflf[l][s][:, ps * T:(ps + 1) * T], [BXIN], CXIN2)
                LOAD(tmp[6][32:40, 0:T], i_cflf[l][s][:, ps * T:(ps + 1) * T], [BXIN], CXIN2)
                cfo = tmp[7][0:40, 0:T]
                SCAN(cfo, ones32[0:40, 0:1].to_broadcast([40, T]), cfp, cfc[:, l:l + 1], [BXIN, BC, BCFC[l]], [BT[7]])
                VCOPY(cfc[:, l:l + 1], cfo[:, T - 1:T], [BT[7]], [BCFC[l]])
                VCOPY(augs[0:8, 0:T], cfo[0:8, :], [BT[7]], [BAUG])
                VCOPY(hi2[32:40, 0:T], cfo[32:40, :], [BT[7]], [BAUG])
                TT(augs[32:40, 0:T], cfo[32:40, :], hi2[32:40, 0:T], ALU.subtract, [BT[7], BAUG], [BAUG])
                kc_src = i_cfk[l][s].rearrange("p (h t) -> p h t", h=8)
                for h in range(8):
                    i = h % 2
                    LOAD(stg[i][0:64, 0:T], kc_src[:, h, ps * T:(ps + 1) * T], [BSTG[i]], CSTG[i])
                    VCOPY(stgb[i][0:64, 0:T], stg[i][0:64, 0:T], [BSTG[i]], [BSTGB[i]])
                    pb = 2 + i
                    MM(pbank[pb][0:68, 0:T], i64pad, stgb[i][0:64, 0:T], True, False, [BC, BSTGB[i]], [PB[pb]])
                    MM(pbank[pb][0:68, 0:T], selb[:, h * 68:(h + 1) * 68], augs[:, 0:T], False, True, [BC, BAUG], [PB[pb]])
                    ACOPY(kTa[:, h, 0:T], pbank[pb][0:68, 0:T], [PB[pb]], [BKT])
                STORE(ktscr[l][ps].rearrange("p (h t) -> p h t", h=8), kTa[:, :, 0:T], [BKT], CKT, W=[KTB[l][ps]])
                vsrc = i_cfv[l][s].rearrange("p (b x) -> p b x", x=512)
                for tt in range(NT):
                    i = tt % 2
                    LOAD(stg[i][:, 0:512], vsrc[:, ps * NT + tt, :], [BSTG[i]], CSTG[i])
                    v4 = stg[i][:, 0:512].rearrange("p (a b d) -> p a b d", a=4, b=2)
                    vo = vown[:, tt].rearrange("p (a b) d -> p a b d", a=4)
                    VCOPY(vo[:, :, 0, 0:64], v4[:, :, 0, :], [BSTG[i]], [BVO])
                    VCOPY(vo[:, :, 1, 64:128], v4[:, :, 1, :], [BSTG[i]], [BVO])
                STORE(vscr[l][ps].rearrange("p (t x) -> p t x", t=NT), vown[:, 0:NT].rearrange("p t h d -> p t (h d)"),
                      [BVO], CVO, W=[VSB[l][ps]])
        ln_load(0)
        LOAD(xres[0:64, 0, :], i_xs[s], [BXR], CXIN)
        layer_norm(SS, [0])
        for l in range(L):
            layer(SS, l, 0)
        STORE(o_ys[s], xres[0:64, 0, :], [BXR], COUT)
        for l in range(L):
            for h in range(4):
                STORE(o_glas[l][s][32 * h:32 * h + 32, :], gS[32 * h:32 * h + 32, l, 64 * h:64 * h + 64], [BGS[l]], CGS[l])
            s5_state_out(l, o_s5s[l][s])
            STORE(o_cvs[l][s], halo[:, l].rearrange("p c j -> p (c j)"), BHALO[l], CHALO[l])

    with nc.allow_low_precision(reason="bf16 matmul operands, fp32 accumulation"):
        sch.emit()
    st.close()
    return nc


def _ktile(w, nk=None):
    K, C = w.shape
    nk = K // 128
    return np.ascontiguousarray(w.reshape(nk, 128, C).transpose(1, 0, 2)).reshape(128, nk * C)


def make_consts(cfg):
    T = cfg.T
    c = np.zeros((128, 2048), np.float32)
    c[:, 0:128] = np.eye(128)
    c[64, 128:128 + 64] = 1.0
    c[0, 256 + 64:256 + 128] = 1.0
    k = np.arange(128)[:, None]
    q = np.arange(128)[None, :]
    c[:, 384:512] = (k <= q)
    c[:, 512:640] = 1.0
    s = np.arange(64)[:, None]
    l_ = np.arange(64)[None, :]
    c[0:64, 640:896] = np.tile((s <= l_).astype(np.float32), (1, 4))
    for h in range(4):
        c[32 * h:32 * h + 32, 896 + 64 * h:896 + 64 * h + 64] = 1.0
    c[0:64, 1152:1152 + 64] = np.eye(64)
    cm = np.ones(T, np.float32)
    cm[0::64] = 0.0
    c[:, 1280:1280 + T] = cm[None, :]
    sel = np.zeros((65, 2, 8, 68), np.float32)
    for h in range(8):
        sel[h, 0, h, 64] = -1.0
        sel[32 + h, 0, h, 65] = -1.0
        sel[64, 0, h, 66] = 1.0
        sel[64, 0, h, 67] = 1.0
        sel[64, 1, h, 64] = 8.0
        sel[64, 1, h, 65] = 8.0
        sel[h, 1, h, 66] = 8.0
        sel[32 + h, 1, h, 67] = 8.0
    return c, sel.reshape(65, 2 * 8 * 68)


def pack_weights(inp, l):
    w_in = inp["w_in"][l]
    Z = np.zeros
    t = {}
    ff = w_in[:, OFF["ff"]:OFF["ff"] + 8]
    ff40 = Z((D, 40), np.float32)
    ff40[:, 0:8] = ff
    ff40[:, 32:40] = ff
    t["wfm"] = np.concatenate([w_in[:, 0:256], w_in[:, OFF["su"]:OFF["su"] + 256], w_in[:, OFF["ga"]:OFF["ga"] + 16], ff40], 1)
    for nm, o in (("wfq", OFF["fq"]), ("wfk", OFF["fk"])):
        a = Z((D, 8, 68), np.float32)
        a[:, :, 0:64] = w_in[:, o:o + 512].reshape(D, 8, 64)
        t[nm] = a.reshape(D, 544)
    t["wtv"] = w_in[:, OFF["gv"]:OFF["gv"] + 512]
    t["wtk"] = w_in[:, OFF["fk"]:OFF["fk"] + 512]
    t["wtf"] = np.concatenate([w_in[:, OFF["fv"]:OFF["fv"] + 512], ff], 1)
    bre, bim = inp["s5_b_re"][l], inp["s5_b_im"][l]
    sb_ = Z((128, 8, 256), np.float32)
    for m in range(8):
        for gg in range(2):
            g = 2 * m + gg
            r0 = (g % 8) * 16
            sb_[r0:r0 + 16, m, gg * 64:(gg + 1) * 64] = bre[g].T
            sb_[r0:r0 + 16, m, 128 + gg * 64:128 + (gg + 1) * 64] = bim[g].T
    t["s5b"] = sb_.reshape(128, 8 * 256)
    cre, cim = inp["s5_c_re"][l], inp["s5_c_im"][l]
    sc_ = Z((128, 8, 256), np.float32)
    for c in range(8):
        for gg in range(2):
            g = 2 * c + gg
            c0 = (g % 8) * 16
            sc_[gg * 64:(gg + 1) * 64, c, c0:c0 + 16] = cre[g].T
            sc_[gg * 64:(gg + 1) * 64, c, 128 + c0:128 + c0 + 16] = cim[g].T
    t["s5cr"] = sc_.reshape(128, 8 * 256)
    t["s5g"] = inp["s5_w_glu"][l]
    mx = inp["w_mix_out"][l]
    t["wmx0"], t["wmx1"] = mx[:, 0:512], mx[:, 512:1024]
    for nm, key in (("wq", "mem_w_q"), ("wo", "mem_w_o"), ("wk", "mem_w_k"), ("wv", "mem_w_v")):
        t[nm + "0"], t[nm + "1"] = inp[key][l][:, 0:512], inp[key][l][:, 512:1024]
    up = inp["ffn_w_up"][l]
    for j in range(11):
        cols = np.concatenate([np.arange(ffn_feat(4 * j + r), ffn_feat(4 * j + r) + 128) for r in range(4)])
        t["wup%d" % j] = up[:, cols]
    dn = inp["ffn_w_down"][l]
    for hf in range(2):
        for g in range(3):
            nk = (8, 8, 6)[g]
            t["wdn%d%d" % (hf, g)] = dn[g * 1024:g * 1024 + nk * 128, hf * 512:(hf + 1) * 512]
    out = np.zeros((128, W_TOT), np.float32)
    for n, k, c in W_PACK:
        o = W_OFF[n][0]
        a = t[n]
        if n in ("s5b", "s5cr"):
            out[:, o:o + k * c] = a
        else:
            out[:, o:o + k * c] = _ktile(np.ascontiguousarray(a))
    return out


def pack_small(inp, l):
    sp = np.zeros((128, NSP), np.float32)
    sp[:, SP["gba"]] = inp["gla_b_a"][l]
    sp[0:8, SP["fbf"]] = inp["fox_b_f"][l]
    sp[32:40, SP["fbf"]] = inp["fox_b_f"][l]
    sp[:, SP["s5d"]:SP["s5d"] + 2] = inp["s5_d"][l].reshape(2, 128).T
    sp[:, SP["s5bg"]:SP["s5bg"] + 2] = inp["s5_b_glu"][l].reshape(2, 128).T
    idx = np.stack([ffn_feat(ci) + np.arange(128) for ci in range(NCH_FF)], 1)
    for tap in range(3):
        sp[:, SP["cw%d" % tap]:SP["cw%d" % tap] + NCH_FF] = inp["ffn_conv_w"][l][tap][idx]
    sp[:, SP["cb"]:SP["cb"] + NCH_FF] = inp["ffn_conv_b"][l][idx]
    for nm, key in (("lre", "s5_lam_re"), ("lim", "s5_lam_im"), ("ldt", "s5_log_dt")):
        sp[:, SP[nm]:SP[nm] + 8] = inp[key][l].reshape(8, 128).T
    for h in range(4):
        sp[32 * h:32 * h + 32, SP["hm"] + h] = 1.0
    tp = np.zeros((128, NTP), np.float32)
    tp[:, TP["gng"]:TP["gng"] + 256] = np.tile(inp["gla_norm_g"][l], 4)[None, :]
    tp[:, TP["fbf"]:TP["fbf"] + 8] = inp["fox_b_f"][l][None, :]
    return sp, tp


def prep_core(inp, c, cfg, shared):
    b = c % 2
    PAST, T = cfg.PAST, cfg.T
    ss = [2 * c, 2 * c + 1]
    m = {}
    m["xp"] = np.ascontiguousarray(inp["x_prompt"][b])
    m["xs"] = np.ascontiguousarray(inp["x_sample"][ss])
    m["memT"] = _ktile(np.ascontiguousarray(inp["mem_prompt"][b].T))
    m["stgla"] = np.ascontiguousarray(inp["state_gla"][:, ss].reshape(L, 2, 128, 64))
    s5 = np.stack([inp["state_s5_re"][:, ss], inp["state_s5_im"][:, ss]], 2)
    m["sts5"] = np.ascontiguousarray(s5.reshape(L, 2, 2, 8, 128).transpose(0, 1, 2, 4, 3))
    ck = inp["cache_fox_k"][:, ss]
    m["cfk"] = np.ascontiguousarray(ck.transpose(0, 1, 4, 3, 2)).reshape(L, 2, 64, 8 * PAST)
    cv = inp["cache_fox_v"][:, ss].reshape(L, 2, PAST // 128, 128, 512)
    m["cfv"] = np.ascontiguousarray(cv.transpose(0, 1, 3, 2, 4)).reshape(L, 2, 128, (PAST // 128) * 512)
    m["cflf"] = np.ascontiguousarray(inp["cache_fox_logf"][:, ss].transpose(0, 1, 3, 2))
    mk = inp["cache_mem_k"][:, ss].reshape(L, 2, 256, 1024)
    m["cmkT"] = np.ascontiguousarray(mk.transpose(0, 1, 3, 2).reshape(L, 2, 8, 128, 256).transpose(0, 1, 3, 2, 4)).reshape(L, 2, 128, 2048)
    mv = inp["cache_mem_v"][:, ss].reshape(L, 2, 2, 128, 1024)
    m["cmv"] = np.ascontiguousarray(mv.transpose(0, 1, 3, 2, 4)).reshape(L, 2, 128, 2048)
    idx = np.stack([ffn_feat(ci) + np.arange(128) for ci in range(NCH_FF)], 1)
    cs = inp["state_ffn_conv"][:, ss]
    m["cvst"] = np.ascontiguousarray(cs[:, :, :, idx].transpose(0, 1, 3, 4, 2)).reshape(L, 2, 128, NCH_FF * 2)
    m.update(shared)
    return m


def kernel(_cfg=None, **inp):
    cfg = _cfg or Cfg()
    SEQ, PAST, T = cfg.SEQ, cfg.PAST, cfg.T
    inp = {k: np.asarray(v) for k, v in inp.items()}
    shared = {}
    shared["wpk"] = np.stack([pack_weights(inp, l) for l in range(L)])
    lnp = np.zeros((1 + 3 * L, 128, 2 * D), np.float32)
    lnp[0, :, 0:D] = inp["ln_in_g"][None]
    lnp[0, :, D:] = inp["ln_in_b"][None]
    for l in range(L):
        for j, nm in enumerate(("ln1", "ln2", "ln3")):
            lnp[1 + 3 * l + j, :, 0:D] = inp[nm + "_g"][l][None]
            lnp[1 + 3 * l + j, :, D:] = inp[nm + "_b"][l][None]
    shared["lnp"] = lnp
    sps, tps = zip(*[pack_small(inp, l) for l in range(L)])
    shared["sp"] = np.stack(sps)
    shared["tp"] = np.stack(tps)
    shared["wa2"] = np.ascontiguousarray(inp["gla_w_a2"])
    shared["cst"], shared["sel"] = make_consts(cfg)
    nc = build(cfg)
    in_maps = [prep_core(inp, c, cfg, shared) for c in range(NCORES)]
    res = run_bass_kernel_spmd(nc, in_maps, core_ids=list(range(NCORES)))
    R = res.results
    B = 2
    f32 = np.float32
    yp = np.stack([R[b]["yp"] for b in range(B)])
    ys = np.concatenate([R[c]["ys"] for c in range(NCORES)], 0)

    def stackp(key, shape):
        return np.stack([R[b][key] for b in range(B)], 1).reshape(shape)

    gla_p = stackp("glap", (L, B, 4, 32, 64))
    s5p = np.stack([R[b]["s5p"] for b in range(B)], 1)
    s5p = s5p.transpose(0, 1, 2, 4, 3).reshape(L, B, 2, 16, 64)
    fk_p = stackp("fkp", (L, B, SEQ, 8, 64))
    fv_p = stackp("fvp", (L, B, SEQ, 8, 64))
    fl_p = stackp("flp", (L, B, SEQ, 8))
    idx = np.stack([ffn_feat(ci) + np.arange(128) for ci in range(NCH_FF)], 1)

    def conv_out(a):
        a = a.reshape(a.shape[:-1] + (NCH_FF, 2))
        o = np.zeros(a.shape[:-3] + (2, 2 * DFF), f32)
        o[..., :, idx] = np.moveaxis(a, -1, -3)
        return o

    cv_p = conv_out(np.stack([R[b]["cvp"] for b in range(B)], 1))
    mk_p = stackp("mkp", (L, B, 256, 4, 256))
    mv_p = stackp("mvp", (L, B, 256, 4, 256))

    def cats(key):
        return np.concatenate([R[c][key] for c in range(NCORES)], 1)

    gla_s = cats("glas").reshape(L, 16, 4, 32, 64)
    s5s = cats("s5s").transpose(0, 1, 2, 4, 3).reshape(L, 16, 2, 16, 64)
    fk_s = cats("fks").reshape(L, 16, 64, 8, 64)
    fv_s = cats("fvs").reshape(L, 16, 64, 8, 64)
    fl_s = cats("fls").reshape(L, 16, 64, 8)
    cv_s = conv_out(cats("cvs"))
    outs = (yp, ys, gla_p, s5p[:, :, 0], s5p[:, :, 1], fk_p, fv_p, fl_p, cv_p, mk_p, mv_p,
            gla_s, s5s[:, :, 0], s5s[:, :, 1], fk_s, fv_s, fl_s, cv_s)
    return tuple(np.ascontiguousarray(o, dtype=f32) for o in outs)
```

```python
import math
from contextlib import ExitStack

import numpy as np
import concourse.bass as bass
import concourse.mybir as mybir
from concourse.bass_utils import run_bass_kernel_spmd

F32 = mybir.dt.float32
BF16 = mybir.dt.bfloat16
ALU = mybir.AluOpType
AF = mybir.ActivationFunctionType
AX = mybir.AxisListType

D = 1024
L = 2
DFF = 2816
NCH_FF = 44
LN_EPS = 1e-5
ALPHA = float((2 * L) ** 0.25)
NCORES = 8


class Buf:
    __slots__ = ("name", "excl", "last_w", "readers")

    def __init__(self, name, excl=False):
        self.name = name
        self.excl = excl
        self.last_w = None
        self.readers = []


class Chan:
    __slots__ = ("sem", "count")

    def __init__(self, sem):
        self.sem = sem
        self.count = 0


class Op:
    __slots__ = ("engine", "fn", "deps", "signal", "sigidx", "chan", "chanval")

    def __init__(self, engine, fn, chan):
        self.engine = engine
        self.fn = fn
        self.deps = []
        self.signal = False
        self.sigidx = 0
        self.chan = chan
        self.chanval = 0


class Sched:
    ENGS = ("pe", "act", "dve", "pool", "sp")

    def __init__(self, nc, stack):
        self.nc = nc
        self.stack = stack
        self.ops = {e: [] for e in self.ENGS}
        self.esem = {e: stack.enter_context(nc.semaphore("es_" + e)) for e in ("pe", "act", "dve", "pool")}
        self.chans = []
        self.nops = 0

    def chan(self, name):
        c = Chan(self.stack.enter_context(self.nc.semaphore("ch_" + name)))
        self.chans.append(c)
        return c

    def add(self, eng, fn, reads=(), writes=(), chan=None):
        op = Op(eng, fn, chan)
        if chan is not None:
            chan.count += 16
            op.chanval = chan.count
        deps = {}
        for b in reads:
            if b.last_w is not None:
                deps[id(b.last_w)] = b.last_w
            if b.excl:
                for r in b.readers:
                    deps[id(r)] = r
        for b in writes:
            if b.last_w is not None:
                deps[id(b.last_w)] = b.last_w
            for r in b.readers:
                deps[id(r)] = r
        for d in deps.values():
            if d is op:
                continue
            if d.chan is None:
                if d.engine == eng and eng == "pe":
                    continue
                d.signal = True
            op.deps.append(d)
        for b in reads:
            if b.excl:
                b.last_w = op
                b.readers = []
            else:
                b.readers.append(op)
        for b in writes:
            b.last_w = op
            b.readers = []
        self.ops[eng].append(op)
        self.nops += 1
        return op

    def emit(self):
        nc = self.nc
        for e in ("pe", "act", "dve", "pool"):
            n = 0
            for op in self.ops[e]:
                if op.signal:
                    n += 1
                    op.sigidx = n
        chans = self.chans
        esem = self.esem

        def run(ename, e):
            waited = {}
            for op in self.ops[ename]:
                need = {}
                for d in op.deps:
                    if d.chan is not None:
                        sem, val = d.chan.sem, d.chanval
                    else:
                        sem, val = esem[d.engine], d.sigidx
                    k = id(sem)
                    if k not in need or need[k][1] < val:
                        need[k] = (sem, val)
                for k, (sem, val) in need.items():
                    if waited.get(k, 0) >= val:
                        continue
                    waited[k] = val
                    e.wait_ge(sem, val)
                inst = op.fn(e)
                if op.chan is not None:
                    inst.then_inc(op.chan.sem, 16)
                elif op.signal:
                    inst.then_inc(esem[ename], 1)
            if ename == "sp":
                for c in chans:
                    if c.count > 0 and waited.get(id(c.sem), 0) < c.count:
                        e.wait_ge(c.sem, c.count)

        with nc.Block() as block:
            @block.tensor
            def _(e):
                run("pe", e)

            @block.scalar
            def _(e):
                run("act", e)

            @block.vector
            def _(e):
                run("dve", e)

            @block.gpsimd
            def _(e):
                run("pool", e)

            @block.sync
            def _(e):
                run("sp", e)


OFF = dict(gq=0, gk=128, gv=256, go=512, ga=768, su=784, fq=1040, fk=1552, fv=2064, ff=2576)

W_STREAM = [("wfm", 8, 568), ("wfq", 8, 544), ("wfk", 8, 544), ("wtv", 8, 512), ("wtk", 8, 512), ("wtf", 8, 520),
            ("s5b", 8, 256), ("s5c", 8, 256), ("s5g", 2, 256),
            ("wmx0", 8, 512), ("wmx1", 8, 512), ("wq0", 8, 512), ("wq1", 8, 512), ("wo0", 8, 512), ("wo1", 8, 512)]
W_STREAM += [("wup%d" % j, 8, 512) for j in range(11)]
W_STREAM += [("wdn%d%d" % (h, g), (8, 8, 6)[g], 512) for h in range(2) for g in range(3)]
W_PACK = [t for t in W_STREAM if t[0] != "s5c"] + [("wk0", 8, 512), ("wk1", 8, 512), ("wv0", 8, 512), ("wv1", 8, 512),
                                                    ("s5cr", 8, 256)]
W_OFF = {}
_o = 0
for _n, _k, _c in W_PACK:
    W_OFF[_n] = (_o, _k, _c)
    _o += _k * _c
W_TOT = _o
SLOT_COLS = 8 * 568

SP = dict(gba=0, fbf=1, s5d=2, s5bg=4, cw0=6, cw1=50, cw2=94, cb=138, lre=182, lim=190, ldt=198, hm=206, hre=210, him=218)
NSP = 226
TP = dict(gng=0, fbf=256)
NTP = 264


def ffn_feat(ci):
    j, r = divmod(ci, 4)
    if r < 2:
        return (2 * j + r) * 128
    return DFF + (2 * j + r - 2) * 128


class Cfg:
    def __init__(self, SEQ=8192, PAST=2048, T=256):
        self.SEQ, self.PAST, self.T = SEQ, PAST, T
        self.NSEG = SEQ // T
        self.NPS = PAST // T
        assert SEQ % T == 0 and PAST % T == 0 and T % 128 == 0


def build(cfg):
    SEQ, PAST, T = cfg.SEQ, cfg.PAST, cfg.T
    NSEG, NPS = cfg.NSEG, cfg.NPS
    SB = 64
    nc = bass.Bass("TRN2", target_bir_lowering=False)
    st = ExitStack()
    sch = Sched(nc, st)
    add = sch.add

    def din(name, shape):
        return nc.dram_tensor(name, list(shape), F32, kind="ExternalInput").ap()

    def dout(name, shape):
        return nc.dram_tensor(name, list(shape), F32, kind="ExternalOutput").ap()

    def dscr(name, shape, dt=BF16):
        return nc.dram_tensor(name, list(shape), dt, kind="Internal").ap()

    def sb(name, shape, dt=F32):
        return st.enter_context(nc.sbuf_tensor("s_" + name, list(shape), dt))

    i_xp = din("xp", [SEQ, D])
    i_xs = din("xs", [2, 64, D])
    i_memT = din("memT", [128, 8 * 256])
    i_stgla = din("stgla", [L, 2, 128, 64])
    i_sts5 = din("sts5", [L, 2, 2, 128, 8])
    i_cfk = din("cfk", [L, 2, 64, 8 * PAST])
    i_cfv = din("cfv", [L, 2, 128, (PAST // 128) * 512])
    i_cflf = din("cflf", [L, 2, 8, PAST])
    i_cmkT = din("cmkT", [L, 2, 128, 8 * 256])
    i_cmv = din("cmv", [L, 2, 128, 2 * 1024])
    i_cvst = din("cvst", [L, 2, 128, NCH_FF * 2])
    i_wpk = din("wpk", [L, 128, W_TOT])
    i_lnp = din("lnp", [1 + 3 * L, 128, 2 * D])
    i_sp = din("sp", [L, 128, NSP])
    i_tp = din("tp", [L, 128, NTP])
    i_wa2 = din("wa2", [L, 16, 128])
    i_cst = din("cst", [128, 2048])
    i_sel = din("sel", [65, 2 * 8 * 68])

    o_yp = dout("yp", [SEQ, D])
    o_ys = dout("ys", [2, 64, D])
    o_glap = dout("glap", [L, 128, 64])
    o_s5p = dout("s5p", [L, 2, 128, 8])
    o_fkp = dout("fkp", [L, SEQ, 512])
    o_fvp = dout("fvp", [L, SEQ, 512])
    o_flp = dout("flp", [L, SEQ, 8])
    o_cvp = dout("cvp", [L, 128, NCH_FF * 2])
    o_mkp = dout("mkp", [L, 256, D])
    o_mvp = dout("mvp", [L, 256, D])
    o_glas = dout("glas", [L, 2, 128, 64])
    o_s5s = dout("s5s", [L, 2, 2, 128, 8])
    o_fks = dout("fks", [L, 2, 64, 512])
    o_fvs = dout("fvs", [L, 2, 64, 512])
    o_fls = dout("fls", [L, 2, 64, 8])
    o_cvs = dout("cvs", [L, 2, 128, NCH_FF * 2])

    wscr = {}
    for l in range(L):
        for n, k, c in W_PACK + [("s5c", 8, 256)]:
            if n == "s5cr":
                continue
            wscr[(l, n)] = dscr("w_%d_%s" % (l, n), [128, k * c])
    NKS = max(NSEG, NPS + 1)
    ktscr = [dscr("kts%d" % l, [NKS, 68, 8 * T]) for l in range(L)]
    vscr = [dscr("vs%d" % l, [NKS, 128, (T // 128) * 8 * 128]) for l in range(L)]
    KTB = [[Buf("kts%d_%d" % (l, s)) for s in range(NKS)] for l in range(L)]
    VSB = [[Buf("vs%d_%d" % (l, s)) for s in range(NKS)] for l in range(L)]
    WSB = {k: Buf("wscr") for k in wscr}

    pbank = [st.enter_context(nc.psum_tensor("pb%d" % i, [128, 512], F32)) for i in range(8)]
    PB = [Buf("pb%d" % i, excl=True) for i in range(8)]

    def pbf(i):
        return pbank[i][:].bitcast(BF16)

    NT = T // 128
    NCHK = T // 64
    cst = sb("cst", [128, 2048])
    cstb = sb("cstb", [128, 1024], BF16)
    selb = sb("selb", [65, 2 * 8 * 68], BF16)
    self32 = sb("self32", [65, 2 * 8 * 68])
    C_ID, C_PD, C_PU, C_TRI, C_ONE = 0, 128, 256, 384, 512
    C_CM4, C_BM, C_I64, C_CHM = 640, 896, 1152, 1280
    ident = cstb[:, 0:128]
    pdown = cstb[:, 128:256]
    pup = cstb[:, 256:384]
    trib = cstb[:, 384:512]
    onesb = cstb[:, 512:640]
    i64pad = cstb[0:64, 640:708]
    cmask4 = cst[0:64, C_CM4:C_CM4 + 256]
    blockmask = cst[:, C_BM:C_BM + 256]
    chunkmask = cst[:, C_CHM:C_CHM + T]
    ones32 = cst[:, C_ONE:C_ONE + 128]
    BC = Buf("consts")

    spt = sb("spt", [128, L, NSP])
    tpt = sb("tpt", [128, L, NTP])
    wa2b = sb("wa2b", [16, L, 128], BF16)
    wa2f = sb("wa2f", [16, L, 128])
    s5tab = sb("s5tab", [128, L, 2, 8, SB])
    s5r = sb("s5r", [128, L, 8])
    s5f = sb("s5f", [128, L, 4, 8])
    BP = BC

    xres = sb("xres", [128, NT, D])
    BXR = Buf("xres")
    xbt = sb("xbt", [128, 2, D], BF16)
    BXB = [Buf("xbt%d" % i) for i in range(2)]
    xTp = sb("xTp", [128, 8, T + 2], BF16)
    xT = xTp[:, :, 2:T + 2]
    BXT = Buf("xT")
    xhc = sb("xhc", [128, L, 8, 2], BF16)
    BXH = [Buf("xhc%d" % l) for l in range(L)]
    NWS = 3
    wring = [sb("wring%d" % i, [128, SLOT_COLS], BF16) for i in range(NWS)]
    BW = [Buf("wring%d" % i) for i in range(NWS)]
    CW = [sch.chan("w%d" % i) for i in range(NWS)]
    lnt = sb("lnt", [128, 2 * D])
    BLN = Buf("lnt")
    CLN = sch.chan("ln")
    s5bt = sb("s5bt", [128, 2, 8 * T])
    BS5H = [Buf("s5bt%d" % i) for i in range(2)]
    stg = [s5bt[:, i, 0:1024] for i in range(2)]
    BSTG = BS5H
    CSTG = [sch.chan("stg%d" % i) for i in range(2)]
    stgb = [sb("stgb%d" % i, [128, 1024], BF16) for i in range(2)]
    BSTGB = [Buf("stgb%d" % i) for i in range(2)]
    CSTGB = [sch.chan("stgb%d" % i) for i in range(2)]
    mkT = sb("mkT", [128, L, 8, 256], BF16)
    mvb = sb("mvb", [128, L, 2, D], BF16)
    BMK = [Buf("mk%d" % l) for l in range(L)]
    gS = sb("gS", [128, L, 256])
    gSb = sb("gSb", [128, L, 256], BF16)
    BGS = [Buf("gS%d" % l) for l in range(L)]
    CGS = [sch.chan("gS%d" % l) for l in range(L)]
    CGSL = [sch.chan("gSl%d" % l) for l in range(L)]
    s5x = sb("s5x", [128, L, 2, 8])
    BS5X = [Buf("s5x%d" % l) for l in range(L)]
    s5e = sb("s5e", [128, L, 2, 8])
    CS5X = sch.chan("s5x")
    CS5XL = sch.chan("s5xl")
    halo = sb("halo", [128, L, NCH_FF, 2])
    BHALO = [[Buf("halo%d_%d" % (l, ci)) for ci in range(NCH_FF)] for l in range(L)]
    CHALO = [sch.chan("halo%d" % l) for l in range(L)]
    CHALOL = [sch.chan("halol%d" % l) for l in range(L)]
    cfc = sb("cfc", [40, L])
    BCFC = [Buf("cfc%d" % l) for l in range(L)]

    gqk = sb("gqk", [128, 2, T])
    BGQK = Buf("gqk")
    suT = sb("suT", [128, 2, T])
    BSU = Buf("suT")
    sub = sb("sub", [128, 2, T], BF16)
    gaT = sb("gaT", [16, T], BF16)
    BGA = Buf("gaT")
    ffT = sb("ffT", [40, T])
    BFF = Buf("ffT")
    qTa = sb("qTa", [68, 8, T], BF16)
    BQT = Buf("qTa")
    kTa = sb("kTa", [68, 8, T], BF16)
    BKT = Buf("kTa")
    CKT = sch.chan("kTa")
    vtb = sb("vtb", [64, NCHK, 256], BF16)
    gtk = sb("gtk", [64, NCHK, 256])
    BVT = Buf("vtok")
    ost = [sb("ost%d" % i, [128, 520]) for i in range(2)]
    BOST = [Buf("ost%d" % i) for i in range(2)]
    COST = [sch.chan("ost%d" % i) for i in range(2)]
    vown = sb("vown", [128, NT, 8, 128], BF16)
    BVO = Buf("vown")
    CVO = sch.chan("vown")
    NTMP = 8
    tmp = [sb("tmp%d" % i, [128, T + 2]) for i in range(NTMP)]
    BT = [Buf("tmp%d" % i) for i in range(NTMP)]
    tmpb = [sb("tmpb%d" % i, [128, 4, T], BF16) for i in range(2)]
    BTB = [Buf("tmpb%d" % i) for i in range(2)]
    s5t = sb("s5t", [128, 10, 8])
    BS5T = Buf("s5t")
    sm = sb("sm", [128, 64])
    smt = sb("smt", [128, 32])
    BSMT = [Buf("smt%d" % i) for i in range(2)]
    BSM = Buf("sm")
    augs = sb("augs", [65, T], BF16)
    BAUG = Buf("augs")
    hi2 = sb("hi2", [40, T], BF16)
    big = sb("big", [128, 22, T], BF16)
    BBIG = Buf("big")
    mixT = sb("mixT", [128, 8, T], BF16)
    BMIX = Buf("mixT")
    qmT = sb("qmT", [128, 8, T], BF16)
    BQM = Buf("qmT")
    ktr = [sb("ktr%d" % i, [68, 8, T], BF16) for i in range(2)]
    BKR = [Buf("ktr%d" % i) for i in range(2)]
    CKR = [sch.chan("ktr%d" % i) for i in range(2)]
    vtr = [sb("vtr%d" % i, [128, NT, 8, 128], BF16) for i in range(2)]
    BVR = [Buf("vtr%d" % i) for i in range(2)]
    CVR = [sch.chan("vtr%d" % i) for i in range(2)]
    ptb = [sb("ptb%d" % i, [128, 512], BF16) for i in range(4)]
    BPT = [Buf("ptb%d" % i) for i in range(4)]
    rlb = sb("rlb", [128, 4, T], BF16)
    BRL = Buf("rlb")
    rl1 = sb("rl1", [1, T], BF16)
    BRL1 = Buf("rl1")
    CXIN = sch.chan("xin")
    CXIN2 = sch.chan("xin2")
    COUT = sch.chan("out")
    CPAR = sch.chan("par")

    def MM(out, lhsT, rhs, start, stop, R, W, skip=False):
        if skip:
            add("pe", lambda e: e.matmul(out, lhsT, rhs, start=start, stop=stop, skip_group_check=True), reads=R, writes=W)
        else:
            add("pe", lambda e: e.matmul(out, lhsT, rhs, start=start, stop=stop), reads=R, writes=W)

    def TR(out, in_, idn, R, W):
        add("pe", lambda e: e.transpose(out, in_, idn), reads=R + [BC], writes=W)

    def ACT(out, in_, func, R, W, bias=None, scale=1.0, accum=None):
        kw = {}
        if bias is not None:
            kw["bias"] = bias
        if accum is not None:
            kw["accum_out"] = accum
        add("act", lambda e: e.activation(out=out, in_=in_, func=func, scale=scale, **kw), reads=R, writes=W)

    def ACOPY(out, in_, R, W):
        add("act", lambda e: e.copy(out=out, in_=in_), reads=R, writes=W)

    def VCOPY(out, in_, R, W, eng="dve"):
        add(eng, lambda e: e.tensor_copy(out=out, in_=in_), reads=R, writes=W)

    def TT(out, a, b, op, R, W, eng="dve"):
        add(eng, lambda e: e.tensor_tensor(out=out, in0=a, in1=b, op=op), reads=R, writes=W)

    def TS(out, a, s1, s2, op0, op1, R, W, eng="dve"):
        if op1 is None:
            add(eng, lambda e: e.tensor_scalar(out=out, in0=a, scalar1=s1, scalar2=None, op0=op0), reads=R, writes=W)
        else:
            add(eng, lambda e: e.tensor_scalar(out=out, in0=a, scalar1=s1, scalar2=s2, op0=op0, op1=op1), reads=R, writes=W)

    def STT(out, a, s, b, op0, op1, R, W):
        add("dve", lambda e: e.scalar_tensor_tensor(out=out, in0=a, scalar=s, in1=b, op0=op0, op1=op1), reads=R, writes=W)

    def SCAN(out, d0, d1, init, R, W):
        add("dve", lambda e: e.tensor_tensor_scan(out=out, data0=d0, data1=d1, initial=init, op0=ALU.mult, op1=ALU.add),
            reads=R, writes=W)

    def RECIP(out, in_, R, W):
        add("dve", lambda e: e.reciprocal(out=out, in_=in_), reads=R, writes=W)

    def MSET(ap, v, W, eng="pool"):
        add(eng, lambda e: e.memset(ap, v), writes=W)

    def LOAD(out, in_, W, chan, R=()):
        add("sp", lambda e: e.dma_start(out=out, in_=in_), reads=list(R), writes=W, chan=chan)

    def STORE(out, in_, R, chan, W=()):
        add("pool", lambda e: e.dma_start(out=out, in_=in_), reads=R, writes=list(W), chan=chan)

    wstate = dict(next_load=0, next_use=0)
    wseq = []

    def plan_stream(nstreams):
        for _ in range(nstreams):
            for l in range(L):
                for n, k, c in W_STREAM:
                    wseq.append((l, n, k, c))

    def issue_wload():
        i = wstate["next_load"]
        if i >= len(wseq):
            return
        l, n, k, c = wseq[i]
        slot = i % NWS
        LOAD(wring[slot][:, 0:k * c], wscr[(l, n)], [BW[slot]], CW[slot], R=[WSB[(l, n)]])
        wstate["next_load"] = i + 1

    def getw(l, name):
        i = wstate["next_use"]
        assert wseq[i][0] == l and wseq[i][1] == name, (wseq[i], l, name)
        while wstate["next_load"] <= i:
            issue_wload()
        k, c = wseq[i][2], wseq[i][3]
        slot = i % NWS
        wstate["next_use"] = i + 1
        return wring[slot][:, 0:k * c].rearrange("p (k c) -> p k c", k=k), BW[slot]

    def wdone():
        while wstate["next_load"] < min(len(wseq), wstate["next_use"] + NWS - 1):
            issue_wload()

    LOAD(cst[:], i_cst, [BC], CPAR)
    LOAD(self32[:], i_sel, [BC], CPAR)
    LOAD(spt[:], i_sp.rearrange("l p n -> p l n"), [BP], CPAR)
    LOAD(tpt[:], i_tp.rearrange("l p n -> p l n"), [BP], CPAR)
    LOAD(wa2f[:], i_wa2.rearrange("l p n -> p l n"), [BP], CPAR)
    VCOPY(cstb[:, 0:640], cst[:, 0:640], [BC], [BC])
    VCOPY(cstb[0:64, 640:708], cst[0:64, C_I64:C_I64 + 68], [BC], [BC])
    VCOPY(selb[:], self32[:], [BC], [BC])
    VCOPY(wa2b[:], wa2f[:], [BP], [BP])
    MSET(rlb[:], 0.0, [BRL])
    MSET(augs[:], 0.0, [BAUG])
    MSET(augs[64:65, :], 1.0, [BAUG])
    MSET(hi2[:], 0.0, [BAUG])
    MSET(ffT[:], 0.0, [BFF])
    for i in range(2):
        MSET(vtr[i][:], 1.0, [BVR[i]])
    MSET(vown[:], 1.0, [BVO])

    castn = [0]

    def cast_pieces(src_ap, dst_ap, ncols, dstbuf):
        c0 = 0
        while c0 < ncols:
            cw = min(1024, ncols - c0)
            i = castn[0] % 2
            castn[0] += 1
            LOAD(stg[i][:, 0:cw], src_ap[:, c0:c0 + cw], [BSTG[i]], CSTG[i])
            eng = ("dve", "act", "pool")[castn[0] % 3]
            if eng == "act":
                ACOPY(stgb[i][:, 0:cw], stg[i][:, 0:cw], [BSTG[i]], [BSTGB[i]])
            else:
                VCOPY(stgb[i][:, 0:cw], stg[i][:, 0:cw], [BSTG[i]], [BSTGB[i]], eng=eng)
            STORE(dst_ap[:, c0:c0 + cw], stgb[i][:, 0:cw], [BSTGB[i]], CSTGB[i], W=[dstbuf])
            c0 += cw

    for l in range(L):
        for n, k, c in W_PACK:
            if n == "s5cr":
                continue
            o = W_OFF[n][0]
            cast_pieces(i_wpk[l][:, o:o + k * c], wscr[(l, n)], k * c, WSB[(l, n)])

    def s5_prep(l):
        lre = spt[:, l, SP["lre"]:SP["lre"] + 8]
        lim = spt[:, l, SP["lim"]:SP["lim"] + 8]
        ldt = spt[:, l, SP["ldt"]:SP["ldt"] + 8]
        t = lambda i: s5t[:, i, :]
        R = [BP]
        Wt = [BS5T]
        ACT(t(0), ldt, AF.Exp, R, Wt)
        TT(t(1), lre, t(0), ALU.mult, R + Wt, Wt)
        TT(t(2), lim, t(0), ALU.mult, R + Wt, Wt)
        ACT(s5r[:, l, :], t(1), AF.Exp, Wt, [BP])
        ACT(t(3), t(2), AF.Sin, Wt, Wt, scale=1.0 / 16)
        TS(t(4), t(2), 1.0 / 16, math.pi / 2, ALU.mult, ALU.add, Wt, Wt)
        ACT(t(4), t(4), AF.Sin, Wt, Wt)
        for _ in range(4):
            TT(t(5), t(4), t(4), ALU.mult, Wt, Wt)
            TT(t(6), t(3), t(3), ALU.mult, Wt, Wt)
            TT(t(3), t(3), t(4), ALU.mult, Wt, Wt)
            TS(t(3), t(3), 2.0, None, ALU.mult, None, Wt, Wt)
            TT(t(4), t(5), t(6), ALU.subtract, Wt, Wt)
        cosT = s5tab[:, l, 0]
        sinT = s5tab[:, l, 1]
        MSET(cosT[:, :, 0:1], 1.0, [BP])
        MSET(sinT[:, :, 0:1], 0.0, [BP])
        VCOPY(t(5), t(4), Wt, Wt)
        VCOPY(t(6), t(3), Wt, Wt)
        n = 1
        while n < SB:
            dc = t(5).unsqueeze(2).to_broadcast([128, 8, n])
            ds = t(6).unsqueeze(2).to_broadcast([128, 8, n])
            a = tmp[1][:, 0:8 * n].rearrange("p (c j) -> p c j", c=8)
            b = tmp[2][:, 0:8 * n].rearrange("p (c j) -> p c j", c=8)
            RW = Wt + [BP, BT[1], BT[2]]
            TT(a, cosT[:, :, 0:n], dc, ALU.mult, RW, [BT[1]])
            TT(b, sinT[:, :, 0:n], ds, ALU.mult, RW, [BT[2]])
            TT(cosT[:, :, n:2 * n], a, b, ALU.subtract, RW, [BP])
            TT(a, cosT[:, :, 0:n], ds, ALU.mult, RW, [BT[1]])
            TT(b, sinT[:, :, 0:n], dc, ALU.mult, RW, [BT[2]])
            TT(sinT[:, :, n:2 * n], a, b, ALU.add, RW, [BP])
            TT(t(7), t(5), t(5), ALU.mult, Wt, Wt)
            TT(t(8), t(6), t(6), ALU.mult, Wt, Wt)
            TT(t(6), t(5), t(6), ALU.mult, Wt, Wt)
            TS(t(6), t(6), 2.0, None, ALU.mult, None, Wt, Wt)
            TT(t(5), t(7), t(8), ALU.subtract, Wt, Wt)
            n *= 2
        TS(spt[:, l, SP["hre"]:SP["hre"] + 2], spt[:, l, SP["s5bg"]:SP["s5bg"] + 2], -1.0, None, ALU.mult, None, [BP], [BP])
        VCOPY(s5e[:, l, 0, :], t(5), Wt, [BP])
        VCOPY(s5e[:, l, 1, :], t(6), Wt, [BP])
        mag = s5r[:, l, :]
        u = lambda i: tmp[4][:, 8 * i:8 * i + 8]
        W4 = [BT[4]]
        RR = R + Wt + W4
        TT(u(0), mag, t(4), ALU.mult, RR, W4)
        TS(u(0), u(0), -1.0, None, ALU.add, None, RR, W4)
        TT(u(1), mag, t(3), ALU.mult, RR, W4)
        TT(u(2), lre, lre, ALU.mult, RR, W4)
        TT(u(3), lim, lim, ALU.mult, RR, W4)
        TT(u(2), u(2), u(3), ALU.add, RR, W4)
        RECIP(u(2), u(2), RR, W4)
        TT(u(3), u(0), lre, ALU.mult, RR, W4)
        TT(u(4), u(1), lim, ALU.mult, RR, W4)
        TT(u(3), u(3), u(4), ALU.add, RR, W4)
        TT(s5f[:, l, 0, :], u(3), u(2), ALU.mult, RR, [BP])
        TT(u(3), u(1), lre, ALU.mult, RR, W4)
        TT(u(4), u(0), lim, ALU.mult, RR, W4)
        TT(u(3), u(3), u(4), ALU.subtract, RR, W4)
        TT(s5f[:, l, 1, :], u(3), u(2), ALU.mult, RR, [BP])
        TT(u(5), s5f[:, l, 0, :], s5f[:, l, 0, :], ALU.mult, RR, W4)
        TT(u(6), s5f[:, l, 1, :], s5f[:, l, 1, :], ALU.mult, RR, W4)
        TT(u(5), u(5), u(6), ALU.add, RR, W4)
        RECIP(u(5), u(5), RR, W4)
        TT(s5f[:, l, 2, :], s5f[:, l, 0, :], u(5), ALU.mult, RR, [BP])
        TT(s5f[:, l, 3, :], s5f[:, l, 1, :], u(5), ALU.mult, RR, [BP])
        TS(s5f[:, l, 3, :], s5f[:, l, 3, :], -1.0, None, ALU.mult, None, RR, [BP])
        o = W_OFF["s5cr"][0]
        LOAD(stg[0][:, :], i_wpk[l][:, o:o + 1024], [BSTG[0]], CSTG[0])
        LOAD(stg[1][:, :], i_wpk[l][:, o + 1024:o + 2048], [BSTG[1]], CSTG[1])
        for c in range(8):
            sg = stg[c // 4]
            bs = BSTG[c // 4]
            cre = sg[:, (c % 4) * 256:(c % 4) * 256 + 128]
            cim = sg[:, (c % 4) * 256 + 128:(c % 4) * 256 + 256]
            fre = s5f[:, l, 0, c:c + 1]
            fim = s5f[:, l, 1, c:c + 1]
            a = tmp[5][:, 0:128]
            b = tmp[6][:, 0:128]
            TS(a, cre, fre, None, ALU.mult, None, [bs, BP], [BT[5]])
            TS(b, cim, fim, None, ALU.mult, None, [bs, BP], [BT[6]])
            TT(stgb[0][:, c * 128:c * 128 + 128], a, b, ALU.subtract, [BT[5], BT[6]], [BSTGB[0]])
            TS(a, cre, fim, None, ALU.mult, None, [bs, BP], [BT[5]])
            TS(b, cim, fre, None, ALU.mult, None, [bs, BP], [BT[6]])
            TT(a, a, b, ALU.add, [BT[5], BT[6]], [BT[5]])
            TS(stgb[1][:, c * 128:c * 128 + 128], a, -1.0, None, ALU.mult, None, [BT[5]], [BSTGB[1]])
        dst = wscr[(l, "s5c")].rearrange("p (c x) -> p c x", c=8)
        STORE(dst[:, :, 0:128], stgb[0][:, 0:1024].rearrange("p (c x) -> p c x", c=8), [BSTGB[0]], CSTGB[0], W=[WSB[(l, "s5c")]])
        STORE(dst[:, :, 128:256], stgb[1][:, 0:1024].rearrange("p (c x) -> p c x", c=8), [BSTGB[1]], CSTGB[1], W=[WSB[(l, "s5c")]])

    for l in range(L):
        s5_prep(l)

    def mem_prep_prompt(l):
        mt = big[:, 0:8, 0:256]
        for i in range(2):
            LOAD(stg[i][:, :], i_memT[:, i * 1024:(i + 1) * 1024], [BSTG[i]], CSTG[i])
            VCOPY(mt[:, 4 * i:4 * i + 4, :], stg[i][:, :].rearrange("p (k m) -> p k m", k=4), [BSTG[i]], [BBIG])
        for which, name, od in ((0, "wk", o_mkp), (1, "wv", o_mvp)):
            for hf in range(2):
                i = hf % 2
                LOAD(wring[i][:, 0:4096], wscr[(l, "%s%d" % (name, hf))], [BW[i]], CW[i], R=[WSB[(l, "%s%d" % (name, hf))]])
                w = wring[i][:, 0:4096].rearrange("p (k c) -> p k c", k=8)
                if which == 0:
                    for c4 in range(4):
                        cidx = hf * 4 + c4
                        pb = 2 + (cidx % 2)
                        for kc in range(8):
                            MM(pbank[pb][:, 0:256], w[:, kc, c4 * 128:(c4 + 1) * 128], mt[:, kc, :], kc == 0, kc == 7,
                               [BW[i], BBIG], [PB[pb]])
                        ACOPY(mkT[:, l, cidx, :], pbank[pb][:, 0:256], [PB[pb]], [BMK[l]])
                for m2 in range(2):
                    pb = 4 + m2
                    for kc in range(8):
                        MM(pbank[pb][:, :], mt[:, kc, m2 * 128:(m2 + 1) * 128], w[:, kc, :], kc == 0, kc == 7,
                           [BW[i], BBIG], [PB[pb]])
                    oi = (hf * 2 + m2) % 2
                    ACOPY(ost[oi][:, 0:512], pbank[pb][:, :], [PB[pb]], [BOST[oi]])
                    if which == 1:
                        VCOPY(mvb[:, l, m2, hf * 512:(hf + 1) * 512], ost[oi][:, 0:512], [BOST[oi]], [BMK[l]])
                    STORE(od[l][m2 * 128:(m2 + 1) * 128, hf * 512:(hf + 1) * 512], ost[oi][:, 0:512], [BOST[oi]], COST[oi])

    for l in range(L):
        mem_prep_prompt(l)

    lnstate = dict(cur=None)

    def ln_load(idx):
        LOAD(lnt[:], i_lnp[idx], [BLN], CLN)

    def layer_norm(S, tiles):
        P = S["P"]
        xs_ = {tt: xres[0:P, tt, :] for tt in tiles}
        o = {tt: 16 * (tt % 2) for tt in tiles}
        BS = {tt: BSMT[tt % 2] for tt in tiles}
        st6 = {tt: smt[0:P, o[tt]:o[tt] + 12].rearrange("p (a b) -> p a b", a=2) for tt in tiles}
        for tt in tiles:
            x = xs_[tt]
            add("dve", lambda e, x=x, a=st6[tt]: e.bn_stats(out=a[:, 0, :], in_=x[:, 0:512]), reads=[BXR], writes=[BS[tt]])
            add("dve", lambda e, x=x, a=st6[tt]: e.bn_stats(out=a[:, 1, :], in_=x[:, 512:1024]), reads=[BXR], writes=[BS[tt]])
        for tt in tiles:
            mv = smt[0:P, o[tt] + 12:o[tt] + 14]
            add("dve", lambda e, mv=mv, a=st6[tt]: e.bn_aggr(out=mv, in_=a), reads=[BS[tt]], writes=[BS[tt]])
        for tt in tiles:
            rs = smt[0:P, o[tt] + 14:o[tt] + 15]
            ACT(rs, smt[0:P, o[tt] + 13:o[tt] + 14], AF.Ln, [BS[tt]], [BS[tt]], bias=LN_EPS)
        for tt in tiles:
            rs = smt[0:P, o[tt] + 14:o[tt] + 15]
            ACT(rs, rs, AF.Exp, [BS[tt]], [BS[tt]], scale=-0.5)
        for tt in tiles:
            rs = smt[0:P, o[tt] + 14:o[tt] + 15]
            TS(xs_[tt], xs_[tt], smt[0:P, o[tt] + 12:o[tt] + 13], rs, ALU.subtract, ALU.mult, [BXR, BS[tt]], [BXR])
            TT(xs_[tt], xs_[tt], lnt[0:P, 0:D], ALU.mult, [BXR, BLN], [BXR])
            TT(xs_[tt], xs_[tt], lnt[0:P, D:2 * D], ALU.add, [BXR, BLN], [BXR])
        for tt in tiles:
            ACOPY(xbt[0:P, tt % 2, :], xs_[tt], [BXR], [BXB[tt % 2]])
        for tt in tiles:
            bk = 4 + tt % 2
            pv = pbf(bk)
            for kc in range(8):
                TR(pv[:, kc * 128:kc * 128 + P], xbt[0:P, tt % 2, kc * 128:(kc + 1) * 128], ident[0:P, 0:P], [BXB[tt % 2]], [PB[bk]])
        for tt in tiles:
            bk = 4 + tt % 2
            VCOPY(xT[:, :, tt * 128:tt * 128 + P], pbf(bk)[:, :].rearrange("p (k t) -> p k t", k=8)[:, :, 0:P], [PB[bk]], [BXT])

    def resid_from_psum(S, tt, half, pb):
        P = S["P"]
        x = xres[0:P, tt, half * 512:(half + 1) * 512]
        STT(x, x, ALPHA, pbank[pb][0:P, :], ALU.mult, ALU.add, [BXR, PB[pb]], [BXR])

    def fm_chunk(w, bw, c0, M, pb, Tn):
        for kc in range(8):
            MM(pbank[pb][0:M, 0:Tn], w[:, kc, c0:c0 + M], xT[:, kc, 0:Tn], kc == 0, kc == 7, [bw, BXT], [PB[pb]])

    def logsig_inplace(ap, R, W):
        ACT(ap, ap, AF.Exp, R, W, scale=-1.0)
        ACT(ap, ap, AF.Ln, R, W, bias=1.0)
        TS(ap, ap, -1.0, None, ALU.mult, None, R, W)

    def layer(S, l, seg):
        Tn, P, nt, nchk = S["T"], S["P"], S["nt"], S["nchk"]
        samp = S["samp"]
        sidx = S["sidx"]
        tok0 = seg * Tn

        w, bw = getw(l, "wfm")
        for j in range(2):
            pb = 2 + j
            fm_chunk(w, bw, j * 128, 128, pb, Tn)
            ACOPY(gqk[:, j, 0:Tn], pbank[pb][:, 0:Tn], [PB[pb]], [BGQK])
        for j in range(2):
            pb = 2 + j
            fm_chunk(w, bw, 256 + j * 128, 128, pb, Tn)
            ACOPY(suT[:, j, 0:Tn], pbank[pb][:, 0:Tn], [PB[pb]], [BSU])
        VCOPY(sub[:, :, 0:Tn], suT[:, :, 0:Tn], [BSU], [BSU])
        fm_chunk(w, bw, 512, 16, 2, Tn)
        ACOPY(gaT[:, 0:Tn], pbank[2][0:16, 0:Tn], [PB[2]], [BGA])
        fm_chunk(w, bw, 528, 40, 3, Tn)
        lf = tmp[7][0:40, 0:Tn]
        TS(lf, pbank[3][0:40, 0:Tn], spt[0:40, l, SP["fbf"]:SP["fbf"] + 1], None, ALU.add, None, [PB[3], BP], [BT[7]])
        logsig_inplace(lf, [BT[7]], [BT[7]])
        SCAN(ffT[:, 0:Tn], ones32[0:40, 0:1].to_broadcast([40, Tn]), lf, cfc[:, l:l + 1], [BT[7], BC, BCFC[l]], [BFF])
        VCOPY(cfc[:, l:l + 1], ffT[:, Tn - 1:Tn], [BFF], [BCFC[l]])
        VCOPY(augs[0:8, 0:Tn], ffT[0:8, 0:Tn], [BFF], [BAUG])
        VCOPY(hi2[32:40, 0:Tn], ffT[32:40, 0:Tn], [BFF], [BAUG])
        TT(augs[32:40, 0:Tn], ffT[32:40, 0:Tn], hi2[32:40, 0:Tn], ALU.subtract, [BFF, BAUG], [BAUG])
        wdone()
        w, bw = getw(l, "wfq")
        for h in range(8):
            pb = 2 + (h % 4)
            for kc in range(8):
                MM(pbank[pb][0:68, 0:Tn], w[:, kc, h * 68:(h + 1) * 68], xT[:, kc, 0:Tn], kc == 0, False, [bw, BXT], [PB[pb]])
            MM(pbank[pb][0:68, 0:Tn], selb[:, (8 + h) * 68:(9 + h) * 68], augs[:, 0:Tn], False, True, [BC, BAUG], [PB[pb]])
            ACT(qTa[:, h, 0:Tn], pbank[pb][0:68, 0:Tn], AF.Copy, [PB[pb]], [BQT], scale=0.125)
        wdone()
        w, bw = getw(l, "wfk")
        for h in range(8):
            pb = 2 + (h % 4)
            for kc in range(8):
                MM(pbank[pb][0:68, 0:Tn], w[:, kc, h * 68:(h + 1) * 68], xT[:, kc, 0:Tn], kc == 0, False, [bw, BXT], [PB[pb]])
            MM(pbank[pb][0:68, 0:Tn], selb[:, h * 68:(h + 1) * 68], augs[:, 0:Tn], False, True, [BC, BAUG], [PB[pb]])
            ACOPY(kTa[:, h, 0:Tn], pbank[pb][0:68, 0:Tn], [PB[pb]], [BKT])
        wdone()
        kslot = S["kslot"](seg)
        STORE(ktscr[l][kslot].rearrange("p (h t) -> p h t", h=8)[:, :, 0:Tn], kTa[:, :, 0:Tn], [BKT], CKT, W=[KTB[l][kslot]])
        w, bw = getw(l, "wtv")
        for c in range(nchk):
            pb = 2 + (c % 4)
            for kc in range(8):
                MM(pbank[pb][0:64, :], xT[:, kc, c * 64:(c + 1) * 64], w[:, kc, :], kc == 0, kc == 7, [bw, BXT], [PB[pb]])
            ACOPY(vtb[:, c, :], pbank[pb][0:64, 0:256], [PB[pb]], [BVT])
            VCOPY(gtk[:, c, :], pbank[pb][0:64, 256:512], [PB[pb]], [BVT])
        wdone()
        w, bw = getw(l, "wtk")
        for tt in range(nt):
            pb = 2 + (tt % 2)
            for kc in range(8):
                MM(pbank[pb][0:P, :], xT[:, kc, tt * 128:tt * 128 + P], w[:, kc, :], kc == 0, kc == 7, [bw, BXT], [PB[pb]])
            oi = tt % 2
            ACOPY(ost[oi][0:P, 0:512], pbank[pb][0:P, :], [PB[pb]], [BOST[oi]])
            STORE(S["o_fk"](l)[tok0 + tt * 128:tok0 + tt * 128 + P, :], ost[oi][0:P, 0:512], [BOST[oi]], COST[oi])
        wdone()
        w, bw = getw(l, "wtf")
        for tt in range(nt):
            for (c0, cn, pb) in ((0, 512, 2), (512, 8, 3)):
                for kc in range(8):
                    MM(pbank[pb][0:P, 0:cn], xT[:, kc, tt * 128:tt * 128 + P], w[:, kc, c0:c0 + cn], kc == 0, kc == 7,
                       [bw, BXT], [PB[pb]])
            oi = tt % 2
            ACOPY(ost[oi][0:P, 0:512], pbank[2][0:P, :], [PB[2]], [BOST[oi]])
            TT(ost[oi][0:P, 512:520], pbank[3][0:P, 0:8], tpt[0:P, l, TP["fbf"]:TP["fbf"] + 8], ALU.add, [PB[3], BP], [BOST[oi]])
            logsig_inplace(ost[oi][0:P, 512:520], [BOST[oi]], [BOST[oi]])
            v4 = ost[oi][0:P, 0:512].rearrange("p (a b d) -> p a b d", a=4, b=2)
            vo = vown[0:P, tt].rearrange("p (a b) d -> p a b d", a=4)
            VCOPY(vo[:, :, 0, 0:64], v4[:, :, 0, :], [BOST[oi]], [BVO])
            VCOPY(vo[:, :, 1, 64:128], v4[:, :, 1, :], [BOST[oi]], [BVO])
            STORE(S["o_fv"](l)[tok0 + tt * 128:tok0 + tt * 128 + P, :], ost[oi][0:P, 0:512], [BOST[oi]], COST[oi])
            STORE(S["o_fl"](l)[tok0 + tt * 128:tok0 + tt * 128 + P, :], ost[oi][0:P, 512:520], [BOST[oi]], COST[oi])
        wdone()
        STORE(vscr[l][kslot].rearrange("p (t x) -> p t x", t=NT)[0:P, 0:nt, :],
              vown[0:P, 0:nt].rearrange("p t h d -> p t (h d)"), [BVO], CVO, W=[VSB[l][kslot]])

        q = gqk[:, 0, 0:Tn]
        k = gqk[:, 1, 0:Tn]
        cum = tmp[0][:, 0:Tn]
        la = tmp[7][:, 0:Tn]
        MM(pbank[2][:, 0:Tn], wa2b[:, l, :], gaT[:, 0:Tn], True, True, [BP, BGA], [PB[2]])
        TS(la, pbank[2][:, 0:Tn], spt[:, l, SP["gba"]:SP["gba"] + 1], None, ALU.add, None, [PB[2], BP], [BT[7]])
        logsig_inplace(la, [BT[7]], [BT[7]])
        TS(la, la, 1.0 / 16.0, None, ALU.mult, None, [BT[7]], [BT[7]])
        SCAN(cum, chunkmask[:, 0:Tn], la, 0.0, [BT[7], BC], [BT[0]])
        cend = tmp[1][:, 0:nchk]
        VCOPY(cend, cum.rearrange("p (c j) -> p c j", j=64)[:, :, 63], [BT[0]], [BT[1]])
        dec = tmp[1][:, 8:8 + nchk]
        ACT(dec, cend, AF.Exp, [BT[1]], [BT[1]])
        e1 = tmp[2][:, 0:Tn]
        ACT(e1, cum, AF.Exp, [BT[0]], [BT[2]])
        qd = tmpb[0][:, 0, 0:Tn]
        STT(qd, q, 32.0 ** -0.5, e1, ALU.mult, ALU.mult, [BGQK, BT[2]], [BTB[0]])
        ACT(e1, cum, AF.Exp, [BT[0]], [BT[2]], scale=-1.0)
        kinv = tmp[3][:, 0:Tn]
        TT(kinv, k, e1, ALU.mult, [BGQK, BT[2]], [BT[3]])
        kih = tmpb[1]
        for h in range(4):
            TS(kih[:, h, 0:Tn], kinv, spt[:, l, SP["hm"] + h:SP["hm"] + h + 1], None, ALU.mult, None, [BT[3], BP], [BTB[1]])
        kend = tmpb[0][:, 1, 0:Tn]
        for c in range(nchk):
            ACT(e1[:, c * 64:(c + 1) * 64], cum[:, c * 64:(c + 1) * 64], AF.Exp, [BT[0], BT[1]], [BT[2]],
                bias=cend[:, c:c + 1], scale=-1.0)
        TT(kend, k, e1, ALU.mult, [BGQK, BT[2]], [BTB[0]])
        Sf = gS[:, l, :]
        Sb = gSb[:, l, :]
        for c in range(nchk):
            cs = slice(c * 64, (c + 1) * 64)
            for h in range(4):
                MM(pbank[2][0:64, h * 64:(h + 1) * 64], kih[:, h, cs], qd[:, cs], True, True, [BTB[0], BTB[1]], [PB[2]])
            scm = tmpb[0][0:64, 2, 0:256]
            TT(scm, pbank[2][0:64, 0:256], cmask4, ALU.mult, [PB[2], BC], [BTB[0]])
            MM(pbank[3][0:64, 0:256], qd[:, cs], Sb, True, False, [BTB[0], BGS[l]], [PB[3]])
            for h in range(4):
                MM(pbank[3][0:64, h * 64:(h + 1) * 64], scm[:, h * 64:(h + 1) * 64], vtb[:, c, h * 64:(h + 1) * 64], False, h == 3,
                   [BTB[0], BVT], [PB[3]])
            TR(pbf(4)[0:64, 0:128], kend[:, cs], ident, [BTB[0]], [PB[4]])
            keT = tmpb[0][0:64, 3, 0:128]
            VCOPY(keT, pbf(4)[0:64, 0:128], [PB[4]], [BTB[0]])
            MM(pbank[5][:, 0:256], keT, vtb[:, c, :], True, True, [BTB[0], BVT], [PB[5]])
            dsm = tmp[4][:, 0:256]
            TT(dsm, pbank[5][:, 0:256], blockmask, ALU.mult, [PB[5], BC], [BT[4]])
            STT(Sf, Sf, dec[:, c:c + 1], dsm, ALU.mult, ALU.add, [BGS[l], BT[1], BT[4]], [BGS[l]])
            ACOPY(Sb, Sf, [BGS[l]], [BGS[l]])
            osq = tmp[5][0:64, 0:256]
            ACT(osq, pbank[3][0:64, 0:256], AF.Square, [PB[3]], [BT[5]])
            ss = sm[0:64, 16:20]
            add("dve", lambda e, ss=ss, osq=osq: e.reduce_sum(out=ss, in_=osq.rearrange("p (h d) -> p h d", h=4), axis=AX.X),
                reads=[BT[5]], writes=[BSM])
            TS(ss, ss, 1.0 / 64.0, LN_EPS, ALU.mult, ALU.add, [BSM], [BSM])
            ACT(ss, ss, AF.Ln, [BSM], [BSM])
            ACT(ss, ss, AF.Exp, [BSM], [BSM], scale=-0.5)
            on = tmp[5][0:64, 0:256]
            TT(on.rearrange("p (h d) -> p h d", h=4), pbank[3][0:64, 0:256].rearrange("p (h d) -> p h d", h=4),
               ss.unsqueeze(2).to_broadcast([64, 4, 64]), ALU.mult, [PB[3], BSM], [BT[5]])
            TT(on, on, tpt[0:64, l, TP["gng"]:TP["gng"] + 256], ALU.mult, [BT[5], BP], [BT[5]])
            sg = tmp[6][0:64, 0:256]
            ACT(sg, gtk[:, c, :], AF.Exp, [BVT], [BT[6]], scale=-1.0)
            ACT(sg, sg, AF.Ln, [BT[6]], [BT[6]], bias=1.0)
            ACT(sg, sg, AF.Exp, [BT[6]], [BT[6]], scale=-1.0)
            TT(sg, sg, gtk[:, c, :], ALU.mult, [BT[6], BVT], [BT[6]])
            og = tmpb[0][0:64, 2, 0:256]
            TT(og, on, sg, ALU.mult, [BT[5], BT[6]], [BTB[0]])
            for j in range(2):
                TR(pbf(4)[:, 256 + j * 64:256 + (j + 1) * 64], og[:, j * 128:(j + 1) * 128], ident[0:64, 0:64], [BTB[0]], [PB[4]])
            VCOPY(mixT[:, 0:2, cs], pbf(4)[:, 256:384].rearrange("p (j t) -> p j t", j=2), [PB[4]], [BMIX])

        w, bw = getw(l, "s5b")
        xre = big[:, 0:8, 0:Tn]
        xim = big[:, 8:16, 0:Tn]
        cosT = s5tab[:, l, 0]
        sinT = s5tab[:, l, 1]
        nsb = Tn // SB
        v3 = lambda ap: ap.rearrange("p (s j) -> p s j", j=SB)
        btre = s5bt[:, 0, :].rearrange("p (m t) -> p m t", m=8)
        btim = s5bt[:, 1, :].rearrange("p (m t) -> p m t", m=8)
        t0_, t1_, t2_, t3_ = (tmp[i][:, 0:Tn] for i in range(4))
        for m in range(8):
            kc = m // 4
            pa, pi_ = (2, 3) if m % 2 == 0 else (4, 5)
            MM(pbank[pa][:, 0:Tn], w[:, m, 0:128], sub[:, kc, 0:Tn], True, True, [bw, BSU], [PB[pa]])
            MM(pbank[pi_][:, 0:Tn], w[:, m, 128:256], sub[:, kc, 0:Tn], True, True, [bw, BSU], [PB[pi_]])
            cb = cosT[:, m, :].unsqueeze(1).to_broadcast([128, nsb, SB])
            sbb = sinT[:, m, :].unsqueeze(1).to_broadcast([128, nsb, SB])
            pre = pbank[pa][:, 0:Tn]
            pim = pbank[pi_][:, 0:Tn]
            TT(v3(t0_), v3(pre), cb, ALU.mult, [PB[pa], BP], [BT[0]])
            TT(v3(t1_), v3(pim), sbb, ALU.mult, [PB[pi_], BP], [BT[1]])
            TT(v3(t2_), v3(pim), cb, ALU.mult, [PB[pi_], BP], [BT[2]])
            TT(v3(t3_), v3(pre), sbb, ALU.mult, [PB[pa], BP], [BT[3]])
            TT(btre[:, m, 0:Tn], t0_, t1_, ALU.add, [BT[0], BT[1]], [BS5H[0]])
            TT(btim[:, m, 0:Tn], t2_, t3_, ALU.subtract, [BT[2], BT[3]], [BS5H[1]])
        wdone()
        for b in range(4):
            MSET(pbank[b][:, :], 0.0, [PB[b]], eng="dve")
        ksegs = S["ksegs"](seg)

        def kv_load(si):
            slot, nk, diag = ksegs[si]
            r = si % 2
            LOAD(ktr[r][:, :, 0:nk], ktscr[l][slot].rearrange("p (h t) -> p h t", h=8)[:, :, 0:nk], [BKR[r]], CKR[r], R=[KTB[l][slot]])
            nkb = (nk + 127) // 128
            kp = min(128, nk)
            LOAD(vtr[r][0:kp, 0:nkb].rearrange("p t h d -> p t (h d)"), vscr[l][slot].rearrange("p (t x) -> p t x", t=NT)[0:kp, 0:nkb, :],
                 [BVR[r]], CVR[r], R=[VSB[l][slot]])

        items = []
        firsts = {}
        for si, (slot, nk, diag) in enumerate(ksegs):
            nkb = (nk + 127) // 128
            firsts[len(items)] = si
            for h in range(8):
                if diag or nkb * Tn > 512:
                    for kb in range(nkb):
                        items.append((si, h, [kb]))
                else:
                    items.append((si, h, list(range(nkb))))
        kv_load(0)
        if len(ksegs) > 1:
            kv_load(1)
        DPIPE = 3

        def att_s12(it, n):
            si, h, kbs = it
            slot, nk, diag = ksegs[si]
            r = si % 2
            pb = 4 + n % 4
            pt = ptb[n % 4]
            bpt = BPT[n % 4]
            for jj, kb in enumerate(kbs):
                kn = min(128, nk - kb * 128)
                q0 = kb * 128 if diag else 0
                MM(pbank[pb][0:kn, jj * Tn + q0:jj * Tn + Tn], ktr[r][:, h, kb * 128:kb * 128 + kn], qTa[:, h, q0:Tn], True, True,
                   [BKR[r], BQT], [PB[pb]])
            if diag:
                ACT(pt[0:kn, q0:Tn], pbank[pb][0:kn, q0:Tn], AF.Exp, [PB[pb]], [bpt])
                qe = min(Tn, q0 + 128)
                TT(pt[0:kn, q0:qe], pt[0:kn, q0:qe], trib[0:kn, 0:qe - q0], ALU.mult, [bpt, BC], [bpt])
            else:
                ACT(pt[0:kn, 0:len(kbs) * Tn], pbank[pb][0:kn, 0:len(kbs) * Tn], AF.Exp, [PB[pb]], [bpt])

        def att_s3(it, n):
            si, h, kbs = it
            slot, nk, diag = ksegs[si]
            nkb = (nk + 127) // 128
            r = si % 2
            pt = ptb[n % 4]
            bpt = BPT[n % 4]
            ab = h // 2
            acc = pbank[ab][:, (h % 2) * 256:(h % 2) * 256 + Tn]
            for jj, kb in enumerate(kbs):
                kn = min(128, nk - kb * 128)
                q0 = kb * 128 if diag else 0
                last = (si == len(ksegs) - 1) and (kb == nkb - 1)
                MM(acc[:, q0:Tn], vtr[r][0:kn, kb, h, :], pt[0:kn, jj * Tn + q0:jj * Tn + Tn], False, last, [BVR[r], bpt], [PB[ab]], skip=True)

        cS = s5e[:, l, 0, :]
        sS = s5e[:, l, 1, :]
        for sbi in range(nsb):
            js = slice(sbi * SB, (sbi + 1) * SB)
            if sbi == 0:
                wl_re, wl_im, Rw = s5x[:, l, 0, :], s5x[:, l, 1, :], [BS5X[l]]
            else:
                wl_re, wl_im, Rw = btre[:, :, sbi * SB - 1], btim[:, :, sbi * SB - 1], [BS5H[0], BS5H[1]]
            ini = sm[:, 32:48].rearrange("p (a c) -> p a c", a=2)
            ta = sm[:, 48:56]
            tb = sm[:, 56:64]
            TT(ini[:, 0, :], cS, wl_re, ALU.mult, [BP] + Rw, [BSM])
            TT(ta, sS, wl_im, ALU.mult, [BP] + Rw, [BSM])
            TT(ini[:, 0, :], ini[:, 0, :], ta, ALU.subtract, [BSM], [BSM])
            TT(ini[:, 1, :], cS, wl_im, ALU.mult, [BP] + Rw, [BSM])
            TT(tb, sS, wl_re, ALU.mult, [BP] + Rw, [BSM])
            TT(ini[:, 1, :], ini[:, 1, :], tb, ALU.add, [BSM], [BSM])
            for m in range(8):
                rbc = s5r[:, l, m:m + 1].to_broadcast([128, SB])
                SCAN(btre[:, m, js], rbc, btre[:, m, js], ini[:, 0, m:m + 1], [BP, BS5H[0], BSM], [BS5H[0]])
                SCAN(btim[:, m, js], rbc, btim[:, m, js], ini[:, 1, m:m + 1], [BP, BS5H[1], BSM], [BS5H[1]])
        VCOPY(s5x[:, l, 0, :], btre[:, :, Tn - 1], [BS5H[0]], [BS5X[l]])
        VCOPY(s5x[:, l, 1, :], btim[:, :, Tn - 1], [BS5H[1]], [BS5X[l]])
        for m in range(8):
            cb = cosT[:, m, :].unsqueeze(1).to_broadcast([128, nsb, SB])
            sbb = sinT[:, m, :].unsqueeze(1).to_broadcast([128, nsb, SB])
            wre = btre[:, m, 0:Tn]
            wim = btim[:, m, 0:Tn]
            TT(v3(t0_), v3(wre), cb, ALU.mult, [BS5H[0], BP], [BT[0]])
            TT(v3(t1_), v3(wim), sbb, ALU.mult, [BS5H[1], BP], [BT[1]])
            TT(v3(t2_), v3(wim), cb, ALU.mult, [BS5H[1], BP], [BT[2]])
            TT(v3(t3_), v3(wre), sbb, ALU.mult, [BS5H[0], BP], [BT[3]])
            TT(xre[:, m, :], t0_, t1_, ALU.subtract, [BT[0], BT[1]], [BBIG])
            TT(xim[:, m, :], t2_, t3_, ALU.add, [BT[2], BT[3]], [BBIG])
        for i in range(len(items) + DPIPE):
            if i < len(items):
                att_s12(items[i], i)
            k_ = i - DPIPE
            if k_ >= 0:
                att_s3(items[k_], k_)
                if k_ in firsts and firsts[k_] >= 1 and firsts[k_] + 1 < len(ksegs):
                    kv_load(firsts[k_] + 1)
        for g4 in range(2):
            hs = list(range(4 * g4, 4 * g4 + 4))
            info = {}
            for hh, h in enumerate(hs):
                ab = h // 2
                acc = pbank[ab][:, (h % 2) * 256:(h % 2) * 256 + Tn]
                lo, hi_ = (0, 64) if h % 2 == 0 else (64, 128)
                l0 = 64 if h % 2 == 0 else 0
                info[h] = (ab, acc, lo, hi_, l0)
                ACT(tmp[7][l0:l0 + 1, 0:Tn], acc[l0:l0 + 1, :], AF.Ln, [PB[ab]], [BT[7]])
                ACT(rlb[l0:l0 + 1, hh, 0:Tn], tmp[7][l0:l0 + 1, 0:Tn], AF.Exp, [BT[7]], [BRL], scale=-1.0)
            for hh, h in enumerate(hs):
                MM(pbank[4 + hh][:, 0:Tn], pdown if h % 2 == 0 else pup, rlb[:, hh, 0:Tn], True, True, [BC, BRL], [PB[4 + hh]])
            for hh, h in enumerate(hs):
                ab, acc, lo, hi_, l0 = info[h]
                ACOPY(tmp[hh][lo:hi_, 0:Tn], pbank[4 + hh][lo:hi_, 0:Tn], [PB[4 + hh]], [BT[hh]])
            for hh, h in enumerate(hs):
                ab, acc, lo, hi_, l0 = info[h]
                TT(mixT[lo:hi_, 4 + h // 2, 0:Tn], acc[lo:hi_, :], tmp[hh][lo:hi_, 0:Tn], ALU.mult, [PB[ab], BT[hh]], [BMIX])

        w, bw = getw(l, "s5c")
        wg, bwg = getw(l, "s5g")
        zT = tmpb[1][:, 0:2, 0:Tn]
        zf = [tmp[0][:, 0:Tn], tmp[1][:, 0:Tn]]
        for oc in range(2):
            pb = 2 + oc
            n = 0
            for c in range(4 * oc, 4 * oc + 4):
                MM(pbank[pb][:, 0:Tn], w[:, c, 0:128], xre[:, c, :], n == 0, False, [bw, BBIG], [PB[pb]])
                MM(pbank[pb][:, 0:Tn], w[:, c, 128:256], xim[:, c, :], False, n == 3, [bw, BBIG], [PB[pb]])
                n += 1
            y = tmp[2][:, 0:Tn]
            STT(y, suT[:, oc, 0:Tn], spt[:, l, SP["s5d"] + oc:SP["s5d"] + oc + 1], pbank[pb][:, 0:Tn], ALU.mult, ALU.add,
                [BSU, BP, PB[pb]], [BT[2]])
            gelu(y, zf[oc], [BT[2]], [BT[oc]], 3, Tn)
            ACOPY(zT[:, oc, :], zf[oc], [BT[oc]], [BTB[1]])
        for oc in range(2):
            pb = 2 + oc
            for kc in range(2):
                MM(pbank[pb][:, 0:Tn], wg[:, kc, oc * 128:(oc + 1) * 128], zT[:, kc, :], kc == 0, kc == 1, [bwg, BTB[1]], [PB[pb]])
            gt = tmp[2][:, 0:Tn]
            ACT(gt, pbank[pb][:, 0:Tn], AF.Exp, [PB[pb], BP], [BT[2]], bias=spt[:, l, SP["hre"] + oc:SP["hre"] + oc + 1], scale=-1.0)
            ACT(gt, gt, AF.Ln, [BT[2]], [BT[2]], bias=1.0)
            ACT(gt, gt, AF.Exp, [BT[2]], [BT[2]], scale=-1.0)
            TT(mixT[:, 2 + oc, 0:Tn], zf[oc], gt, ALU.mult, [BT[oc], BT[2]], [BMIX])
        wdone()

        ln_load(1 + 3 * l)
        for hf in range(2):
            w, bw = getw(l, "wmx%d" % hf)
            for tt in range(nt):
                pb = 2 + tt % 2
                for kc in range(8):
                    MM(pbank[pb][0:P, :], mixT[:, kc, tt * 128:tt * 128 + P], w[:, kc, :], kc == 0, kc == 7, [bw, BMIX], [PB[pb]])
                resid_from_psum(S, tt, hf, pb)
            wdone()
        layer_norm(S, list(range(nt)))

        mk = S["mkT"](l)
        mv_ = S["mv"](l)
        bmk = S["bmk"](l)
        for hf in range(2):
            w, bw = getw(l, "wq%d" % hf)
            for c4 in range(4):
                pb = 2 + c4 % 2
                fm_chunk(w, bw, c4 * 128, 128, pb, Tn)
                ACOPY(qmT[:, hf * 4 + c4, 0:Tn], pbank[pb][:, 0:Tn], [PB[pb]], [BQM])
            wdone()
        def mem_a(h):
            pt = ptb[h]
            for mc in range(2):
                pb = 2 + (h % 2) * 2 + mc
                for dc in range(2):
                    MM(pbank[pb][:, 0:Tn], mk[:, 2 * h + dc, mc * 128:(mc + 1) * 128], qmT[:, 2 * h + dc, 0:Tn], dc == 0, dc == 1,
                       [bmk, BQM], [PB[pb]])
                ACT(pt[:, mc * 256:mc * 256 + Tn], pbank[pb][:, 0:Tn], AF.Exp, [PB[pb]], [BPT[h]], scale=1.0 / 16.0)

        def mem_b(h):
            pt = ptb[h]
            for mc in range(2):
                MM(pbank[6][0:1, 0:Tn], onesb[:, 0:1], pt[:, mc * 256:mc * 256 + Tn], mc == 0, mc == 1, [BC, BPT[h]], [PB[6]])
            ACT(tmp[7][0:1, 0:Tn], pbank[6][0:1, 0:Tn], AF.Ln, [PB[6]], [BT[7]])
            ACT(rl1[0:1, 0:Tn], tmp[7][0:1, 0:Tn], AF.Exp, [BT[7]], [BRL1], scale=-1.0)
            MM(pbank[6][:, 256:256 + Tn], onesb[0:1, :], rl1[0:1, 0:Tn], True, True, [BC, BRL1], [PB[6]])
            rl = tmp[h % 2][:, 0:Tn]
            ACOPY(rl, pbank[6][:, 256:256 + Tn], [PB[6]], [BT[h % 2]])
            for dc in range(2):
                pb = (7, 0)[dc]
                for mc in range(2):
                    MM(pbank[pb][:, 0:Tn], mv_[:, mc, (2 * h + dc) * 128:(2 * h + dc + 1) * 128], pt[:, mc * 256:mc * 256 + Tn],
                       mc == 0, mc == 1, [bmk, BPT[h]], [PB[pb]])
                TT(mixT[:, 2 * h + dc, 0:Tn], pbank[pb][:, 0:Tn], rl, ALU.mult, [PB[pb], BT[h % 2]], [BMIX])

        mem_a(0)
        for h in range(1, 4):
            mem_a(h)
            mem_b(h - 1)
        mem_b(3)
        ln_load(2 + 3 * l)
        for hf in range(2):
            w, bw = getw(l, "wo%d" % hf)
            for tt in range(nt):
                pb = 2 + tt % 2
                for kc in range(8):
                    MM(pbank[pb][0:P, :], mixT[:, kc, tt * 128:tt * 128 + P], w[:, kc, :], kc == 0, kc == 7, [bw, BMIX], [PB[pb]])
                resid_from_psum(S, tt, hf, pb)
            wdone()
        layer_norm(S, list(range(nt)))

        hl = halo[:, l]
        if not samp:
            VCOPY(xTp[:, :, 0:2], xhc[:, l], [BXH[l]], [BXT])
        for j in range(11):
            w, bw = getw(l, "wup%d" % j)
            bk = [(j % 2) * 4 + r4 for r4 in range(4)]
            ys = [tmp[(j % 2) * 4 + r4][:, 0:Tn] for r4 in range(4)]
            bys = [BT[(j % 2) * 4 + r4] for r4 in range(4)]
            cwp = lambda tap, ci: spt[:, l, SP["cw%d" % tap] + ci:SP["cw%d" % tap] + ci + 1]
            c0 = 0 if samp else 2
            for r4 in range(4):
                pb = bk[r4]
                for kc in range(8):
                    if samp:
                        MM(pbank[pb][:, 0:Tn], w[:, kc, r4 * 128:(r4 + 1) * 128], xT[:, kc, 0:Tn], kc == 0, kc == 7, [bw, BXT], [PB[pb]])
                    else:
                        MM(pbank[pb][:, 0:Tn + 2], w[:, kc, r4 * 128:(r4 + 1) * 128], xTp[:, kc, 0:Tn + 2], kc == 0, kc == 7,
                           [bw, BXT], [PB[pb]])
            for r4 in range(4):
                ci = 4 * j + r4
                ACT(ys[r4], pbank[bk[r4]][:, c0:c0 + Tn], AF.Identity, [PB[bk[r4]], BP], [bys[r4]],
                    bias=spt[:, l, SP["cb"] + ci:SP["cb"] + ci + 1], scale=cwp(2, ci))
            for r4 in range(4):
                ci = 4 * j + r4
                pb = bk[r4]
                ps = pbank[pb]
                y = ys[r4]
                by = bys[r4]
                bh = BHALO[l][ci]
                if samp:
                    STT(y[:, 1:Tn], ps[:, 0:Tn - 1], cwp(1, ci), y[:, 1:Tn], ALU.mult, ALU.add, [PB[pb], BP, by], [by])
                    STT(y[:, 0:1], hl[:, ci, 1:2], cwp(1, ci), y[:, 0:1], ALU.mult, ALU.add, [bh, BP, by], [by])
                    STT(y[:, 2:Tn], ps[:, 0:Tn - 2], cwp(0, ci), y[:, 2:Tn], ALU.mult, ALU.add, [PB[pb], BP, by], [by])
                    STT(y[:, 0:2], hl[:, ci, 0:2], cwp(0, ci), y[:, 0:2], ALU.mult, ALU.add, [bh, BP, by], [by])
                    ACOPY(hl[:, ci, :], ps[:, Tn - 2:Tn], [PB[pb]], [bh])
                else:
                    STT(y, ps[:, 1:Tn + 1], cwp(1, ci), y, ALU.mult, ALU.add, [PB[pb], BP, by], [by])
                    STT(y, ps[:, 0:Tn], cwp(0, ci), y, ALU.mult, ALU.add, [PB[pb], BP, by], [by])
                    if seg == NSEG - 1:
                        ACOPY(hl[:, ci, :], ps[:, Tn:Tn + 2], [PB[pb]], [bh])
                if r4 < 2:
                    ACT(y, y, AF.Gelu_apprx_tanh, [by], [by])
            for r4 in range(2):
                TT(big[:, 2 * j + r4, 0:Tn], ys[r4], ys[2 + r4], ALU.mult, [bys[r4], bys[2 + r4]], [BBIG])
            wdone()
        if not samp:
            VCOPY(xhc[:, l], xTp[:, :, Tn:Tn + 2], [BXT], [BXH[l]])
        ln_load(3 + 3 * l)
        for hf in range(2):
            for g in range(3):
                w, bw = getw(l, "wdn%d%d" % (hf, g))
                nk = (8, 8, 6)[g]
                for tt in range(nt):
                    pb = 2 + tt
                    for kk in range(nk):
                        kc = g * 8 + kk
                        MM(pbank[pb][0:P, :], big[:, kc, tt * 128:tt * 128 + P], w[:, kk, :], kc == 0, kc == 21, [bw, BBIG], [PB[pb]])
                wdone()
            for tt in range(nt):
                resid_from_psum(S, tt, hf, 2 + tt)
        layer_norm(S, list(range(nt)))

    def gelu(y, out, R, W, ti, Tn):
        t = tmp[ti][:, 0:Tn]
        bt = BT[ti]
        ACT(t, y, AF.Square, R, [bt])
        TS(t, t, 0.044715, 1.0, ALU.mult, ALU.add, [bt], [bt])
        TT(t, t, y, ALU.mult, [bt] + R, [bt])
        ACT(t, t, AF.Sigmoid, [bt], [bt], scale=1.5957691216057308)
        TT(out, y, t, ALU.mult, R + [bt], W)

    def cmul(o_re, o_im, a_re, a_im, b_re, b_im, conj_a, R, W):
        ta = sm[:, 48:56]
        tb = sm[:, 56:64]
        TT(ta, a_re, b_re, ALU.mult, R, [BSM])
        TT(tb, a_im, b_im, ALU.mult, R, [BSM])
        TT(o_re, ta, tb, ALU.add if conj_a else ALU.subtract, [BSM], W)
        TT(ta, a_re, b_im, ALU.mult, R, [BSM])
        TT(tb, a_im, b_re, ALU.mult, R, [BSM])
        TT(o_im, ta, tb, ALU.subtract if conj_a else ALU.add, [BSM], W)

    def s5_state_out(l, dst):
        xq = sm[:, 0:16].rearrange("p (a c) -> p a c", a=2)
        so = tmp[0][:, 0:16].rearrange("p (a c) -> p a c", a=2)
        cmul(xq[:, 0, :], xq[:, 1, :], s5tab[:, l, 0, :, SB - 1], s5tab[:, l, 1, :, SB - 1], s5x[:, l, 0, :], s5x[:, l, 1, :], False,
             [BP, BS5X[l], BSM], [BSM])
        cmul(so[:, 0, :], so[:, 1, :], s5f[:, l, 0, :], s5f[:, l, 1, :], xq[:, 0, :], xq[:, 1, :], False, [BP, BSM], [BT[0]])
        STORE(dst.rearrange("a p c -> p a c"), so, [BT[0]], CS5X)

    def s5_state_in(l, src):
        h0 = tmp[0][:, 0:16].rearrange("p (a c) -> p a c", a=2)
        LOAD(h0, src.rearrange("a p c -> p a c"), [BT[0]], CS5XL)
        xq = sm[:, 0:16].rearrange("p (a c) -> p a c", a=2)
        cmul(xq[:, 0, :], xq[:, 1, :], s5f[:, l, 2, :], s5f[:, l, 3, :], h0[:, 0, :], h0[:, 1, :], False, [BP, BT[0], BSM], [BSM])
        cmul(s5x[:, l, 0, :], s5x[:, l, 1, :], s5tab[:, l, 0, :, SB - 1], s5tab[:, l, 1, :, SB - 1], xq[:, 0, :], xq[:, 1, :], True,
             [BP, BSM], [BS5X[l]])

    plan_stream(NSEG + 2)

    SP_ = dict(T=T, P=128, nt=NT, nchk=NCHK, samp=False, sidx=0,
               kslot=lambda seg: seg,
               ksegs=lambda seg: [(s, T, False) for s in range(seg)] + [(seg, T, True)],
               o_fk=lambda l: o_fkp[l], o_fv=lambda l: o_fvp[l], o_fl=lambda l: o_flp[l],
               mkT=lambda l: mkT[:, l], mv=lambda l: mvb[:, l], bmk=lambda l: BMK[l])
    for l in range(L):
        MSET(gS[:, l, :], 0.0, [BGS[l]])
        MSET(gSb[:, l, :], 0.0, [BGS[l]])
        MSET(s5x[:, l], 0.0, [BS5X[l]])
        MSET(halo[:, l], 0.0, BHALO[l])
        MSET(cfc[:, l:l + 1], 0.0, [BCFC[l]])
        MSET(xhc[:, l], 0.0, [BXH[l]])
    for seg in range(NSEG):
        ln_load(0)
        for tt in range(NT):
            LOAD(xres[:, tt, :], i_xp[seg * T + tt * 128:seg * T + (tt + 1) * 128, :], [BXR], CXIN)
        layer_norm(SP_, list(range(NT)))
        for l in range(L):
            layer(SP_, l, seg)
        for tt in range(NT):
            STORE(o_yp[seg * T + tt * 128:seg * T + (tt + 1) * 128, :], xres[:, tt, :], [BXR], COUT)
    for l in range(L):
        for h in range(4):
            STORE(o_glap[l][32 * h:32 * h + 32, :], gS[32 * h:32 * h + 32, l, 64 * h:64 * h + 64], [BGS[l]], CGS[l])
        s5_state_out(l, o_s5p[l])
        STORE(o_cvp[l], halo[:, l].rearrange("p c j -> p (c j)"), BHALO[l], CHALO[l])

    smk = mkT
    for s in range(2):
        SS = dict(T=64, P=64, nt=1, nchk=1, samp=True, sidx=s,
                  kslot=lambda seg: NPS,
                  ksegs=lambda seg: [(ps, T, False) for ps in range(NPS)] + [(NPS, 64, True)],
                  o_fk=lambda l, s=s: o_fks[l][s], o_fv=lambda l, s=s: o_fvs[l][s], o_fl=lambda l, s=s: o_fls[l][s],
                  mkT=lambda l: mkT[:, l], mv=lambda l: mvb[:, l], bmk=lambda l: BMK[l])
        for l in range(L):
            MSET(gS[:, l, :], 0.0, [BGS[l]])
            for h in range(4):
                LOAD(gS[32 * h:32 * h + 32, l, 64 * h:64 * h + 64], i_stgla[l][s][32 * h:32 * h + 32, :], [BGS[l]], CGSL[l])
            ACOPY(gSb[:, l, :], gS[:, l, :], [BGS[l]], [BGS[l]])
            s5_state_in(l, i_sts5[l][s])
            LOAD(halo[:, l].rearrange("p c j -> p (c j)"), i_cvst[l][s], BHALO[l], CHALOL[l])
            for i in range(2):
                LOAD(stg[i][:, :], i_cmkT[l][s][:, i * 1024:(i + 1) * 1024], [BSTG[i]], CSTG[i])
                VCOPY(mkT[:, l, 4 * i:4 * i + 4, :], stg[i][:, :].rearrange("p (k m) -> p k m", k=4), [BSTG[i]], [BMK[l]])
            for i in range(2):
                LOAD(stg[i][:, :], i_cmv[l][s][:, i * 1024:(i + 1) * 1024], [BSTG[i]], CSTG[i])
                VCOPY(mvb[:, l, i, :], stg[i][:, :], [BSTG[i]], [BMK[l]])
            cfp = tmp[6][0:40, 0:T]
            BXIN = BT[6]
            MSET(cfc[:, l:l + 1], 0.0, [BCFC[l]])
            for ps in range(NPS):
                MSET(tmp[6][0:40, 0:T], 0.0, [BXIN])
                LOAD(tmp[6][0:8, 0:T], i_cflf[l][s][:, ps * T:(ps + 1) * T], [BXIN], CXIN2)
                LOAD(tmp[6][32:40, 0:T], i_cflf[l][s][:, ps * T:(ps + 1) * T], [BXIN], CXIN2)
                cfo = tmp[7][0:40, 0:T]
                SCAN(cfo, ones32[0:40, 0:1].to_broadcast([40, T]), cfp, cfc[:, l:l + 1], [BXIN, BC, BCFC[l]], [BT[7]])
                VCOPY(cfc[:, l:l + 1], cfo[:, T - 1:T], [BT[7]], [BCFC[l]])
                VCOPY(augs[0:8, 0:T], cfo[0:8, :], [BT[7]], [BAUG])
                VCOPY(hi2[32:40, 0:T], cfo[32:40, :], [BT[7]], [BAUG])
                TT(augs[32:40, 0:T], cfo[32:40, :], hi2[32:40, 0:T], ALU.subtract, [BT[7], BAUG], [BAUG])
                kc_src = i_cfk[l][s].rearrange("p (h t) -> p h t", h=8)
                for h in range(8):
                    i = h % 2
                    LOAD(stg[i][0:64, 0:T], kc_src[:, h, ps * T:(ps + 1) * T], [BSTG[i]], CSTG[i])
                    VCOPY(stgb[i][0:64, 0:T], stg[i][0:64, 0:T], [BSTG[i]], [BSTGB[i]])
                    pb = 2 + i
                    MM(pbank[pb][0:68, 0:T], i64pad, stgb[i][0:64, 0:T], True, False, [BC, BSTGB[i]], [PB[pb]])
                    MM(pbank[pb][0:68, 0:T], selb[:, h * 68:(h + 1) * 68], augs[:, 0:T], False, True, [BC, BAUG], [PB[pb]])
                    ACOPY(kTa[:, h, 0:T], pbank[pb][0:68, 0:T], [PB[pb]], [BKT])
                STORE(ktscr[l][ps].rearrange("p (h t) -> p h t", h=8), kTa[:, :, 0:T], [BKT], CKT, W=[KTB[l][ps]])
                vsrc = i_cfv[l][s].rearrange("p (b x) -> p b x", x=512)
                for tt in range(NT):
                    i = tt % 2
                    LOAD(stg[i][:, 0:512], vsrc[:, ps * NT + tt, :], [BSTG[i]], CSTG[i])
                    v4 = stg[i][:, 0:512].rearrange("p (a b d) -> p a b d", a=4, b=2)
                    vo = vown[:, tt].rearrange("p (a b) d -> p a b d", a=4)
                    VCOPY(vo[:, :, 0, 0:64], v4[:, :, 0, :], [BSTG[i]], [BVO])
                    VCOPY(vo[:, :, 1, 64:128], v4[:, :, 1, :], [BSTG[i]], [BVO])
                STORE(vscr[l][ps].rearrange("p (t x) -> p t x", t=NT), vown[:, 0:NT].rearrange("p t h d -> p t (h d)"),
                      [BVO], CVO, W=[VSB[l][ps]])
        ln_load(0)
        LOAD(xres[0:64, 0, :], i_xs[s], [BXR], CXIN)
        layer_norm(SS, [0])
        for l in range(L):
            layer(SS, l, 0)
        STORE(o_ys[s], xres[0:64, 0, :], [BXR], COUT)
        for l in range(L):
            for h in range(4):
                STORE(o_glas[l][s][32 * h:32 * h + 32, :], gS[32 * h:32 * h + 32, l, 64 * h:64 * h + 64], [BGS[l]], CGS[l])
            s5_state_out(l, o_s5s[l][s])
            STORE(o_cvs[l][s], halo[:, l].rearrange("p c j -> p (c j)"), BHALO[l], CHALO[l])

    with nc.allow_low_precision(reason="bf16 matmul operands, fp32 accumulation"):
        sch.emit()
    st.close()
    return nc


def _ktile(w, nk=None):
    K, C = w.shape
    nk = K // 128
    return np.ascontiguousarray(w.reshape(nk, 128, C).transpose(1, 0, 2)).reshape(128, nk * C)


def make_consts(cfg):
    T = cfg.T
    c = np.zeros((128, 2048), np.float32)
    c[:, 0:128] = np.eye(128)
    c[64, 128:128 + 64] = 1.0
    c[0, 256 + 64:256 + 128] = 1.0
    k = np.arange(128)[:, None]
    q = np.arange(128)[None, :]
    c[:, 384:512] = (k <= q)
    c[:, 512:640] = 1.0
    s = np.arange(64)[:, None]
    l_ = np.arange(64)[None, :]
    c[0:64, 640:896] = np.tile((s <= l_).astype(np.float32), (1, 4))
    for h in range(4):
        c[32 * h:32 * h + 32, 896 + 64 * h:896 + 64 * h + 64] = 1.0
    c[0:64, 1152:1152 + 64] = np.eye(64)
    cm = np.ones(T, np.float32)
    cm[0::64] = 0.0
    c[:, 1280:1280 + T] = cm[None, :]
    sel = np.zeros((65, 2, 8, 68), np.float32)
    for h in range(8):
        sel[h, 0, h, 64] = -1.0
        sel[32 + h, 0, h, 65] = -1.0
        sel[64, 0, h, 66] = 1.0
        sel[64, 0, h, 67] = 1.0
        sel[64, 1, h, 64] = 8.0
        sel[64, 1, h, 65] = 8.0
        sel[h, 1, h, 66] = 8.0
        sel[32 + h, 1, h, 67] = 8.0
    return c, sel.reshape(65, 2 * 8 * 68)


def pack_weights(inp, l):
    w_in = inp["w_in"][l]
    Z = np.zeros
    t = {}
    ff = w_in[:, OFF["ff"]:OFF["ff"] + 8]
    ff40 = Z((D, 40), np.float32)
    ff40[:, 0:8] = ff
    ff40[:, 32:40] = ff
    t["wfm"] = np.concatenate([w_in[:, 0:256], w_in[:, OFF["su"]:OFF["su"] + 256], w_in[:, OFF["ga"]:OFF["ga"] + 16], ff40], 1)
    for nm, o in (("wfq", OFF["fq"]), ("wfk", OFF["fk"])):
        a = Z((D, 8, 68), np.float32)
        a[:, :, 0:64] = w_in[:, o:o + 512].reshape(D, 8, 64)
        t[nm] = a.reshape(D, 544)
    t["wtv"] = w_in[:, OFF["gv"]:OFF["gv"] + 512]
    t["wtk"] = w_in[:, OFF["fk"]:OFF["fk"] + 512]
    t["wtf"] = np.concatenate([w_in[:, OFF["fv"]:OFF["fv"] + 512], ff], 1)
    bre, bim = inp["s5_b_re"][l], inp["s5_b_im"][l]
    sb_ = Z((128, 8, 256), np.float32)
    for m in range(8):
        for gg in range(2):
            g = 2 * m + gg
            r0 = (g % 8) * 16
            sb_[r0:r0 + 16, m, gg * 64:(gg + 1) * 64] = bre[g].T
            sb_[r0:r0 + 16, m, 128 + gg * 64:128 + (gg + 1) * 64] = bim[g].T
    t["s5b"] = sb_.reshape(128, 8 * 256)
    cre, cim = inp["s5_c_re"][l], inp["s5_c_im"][l]
    sc_ = Z((128, 8, 256), np.float32)
    for c in range(8):
        for gg in range(2):
            g = 2 * c + gg
            c0 = (g % 8) * 16
            sc_[gg * 64:(gg + 1) * 64, c, c0:c0 + 16] = cre[g].T
            sc_[gg * 64:(gg + 1) * 64, c, 128 + c0:128 + c0 + 16] = cim[g].T
    t["s5cr"] = sc_.reshape(128, 8 * 256)
    t["s5g"] = inp["s5_w_glu"][l]
    mx = inp["w_mix_out"][l]
    t["wmx0"], t["wmx1"] = mx[:, 0:512], mx[:, 512:1024]
    for nm, key in (("wq", "mem_w_q"), ("wo", "mem_w_o"), ("wk", "mem_w_k"), ("wv", "mem_w_v")):
        t[nm + "0"], t[nm + "1"] = inp[key][l][:, 0:512], inp[key][l][:, 512:1024]
    up = inp["ffn_w_up"][l]
    for j in range(11):
        cols = np.concatenate([np.arange(ffn_feat(4 * j + r), ffn_feat(4 * j + r) + 128) for r in range(4)])
        t["wup%d" % j] = up[:, cols]
    dn = inp["ffn_w_down"][l]
    for hf in range(2):
        for g in range(3):
            nk = (8, 8, 6)[g]
            t["wdn%d%d" % (hf, g)] = dn[g * 1024:g * 1024 + nk * 128, hf * 512:(hf + 1) * 512]
    out = np.zeros((128, W_TOT), np.float32)
    for n, k, c in W_PACK:
        o = W_OFF[n][0]
        a = t[n]
        if n in ("s5b", "s5cr"):
            out[:, o:o + k * c] = a
        else:
            out[:, o:o + k * c] = _ktile(np.ascontiguousarray(a))
    return out


def pack_small(inp, l):
    sp = np.zeros((128, NSP), np.float32)
    sp[:, SP["gba"]] = inp["gla_b_a"][l]
    sp[0:8, SP["fbf"]] = inp["fox_b_f"][l]
    sp[32:40, SP["fbf"]] = inp["fox_b_f"][l]
    sp[:, SP["s5d"]:SP["s5d"] + 2] = inp["s5_d"][l].reshape(2, 128).T
    sp[:, SP["s5bg"]:SP["s5bg"] + 2] = inp["s5_b_glu"][l].reshape(2, 128).T
    idx = np.stack([ffn_feat(ci) + np.arange(128) for ci in range(NCH_FF)], 1)
    for tap in range(3):
        sp[:, SP["cw%d" % tap]:SP["cw%d" % tap] + NCH_FF] = inp["ffn_conv_w"][l][tap][idx]
    sp[:, SP["cb"]:SP["cb"] + NCH_FF] = inp["ffn_conv_b"][l][idx]
    for nm, key in (("lre", "s5_lam_re"), ("lim", "s5_lam_im"), ("ldt", "s5_log_dt")):
        sp[:, SP[nm]:SP[nm] + 8] = inp[key][l].reshape(8, 128).T
    for h in range(4):
        sp[32 * h:32 * h + 32, SP["hm"] + h] = 1.0
    tp = np.zeros((128, NTP), np.float32)
    tp[:, TP["gng"]:TP["gng"] + 256] = np.tile(inp["gla_norm_g"][l], 4)[None, :]
    tp[:, TP["fbf"]:TP["fbf"] + 8] = inp["fox_b_f"][l][None, :]
    return sp, tp


def prep_core(inp, c, cfg, shared):
    b = c % 2
    PAST, T = cfg.PAST, cfg.T
    ss = [2 * c, 2 * c + 1]
    m = {}
    m["xp"] = np.ascontiguousarray(inp["x_prompt"][b])
    m["xs"] = np.ascontiguousarray(inp["x_sample"][ss])
    m["memT"] = _ktile(np.ascontiguousarray(inp["mem_prompt"][b].T))
    m["stgla"] = np.ascontiguousarray(inp["state_gla"][:, ss].reshape(L, 2, 128, 64))
    s5 = np.stack([inp["state_s5_re"][:, ss], inp["state_s5_im"][:, ss]], 2)
    m["sts5"] = np.ascontiguousarray(s5.reshape(L, 2, 2, 8, 128).transpose(0, 1, 2, 4, 3))
    ck = inp["cache_fox_k"][:, ss]
    m["cfk"] = np.ascontiguousarray(ck.transpose(0, 1, 4, 3, 2)).reshape(L, 2, 64, 8 * PAST)
    cv = inp["cache_fox_v"][:, ss].reshape(L, 2, PAST // 128, 128, 512)
    m["cfv"] = np.ascontiguousarray(cv.transpose(0, 1, 3, 2, 4)).reshape(L, 2, 128, (PAST // 128) * 512)
    m["cflf"] = np.ascontiguousarray(inp["cache_fox_logf"][:, ss].transpose(0, 1, 3, 2))
    mk = inp["cache_mem_k"][:, ss].reshape(L, 2, 256, 1024)
    m["cmkT"] = np.ascontiguousarray(mk.transpose(0, 1, 3, 2).reshape(L, 2, 8, 128, 256).transpose(0, 1, 3, 2, 4)).reshape(L, 2, 128, 2048)
    mv = inp["cache_mem_v"][:, ss].reshape(L, 2, 2, 128, 1024)
    m["cmv"] = np.ascontiguousarray(mv.transpose(0, 1, 3, 2, 4)).reshape(L, 2, 128, 2048)
    idx = np.stack([ffn_feat(ci) + np.arange(128) for ci in range(NCH_FF)], 1)
    cs = inp["state_ffn_conv"][:, ss]
    m["cvst"] = np.ascontiguousarray(cs[:, :, :, idx].transpose(0, 1, 3, 4, 2)).reshape(L, 2, 128, NCH_FF * 2)
    m.update(shared)
    return m


def kernel(_cfg=None, **inp):
    cfg = _cfg or Cfg()
    SEQ, PAST, T = cfg.SEQ, cfg.PAST, cfg.T
    inp = {k: np.asarray(v) for k, v in inp.items()}
    shared = {}
    shared["wpk"] = np.stack([pack_weights(inp, l) for l in range(L)])
    lnp = np.zeros((1 + 3 * L, 128, 2 * D), np.float32)
    lnp[0, :, 0:D] = inp["ln_in_g"][None]
    lnp[0, :, D:] = inp["ln_in_b"][None]
    for l in range(L):
        for j, nm in enumerate(("ln1", "ln2", "ln3")):
            lnp[1 + 3 * l + j, :, 0:D] = inp[nm + "_g"][l][None]
            lnp[1 + 3 * l + j, :, D:] = inp[nm + "_b"][l][None]
    shared["lnp"] = lnp
    sps, tps = zip(*[pack_small(inp, l) for l in range(L)])
    shared["sp"] = np.stack(sps)
    shared["tp"] = np.stack(tps)
    shared["wa2"] = np.ascontiguousarray(inp["gla_w_a2"])
    shared["cst"], shared["sel"] = make_consts(cfg)
    nc = build(cfg)
    in_maps = [prep_core(inp, c, cfg, shared) for c in range(NCORES)]
    res = run_bass_kernel_spmd(nc, in_maps, core_ids=list(range(NCORES)))
    R = res.results
    B = 2
    f32 = np.float32
    yp = np.stack([R[b]["yp"] for b in range(B)])
    ys = np.concatenate([R[c]["ys"] for c in range(NCORES)], 0)

    def stackp(key, shape):
        return np.stack([R[b][key] for b in range(B)], 1).reshape(shape)

    gla_p = stackp("glap", (L, B, 4, 32, 64))
    s5p = np.stack([R[b]["s5p"] for b in range(B)], 1)
    s5p = s5p.transpose(0, 1, 2, 4, 3).reshape(L, B, 2, 16, 64)
    fk_p = stackp("fkp", (L, B, SEQ, 8, 64))
    fv_p = stackp("fvp", (L, B, SEQ, 8, 64))
    fl_p = stackp("flp", (L, B, SEQ, 8))
    idx = np.stack([ffn_feat(ci) + np.arange(128) for ci in range(NCH_FF)], 1)

    def conv_out(a):
        a = a.reshape(a.shape[:-1] + (NCH_FF, 2))
        o = np.zeros(a.shape[:-3] + (2, 2 * DFF), f32)
        o[..., :, idx] = np.moveaxis(a, -1, -3)
        return o

    cv_p = conv_out(np.stack([R[b]["cvp"] for b in range(B)], 1))
    mk_p = stackp("mkp", (L, B, 256, 4, 256))
    mv_p = stackp("mvp", (L, B, 256, 4, 256))

    def cats(key):
        return np.concatenate([R[c][key] for c in range(NCORES)], 1)

    gla_s = cats("glas").reshape(L, 16, 4, 32, 64)
    s5s = cats("s5s").transpose(0, 1, 2, 4, 3).reshape(L, 16, 2, 16, 64)
    fk_s = cats("fks").reshape(L, 16, 64, 8, 64)
    fv_s = cats("fvs").reshape(L, 16, 64, 8, 64)
    fl_s = cats("fls").reshape(L, 16, 64, 8)
    cv_s = conv_out(cats("cvs"))
    outs = (yp, ys, gla_p, s5p[:, :, 0], s5p[:, :, 1], fk_p, fv_p, fl_p, cv_p, mk_p, mv_p,
            gla_s, s5s[:, :, 0], s5s[:, :, 1], fk_s, fv_s, fl_s, cv_s)
    return tuple(np.ascontiguousarray(o, dtype=f32) for o in outs)
```

```python
import math
from contextlib import ExitStack

import numpy as np
import concourse.bass as bass
import concourse.mybir as mybir
from concourse.bass_utils import run_bass_kernel_spmd

F32 = mybir.dt.float32
BF16 = mybir.dt.bfloat16
ALU = mybir.AluOpType
AF = mybir.ActivationFunctionType
AX = mybir.AxisListType

D = 1024
L = 2
DFF = 2816
NCH_FF = 44
LN_EPS = 1e-5
ALPHA = float((2 * L) ** 0.25)
NCORES = 8


class Buf:
    __slots__ = ("name", "excl", "last_w", "readers")

    def __init__(self, name, excl=False):
        self.name = name
        self.excl = excl
        self.last_w = None
        self.readers = []


class Chan:
    __slots__ = ("sem", "count")

    def __init__(self, sem):
        self.sem = sem
        self.count = 0


class Op:
    __slots__ = ("engine", "fn", "deps", "signal", "sigidx", "chan", "chanval")

    def __init__(self, engine, fn, chan):
        self.engine = engine
        self.fn = fn
        self.deps = []
        self.signal = False
        self.sigidx = 0
        self.chan = chan
        self.chanval = 0


class Sched:
    ENGS = ("pe", "act", "dve", "pool", "sp")

    def __init__(self, nc, stack):
        self.nc = nc
        self.stack = stack
        self.ops = {e: [] for e in self.ENGS}
        self.esem = {e: stack.enter_context(nc.semaphore("es_" + e)) for e in ("pe", "act", "dve", "pool")}
        self.chans = []
        self.nops = 0

    def chan(self, name):
        c = Chan(self.stack.enter_context(self.nc.semaphore("ch_" + name)))
        self.chans.append(c)
        return c

    def add(self, eng, fn, reads=(), writes=(), chan=None):
        op = Op(eng, fn, chan)
        if chan is not None:
            chan.count += 16
            op.chanval = chan.count
        deps = {}
        for b in reads:
            if b.last_w is not None:
                deps[id(b.last_w)] = b.last_w
            if b.excl:
                for r in b.readers:
                    deps[id(r)] = r
        for b in writes:
            if b.last_w is not None:
                deps[id(b.last_w)] = b.last_w
            for r in b.readers:
                deps[id(r)] = r
        for d in deps.values():
            if d is op:
                continue
            if d.chan is None:
                if d.engine == eng and eng == "pe":
                    continue
                d.signal = True
            op.deps.append(d)
        for b in reads:
            if b.excl:
                b.last_w = op
                b.readers = []
            else:
                b.readers.append(op)
        for b in writes:
            b.last_w = op
            b.readers = []
        self.ops[eng].append(op)
        self.nops += 1
        return op

    def emit(self):
        nc = self.nc
        for e in ("pe", "act", "dve", "pool"):
            n = 0
            for op in self.ops[e]:
                if op.signal:
                    n += 1
                    op.sigidx = n
        chans = self.chans
        esem = self.esem

        def run(ename, e):
            waited = {}
            for op in self.ops[ename]:
                need = {}
                for d in op.deps:
                    if d.chan is not None:
                        sem, val = d.chan.sem, d.chanval
                    else:
                        sem, val = esem[d.engine], d.sigidx
                    k = id(sem)
                    if k not in need or need[k][1] < val:
                        need[k] = (sem, val)
                for k, (sem, val) in need.items():
                    if waited.get(k, 0) >= val:
                        continue
                    waited[k] = val
                    e.wait_ge(sem, val)
                inst = op.fn(e)
                if op.chan is not None:
                    inst.then_inc(op.chan.sem, 16)
                elif op.signal:
                    inst.then_inc(esem[ename], 1)
            if ename == "sp":
                for c in chans:
                    if c.count > 0 and waited.get(id(c.sem), 0) < c.count:
                        e.wait_ge(c.sem, c.count)

        with nc.Block() as block:
            @block.tensor
            def _(e):
                run("pe", e)

            @block.scalar
            def _(e):
                run("act", e)

            @block.vector
            def _(e):
                run("dve", e)

            @block.gpsimd
            def _(e):
                run("pool", e)

            @block.sync
            def _(e):
                run("sp", e)


OFF = dict(gq=0, gk=128, gv=256, go=512, ga=768, su=784, fq=1040, fk=1552, fv=2064, ff=2576)

W_STREAM = [("wfm", 8, 568), ("wfq", 8, 544), ("wfk", 8, 544), ("wtv", 8, 512), ("wtk", 8, 512), ("wtf", 8, 520),
            ("s5b", 8, 256), ("s5c", 8, 256), ("s5g", 2, 256),
            ("wmx0", 8, 512), ("wmx1", 8, 512), ("wq0", 8, 512), ("wq1", 8, 512), ("wo0", 8, 512), ("wo1", 8, 512)]
W_STREAM += [("wup%d" % j, 8, 512) for j in range(11)]
W_STREAM += [("wdn%d%d" % (h, g), (8, 8, 6)[g], 512) for h in range(2) for g in range(3)]
W_PACK = [t for t in W_STREAM if t[0] != "s5c"] + [("wk0", 8, 512), ("wk1", 8, 512), ("wv0", 8, 512), ("wv1", 8, 512),
                                                    ("s5cr", 8, 256)]
W_OFF = {}
_o = 0
for _n, _k, _c in W_PACK:
    W_OFF[_n] = (_o, _k, _c)
    _o += _k * _c
W_TOT = _o
SLOT_COLS = 8 * 568

SP = dict(gba=0, fbf=1, s5d=2, s5bg=4, cw0=6, cw1=50, cw2=94, cb=138, lre=182, lim=190, ldt=198, hm=206, hre=210, him=218)
NSP = 226
TP = dict(gng=0, fbf=256)
NTP = 264


def ffn_feat(ci):
    j, r = divmod(ci, 4)
    if r < 2:
        return (2 * j + r) * 128
    return DFF + (2 * j + r - 2) * 128


class Cfg:
    def __init__(self, SEQ=8192, PAST=2048, T=256):
        self.SEQ, self.PAST, self.T = SEQ, PAST, T
        self.NSEG = SEQ // T
        self.NPS = PAST // T
        assert SEQ % T == 0 and PAST % T == 0 and T % 128 == 0


def build(cfg):
    SEQ, PAST, T = cfg.SEQ, cfg.PAST, cfg.T
    NSEG, NPS = cfg.NSEG, cfg.NPS
    SB = 64
    nc = bass.Bass("TRN2", target_bir_lowering=False)
    st = ExitStack()
    sch = Sched(nc, st)
    add = sch.add

    def din(name, shape):
        return nc.dram_tensor(name, list(shape), F32, kind="ExternalInput").ap()

    def dout(name, shape):
        return nc.dram_tensor(name, list(shape), F32, kind="ExternalOutput").ap()

    def dscr(name, shape, dt=BF16):
        return nc.dram_tensor(name, list(shape), dt, kind="Internal").ap()

    def sb(name, shape, dt=F32):
        return st.enter_context(nc.sbuf_tensor("s_" + name, list(shape), dt))

    i_xp = din("xp", [SEQ, D])
    i_xs = din("xs", [2, 64, D])
    i_memT = din("memT", [128, 8 * 256])
    i_stgla = din("stgla", [L, 2, 128, 64])
    i_sts5 = din("sts5", [L, 2, 2, 128, 8])
    i_cfk = din("cfk", [L, 2, 64, 8 * PAST])
    i_cfv = din("cfv", [L, 2, 128, (PAST // 128) * 512])
    i_cflf = din("cflf", [L, 2, 8, PAST])
    i_cmkT = din("cmkT", [L, 2, 128, 8 * 256])
    i_cmv = din("cmv", [L, 2, 128, 2 * 1024])
    i_cvst = din("cvst", [L, 2, 128, NCH_FF * 2])
    i_wpk = din("wpk", [L, 128, W_TOT])
    i_lnp = din("lnp", [1 + 3 * L, 128, 2 * D])
    i_sp = din("sp", [L, 128, NSP])
    i_tp = din("tp", [L, 128, NTP])
    i_wa2 = din("wa2", [L, 16, 128])
    i_cst = din("cst", [128, 2048])
    i_sel = din("sel", [65, 2 * 8 * 68])

    o_yp = dout("yp", [SEQ, D])
    o_ys = dout("ys", [2, 64, D])
    o_glap = dout("glap", [L, 128, 64])
    o_s5p = dout("s5p", [L, 2, 128, 8])
    o_fkp = dout("fkp", [L, SEQ, 512])
    o_fvp = dout("fvp", [L, SEQ, 512])
    o_flp = dout("flp", [L, SEQ, 8])
    o_cvp = dout("cvp", [L, 128, NCH_FF * 2])
    o_mkp = dout("mkp", [L, 256, D])
    o_mvp = dout("mvp", [L, 256, D])
    o_glas = dout("glas", [L, 2, 128, 64])
    o_s5s = dout("s5s", [L, 2, 2, 128, 8])
    o_fks = dout("fks", [L, 2, 64, 512])
    o_fvs = dout("fvs", [L, 2, 64, 512])
    o_fls = dout("fls", [L, 2, 64, 8])
    o_cvs = dout("cvs", [L, 2, 128, NCH_FF * 2])

    wscr = {}
    for l in range(L):
        for n, k, c in W_PACK + [("s5c", 8, 256)]:
            if n == "s5cr":
                continue
            wscr[(l, n)] = dscr("w_%d_%s" % (l, n), [128, k * c])
    NKS = max(NSEG, NPS + 1)
    ktscr = [dscr("kts%d" % l, [NKS, 68, 8 * T]) for l in range(L)]
    vscr = [dscr("vs%d" % l, [NKS, 128, (T // 128) * 8 * 128]) for l in range(L)]
    KTB = [[Buf("kts%d_%d" % (l, s)) for s in range(NKS)] for l in range(L)]
    VSB = [[Buf("vs%d_%d" % (l, s)) for s in range(NKS)] for l in range(L)]
    WSB = {k: Buf("wscr") for k in wscr}

    pbank = [st.enter_context(nc.psum_tensor("pb%d" % i, [128, 512], F32)) for i in range(8)]
    PB = [Buf("pb%d" % i, excl=True) for i in range(8)]

    def pbf(i):
        return pbank[i][:].bitcast(BF16)

    NT = T // 128
    NCHK = T // 64
    cst = sb("cst", [128, 2048])
    cstb = sb("cstb", [128, 1024], BF16)
    selb = sb("selb", [65, 2 * 8 * 68], BF16)
    self32 = sb("self32", [65, 2 * 8 * 68])
    C_ID, C_PD, C_PU, C_TRI, C_ONE = 0, 128, 256, 384, 512
    C_CM4, C_BM, C_I64, C_CHM = 640, 896, 1152, 1280
    ident = cstb[:, 0:128]
    pdown = cstb[:, 128:256]
    pup = cstb[:, 256:384]
    trib = cstb[:, 384:512]
    onesb = cstb[:, 512:640]
    i64pad = cstb[0:64, 640:708]
    cmask4 = cst[0:64, C_CM4:C_CM4 + 256]
    blockmask = cst[:, C_BM:C_BM + 256]
    chunkmask = cst[:, C_CHM:C_CHM + T]
    ones32 = cst[:, C_ONE:C_ONE + 128]
    BC = Buf("consts")

    spt = sb("spt", [128, L, NSP])
    tpt = sb("tpt", [128, L, NTP])
    wa2b = sb("wa2b", [16, L, 128], BF16)
    wa2f = sb("wa2f", [16, L, 128])
    s5tab = sb("s5tab", [128, L, 2, 8, SB])
    s5r = sb("s5r", [128, L, 8])
    s5f = sb("s5f", [128, L, 4, 8])
    BP = BC

    xres = sb("xres", [128, NT, D])
    BXR = Buf("xres")
    xbt = sb("xbt", [128, 2, D], BF16)
    BXB = [Buf("xbt%d" % i) for i in range(2)]
    xTp = sb("xTp", [128, 8, T + 2], BF16)
    xT = xTp[:, :, 2:T + 2]
    BXT = Buf("xT")
    xhc = sb("xhc", [128, L, 8, 2], BF16)
    BXH = [Buf("xhc%d" % l) for l in range(L)]
    NWS = 3
    wring = [sb("wring%d" % i, [128, SLOT_COLS], BF16) for i in range(NWS)]
    BW = [Buf("wring%d" % i) for i in range(NWS)]
    CW = [sch.chan("w%d" % i) for i in range(NWS)]
    lnt = sb("lnt", [128, 2 * D])
    BLN = Buf("lnt")
    CLN = sch.chan("ln")
    s5bt = sb("s5bt", [128, 2, 8 * T])
    BS5H = [Buf("s5bt%d" % i) for i in range(2)]
    stg = [s5bt[:, i, 0:1024] for i in range(2)]
    BSTG = BS5H
    CSTG = [sch.chan("stg%d" % i) for i in range(2)]
    stgb = [sb("stgb%d" % i, [128, 1024], BF16) for i in range(2)]
    BSTGB = [Buf("stgb%d" % i) for i in range(2)]
    CSTGB = [sch.chan("stgb%d" % i) for i in range(2)]
    mkT = sb("mkT", [128, L, 8, 256], BF16)
    mvb = sb("mvb", [128, L, 2, D], BF16)
    BMK = [Buf("mk%d" % l) for l in range(L)]
    gS = sb("gS", [128, L, 256])
    gSb = sb("gSb", [128, L, 256], BF16)
    BGS = [Buf("gS%d" % l) for l in range(L)]
    CGS = [sch.chan("gS%d" % l) for l in range(L)]
    CGSL = [sch.chan("gSl%d" % l) for l in range(L)]
    s5x = sb("s5x", [128, L, 2, 8])
    BS5X = [Buf("s5x%d" % l) for l in range(L)]
    s5e = sb("s5e", [128, L, 2, 8])
    CS5X = sch.chan("s5x")
    CS5XL = sch.chan("s5xl")
    halo = sb("halo", [128, L, NCH_FF, 2])
    BHALO = [[Buf("halo%d_%d" % (l, ci)) for ci in range(NCH_FF)] for l in range(L)]
    CHALO = [sch.chan("halo%d" % l) for l in range(L)]
    CHALOL = [sch.chan("halol%d" % l) for l in range(L)]
    cfc = sb("cfc", [40, L])
    BCFC = [Buf("cfc%d" % l) for l in range(L)]

    gqk = sb("gqk", [128, 2, T])
    BGQK = Buf("gqk")
    suT = sb("suT", [128, 2, T])
    BSU = Buf("suT")
    sub = sb("sub", [128, 2, T], BF16)
    gaT = sb("gaT", [16, T], BF16)
    BGA = Buf("gaT")
    ffT = sb("ffT", [40, T])
    BFF = Buf("ffT")
    qTa = sb("qTa", [68, 8, T], BF16)
    BQT = Buf("qTa")
    kTa = sb("kTa", [68, 8, T], BF16)
    BKT = Buf("kTa")
    CKT = sch.chan("kTa")
    vtb = sb("vtb", [64, NCHK, 256], BF16)
    gtk = sb("gtk", [64, NCHK, 256])
    BVT = Buf("vtok")
    ost = [sb("ost%d" % i, [128, 520]) for i in range(2)]
    BOST = [Buf("ost%d" % i) for i in range(2)]
    COST = [sch.chan("ost%d" % i) for i in range(2)]
    vown = sb("vown", [128, NT, 8, 128], BF16)
    BVO = Buf("vown")
    CVO = sch.chan("vown")
    NTMP = 8
    tmp = [sb("tmp%d" % i, [128, T + 2]) for i in range(NTMP)]
    BT = [Buf("tmp%d" % i) for i in range(NTMP)]
    tmpb = [sb("tmpb%d" % i, [128, 4, T], BF16) for i in range(2)]
    BTB = [Buf("tmpb%d" % i) for i in range(2)]
    s5t = sb("s5t", [128, 10, 8])
    BS5T = Buf("s5t")
    sm = sb("sm", [128, 64])
    smt = sb("smt", [128, 32])
    BSMT = [Buf("smt%d" % i) for i in range(2)]
    BSM = Buf("sm")
    augs = sb("augs", [65, T], BF16)
    BAUG = Buf("augs")
    hi2 = sb("hi2", [40, T], BF16)
    big = sb("big", [128, 22, T], BF16)
    BBIG = Buf("big")
    mixT = sb("mixT", [128, 8, T], BF16)
    BMIX = Buf("mixT")
    qmT = sb("qmT", [128, 8, T], BF16)
    BQM = Buf("qmT")
    ktr = [sb("ktr%d" % i, [68, 8, T], BF16) for i in range(2)]
    BKR = [Buf("ktr%d" % i) for i in range(2)]
    CKR = [sch.chan("ktr%d" % i) for i in range(2)]
    vtr = [sb("vtr%d" % i, [128, NT, 8, 128], BF16) for i in range(2)]
    BVR = [Buf("vtr%d" % i) for i in range(2)]
    CVR = [sch.chan("vtr%d" % i) for i in range(2)]
    ptb = [sb("ptb%d" % i, [128, 512], BF16) for i in range(4)]
    BPT = [Buf("ptb%d" % i) for i in range(4)]
    rlb = sb("rlb", [128, 4, T], BF16)
    BRL = Buf("rlb")
    rl1 = sb("rl1", [1, T], BF16)
    BRL1 = Buf("rl1")
    CXIN = sch.chan("xin")
    CXIN2 = sch.chan("xin2")
    COUT = sch.chan("out")
    CPAR = sch.chan("par")

    def MM(out, lhsT, rhs, start, stop, R, W, skip=False):
        if skip:
            add("pe", lambda e: e.matmul(out, lhsT, rhs, start=start, stop=stop, skip_group_check=True), reads=R, writes=W)
        else:
            add("pe", lambda e: e.matmul(out, lhsT, rhs, start=start, stop=stop), reads=R, writes=W)

    def TR(out, in_, idn, R, W):
        add("pe", lambda e: e.transpose(out, in_, idn), reads=R + [BC], writes=W)

    def ACT(out, in_, func, R, W, bias=None, scale=1.0, accum=None):
        kw = {}
        if bias is not None:
            kw["bias"] = bias
        if accum is not None:
            kw["accum_out"] = accum
        add("act", lambda e: e.activation(out=out, in_=in_, func=func, scale=scale, **kw), reads=R, writes=W)

    def ACOPY(out, in_, R, W):
        add("act", lambda e: e.copy(out=out, in_=in_), reads=R, writes=W)

    def VCOPY(out, in_, R, W, eng="dve"):
        add(eng, lambda e: e.tensor_copy(out=out, in_=in_), reads=R, writes=W)

    def TT(out, a, b, op, R, W, eng="dve"):
        add(eng, lambda e: e.tensor_tensor(out=out, in0=a, in1=b, op=op), reads=R, writes=W)

    def TS(out, a, s1, s2, op0, op1, R, W, eng="dve"):
        if op1 is None:
            add(eng, lambda e: e.tensor_scalar(out=out, in0=a, scalar1=s1, scalar2=None, op0=op0), reads=R, writes=W)
        else:
            add(eng, lambda e: e.tensor_scalar(out=out, in0=a, scalar1=s1, scalar2=s2, op0=op0, op1=op1), reads=R, writes=W)

    def STT(out, a, s, b, op0, op1, R, W):
        add("dve", lambda e: e.scalar_tensor_tensor(out=out, in0=a, scalar=s, in1=b, op0=op0, op1=op1), reads=R, writes=W)

    def SCAN(out, d0, d1, init, R, W):
        add("dve", lambda e: e.tensor_tensor_scan(out=out, data0=d0, data1=d1, initial=init, op0=ALU.mult, op1=ALU.add),
            reads=R, writes=W)

    def RECIP(out, in_, R, W):
        add("dve", lambda e: e.reciprocal(out=out, in_=in_), reads=R, writes=W)

    def MSET(ap, v, W, eng="pool"):
        add(eng, lambda e: e.memset(ap, v), writes=W)

    def LOAD(out, in_, W, chan, R=()):
        add("sp", lambda e: e.dma_start(out=out, in_=in_), reads=list(R), writes=W, chan=chan)

    def STORE(out, in_, R, chan, W=()):
        add("pool", lambda e: e.dma_start(out=out, in_=in_), reads=R, writes=list(W), chan=chan)

    wstate = dict(next_load=0, next_use=0)
    wseq = []

    def plan_stream(nstreams):
        for _ in range(nstreams):
            for l in range(L):
                for n, k, c in W_STREAM:
                    wseq.append((l, n, k, c))

    def issue_wload():
        i = wstate["next_load"]
        if i >= len(wseq):
            return
        l, n, k, c = wseq[i]
        slot = i % NWS
        LOAD(wring[slot][:, 0:k * c], wscr[(l, n)], [BW[slot]], CW[slot], R=[WSB[(l, n)]])
        wstate["next_load"] = i + 1

    def getw(l, name):
        i = wstate["next_use"]
        assert wseq[i][0] == l and wseq[i][1] == name, (wseq[i], l, name)
        while wstate["next_load"] <= i:
            issue_wload()
        k, c = wseq[i][2], wseq[i][3]
        slot = i % NWS
        wstate["next_use"] = i + 1
        return wring[slot][:, 0:k * c].rearrange("p (k c) -> p k c", k=k), BW[slot]

    def wdone():
        while wstate["next_load"] < min(len(wseq), wstate["next_use"] + NWS - 1):
            issue_wload()

    LOAD(cst[:], i_cst, [BC], CPAR)
    LOAD(self32[:], i_sel, [BC], CPAR)
    LOAD(spt[:], i_sp.rearrange("l p n -> p l n"), [BP], CPAR)
    LOAD(tpt[:], i_tp.rearrange("l p n -> p l n"), [BP], CPAR)
    LOAD(wa2f[:], i_wa2.rearrange("l p n -> p l n"), [BP], CPAR)
    VCOPY(cstb[:, 0:640], cst[:, 0:640], [BC], [BC])
    VCOPY(cstb[0:64, 640:708], cst[0:64, C_I64:C_I64 + 68], [BC], [BC])
    VCOPY(selb[:], self32[:], [BC], [BC])
    VCOPY(wa2b[:], wa2f[:], [BP], [BP])
    MSET(rlb[:], 0.0, [BRL])
    MSET(augs[:], 0.0, [BAUG])
    MSET(augs[64:65, :], 1.0, [BAUG])
    MSET(hi2[:], 0.0, [BAUG])
    MSET(ffT[:], 0.0, [BFF])
    for i in range(2):
        MSET(vtr[i][:], 1.0, [BVR[i]])
    MSET(vown[:], 1.0, [BVO])

    castn = [0]

    def cast_pieces(src_ap, dst_ap, ncols, dstbuf):
        c0 = 0
        while c0 < ncols:
            cw = min(1024, ncols - c0)
            i = castn[0] % 2
            castn[0] += 1
            LOAD(stg[i][:, 0:cw], src_ap[:, c0:c0 + cw], [BSTG[i]], CSTG[i])
            eng = ("dve", "act", "pool")[castn[0] % 3]
            if eng == "act":
                ACOPY(stgb[i][:, 0:cw], stg[i][:, 0:cw], [BSTG[i]], [BSTGB[i]])
            else:
                VCOPY(stgb[i][:, 0:cw], stg[i][:, 0:cw], [BSTG[i]], [BSTGB[i]], eng=eng)
            STORE(dst_ap[:, c0:c0 + cw], stgb[i][:, 0:cw], [BSTGB[i]], CSTGB[i], W=[dstbuf])
            c0 += cw

    for l in range(L):
        for n, k, c in W_PACK:
            if n == "s5cr":
                continue
            o = W_OFF[n][0]
            cast_pieces(i_wpk[l][:, o:o + k * c], wscr[(l, n)], k * c, WSB[(l, n)])

    def s5_prep(l):
        lre = spt[:, l, SP["lre"]:SP["lre"] + 8]
        lim = spt[:, l, SP["lim"]:SP["lim"] + 8]
        ldt = spt[:, l, SP["ldt"]:SP["ldt"] + 8]
        t = lambda i: s5t[:, i, :]
        R = [BP]
        Wt = [BS5T]
        ACT(t(0), ldt, AF.Exp, R, Wt)
        TT(t(1), lre, t(0), ALU.mult, R + Wt, Wt)
        TT(t(2), lim, t(0), ALU.mult, R + Wt, Wt)
        ACT(s5r[:, l, :], t(1), AF.Exp, Wt, [BP])
        ACT(t(3), t(2), AF.Sin, Wt, Wt, scale=1.0 / 16)
        TS(t(4), t(2), 1.0 / 16, math.pi / 2, ALU.mult, ALU.add, Wt, Wt)
        ACT(t(4), t(4), AF.Sin, Wt, Wt)
        for _ in range(4):
            TT(t(5), t(4), t(4), ALU.mult, Wt, Wt)
            TT(t(6), t(3), t(3), ALU.mult, Wt, Wt)
            TT(t(3), t(3), t(4), ALU.mult, Wt, Wt)
            TS(t(3), t(3), 2.0, None, ALU.mult, None, Wt, Wt)
            TT(t(4), t(5), t(6), ALU.subtract, Wt, Wt)
        cosT = s5tab[:, l, 0]
        sinT = s5tab[:, l, 1]
        MSET(cosT[:, :, 0:1], 1.0, [BP])
        MSET(sinT[:, :, 0:1], 0.0, [BP])
        VCOPY(t(5), t(4), Wt, Wt)
        VCOPY(t(6), t(3), Wt, Wt)
        n = 1
        while n < SB:
            dc = t(5).unsqueeze(2).to_broadcast([128, 8, n])
            ds = t(6).unsqueeze(2).to_broadcast([128, 8, n])
            a = tmp[1][:, 0:8 * n].rearrange("p (c j) -> p c j", c=8)
            b = tmp[2][:, 0:8 * n].rearrange("p (c j) -> p c j", c=8)
            RW = Wt + [BP, BT[1], BT[2]]
            TT(a, cosT[:, :, 0:n], dc, ALU.mult, RW, [BT[1]])
            TT(b, sinT[:, :, 0:n], ds, ALU.mult, RW, [BT[2]])
            TT(cosT[:, :, n:2 * n], a, b, ALU.subtract, RW, [BP])
            TT(a, cosT[:, :, 0:n], ds, ALU.mult, RW, [BT[1]])
            TT(b, sinT[:, :, 0:n], dc, ALU.mult, RW, [BT[2]])
            TT(sinT[:, :, n:2 * n], a, b, ALU.add, RW, [BP])
            TT(t(7), t(5), t(5), ALU.mult, Wt, Wt)
            TT(t(8), t(6), t(6), ALU.mult, Wt, Wt)
            TT(t(6), t(5), t(6), ALU.mult, Wt, Wt)
            TS(t(6), t(6), 2.0, None, ALU.mult, None, Wt, Wt)
            TT(t(5), t(7), t(8), ALU.subtract, Wt, Wt)
            n *= 2
        TS(spt[:, l, SP["hre"]:SP["hre"] + 2], spt[:, l, SP["s5bg"]:SP["s5bg"] + 2], -1.0, None, ALU.mult, None, [BP], [BP])
        VCOPY(s5e[:, l, 0, :], t(5), Wt, [BP])
        VCOPY(s5e[:, l, 1, :], t(6), Wt, [BP])
        mag = s5r[:, l, :]
        u = lambda i: tmp[4][:, 8 * i:8 * i + 8]
        W4 = [BT[4]]
        RR = R + Wt + W4
        TT(u(0), mag, t(4), ALU.mult, RR, W4)
        TS(u(0), u(0), -1.0, None, ALU.add, None, RR, W4)
        TT(u(1), mag, t(3), ALU.mult, RR, W4)
        TT(u(2), lre, lre, ALU.mult, RR, W4)
        TT(u(3), lim, lim, ALU.mult, RR, W4)
        TT(u(2), u(2), u(3), ALU.add, RR, W4)
        RECIP(u(2), u(2), RR, W4)
        TT(u(3), u(0), lre, ALU.mult, RR, W4)
        TT(u(4), u(1), lim, ALU.mult, RR, W4)
        TT(u(3), u(3), u(4), ALU.add, RR, W4)
        TT(s5f[:, l, 0, :], u(3), u(2), ALU.mult, RR, [BP])
        TT(u(3), u(1), lre, ALU.mult, RR, W4)
        TT(u(4), u(0), lim, ALU.mult, RR, W4)
        TT(u(3), u(3), u(4), ALU.subtract, RR, W4)
        TT(s5f[:, l, 1, :], u(3), u(2), ALU.mult, RR, [BP])
        TT(u(5), s5f[:, l, 0, :], s5f[:, l, 0, :], ALU.mult, RR, W4)
        TT(u(6), s5f[:, l, 1, :], s5f[:, l, 1, :], ALU.mult, RR, W4)
        TT(u(5), u(5), u(6), ALU.add, RR, W4)
        RECIP(u(5), u(5), RR, W4)
        TT(s5f[:, l, 2, :], s5f[:, l, 0, :], u(5), ALU.mult, RR, [BP])
        TT(s5f[:, l, 3, :], s5f[:, l, 1, :], u(5), ALU.mult, RR, [BP])
        TS(s5f[:, l, 3, :], s5f[:, l, 3, :], -1.0, None, ALU.mult, None, RR, [BP])
        o = W_OFF["s5cr"][0]
        LOAD(stg[0][:, :], i_wpk[l][:, o:o + 1024], [BSTG[0]], CSTG[0])
        LOAD(stg[1][:, :], i_wpk[l][:, o + 1024:o + 2048], [BSTG[1]], CSTG[1])
        for c in range(8):
            sg = stg[c // 4]
            bs = BSTG[c // 4]
            cre = sg[:, (c % 4) * 256:(c % 4) * 256 + 128]
            cim = sg[:, (c % 4) * 256 + 128:(c % 4) * 256 + 256]
            fre = s5f[:, l, 0, c:c + 1]
            fim = s5f[:, l, 1, c:c + 1]
            a = tmp[5][:, 0:128]
            b = tmp[6][:, 0:128]
            TS(a, cre, fre, None, ALU.mult, None, [bs, BP], [BT[5]])
            TS(b, cim, fim, None, ALU.mult, None, [bs, BP], [BT[6]])
            TT(stgb[0][:, c * 128:c * 128 + 128], a, b, ALU.subtract, [BT[5], BT[6]], [BSTGB[0]])
            TS(a, cre, fim, None, ALU.mult, None, [bs, BP], [BT[5]])
            TS(b, cim, fre, None, ALU.mult, None, [bs, BP], [BT[6]])
            TT(a, a, b, ALU.add, [BT[5], BT[6]], [BT[5]])
            TS(stgb[1][:, c * 128:c * 128 + 128], a, -1.0, None, ALU.mult, None, [BT[5]], [BSTGB[1]])
        dst = wscr[(l, "s5c")].rearrange("p (c x) -> p c x", c=8)
        STORE(dst[:, :, 0:128], stgb[0][:, 0:1024].rearrange("p (c x) -> p c x", c=8), [BSTGB[0]], CSTGB[0], W=[WSB[(l, "s5c")]])
        STORE(dst[:, :, 128:256], stgb[1][:, 0:1024].rearrange("p (c x) -> p c x", c=8), [BSTGB[1]], CSTGB[1], W=[WSB[(l, "s5c")]])

    for l in range(L):
        s5_prep(l)

    def mem_prep_prompt(l):
        mt = big[:, 0:8, 0:256]
        for i in range(2):
            LOAD(stg[i][:, :], i_memT[:, i * 1024:(i + 1) * 1024], [BSTG[i]], CSTG[i])
            VCOPY(mt[:, 4 * i:4 * i + 4, :], stg[i][:, :].rearrange("p (k m) -> p k m", k=4), [BSTG[i]], [BBIG])
        for which, name, od in ((0, "wk", o_mkp), (1, "wv", o_mvp)):
            for hf in range(2):
                i = hf % 2
                LOAD(wring[i][:, 0:4096], wscr[(l, "%s%d" % (name, hf))], [BW[i]], CW[i], R=[WSB[(l, "%s%d" % (name, hf))]])
                w = wring[i][:, 0:4096].rearrange("p (k c) -> p k c", k=8)
                if which == 0:
                    for c4 in range(4):
                        cidx = hf * 4 + c4
                        pb = 2 + (cidx % 2)
                        for kc in range(8):
                            MM(pbank[pb][:, 0:256], w[:, kc, c4 * 128:(c4 + 1) * 128], mt[:, kc, :], kc == 0, kc == 7,
                               [BW[i], BBIG], [PB[pb]])
                        ACOPY(mkT[:, l, cidx, :], pbank[pb][:, 0:256], [PB[pb]], [BMK[l]])
                for m2 in range(2):
                    pb = 4 + m2
                    for kc in range(8):
                        MM(pbank[pb][:, :], mt[:, kc, m2 * 128:(m2 + 1) * 128], w[:, kc, :], kc == 0, kc == 7,
                           [BW[i], BBIG], [PB[pb]])
                    oi = (hf * 2 + m2) % 2
                    ACOPY(ost[oi][:, 0:512], pbank[pb][:, :], [PB[pb]], [BOST[oi]])
                    if which == 1:
                        VCOPY(mvb[:, l, m2, hf * 512:(hf + 1) * 512], ost[oi][:, 0:512], [BOST[oi]], [BMK[l]])
                    STORE(od[l][m2 * 128:(m2 + 1) * 128, hf * 512:(hf + 1) * 512], ost[oi][:, 0:512], [BOST[oi]], COST[oi])

    for l in range(L):
        mem_prep_prompt(l)

    lnstate = dict(cur=None)

    def ln_load(idx):
        LOAD(lnt[:], i_lnp[idx], [BLN], CLN)

    def layer_norm(S, tiles):
        P = S["P"]
        xs_ = {tt: xres[0:P, tt, :] for tt in tiles}
        o = {tt: 16 * (tt % 2) for tt in tiles}
        BS = {tt: BSMT[tt % 2] for tt in tiles}
        st6 = {tt: smt[0:P, o[tt]:o[tt] + 12].rearrange("p (a b) -> p a b", a=2) for tt in tiles}
        for tt in tiles:
            x = xs_[tt]
            add("dve", lambda e, x=x, a=st6[tt]: e.bn_stats(out=a[:, 0, :], in_=x[:, 0:512]), reads=[BXR], writes=[BS[tt]])
            add("dve", lambda e, x=x, a=st6[tt]: e.bn_stats(out=a[:, 1, :], in_=x[:, 512:1024]), reads=[BXR], writes=[BS[tt]])
        for tt in tiles:
            mv = smt[0:P, o[tt] + 12:o[tt] + 14]
            add("dve", lambda e, mv=mv, a=st6[tt]: e.bn_aggr(out=mv, in_=a), reads=[BS[tt]], writes=[BS[tt]])
        for tt in tiles:
            rs = smt[0:P, o[tt] + 14:o[tt] + 15]
            ACT(rs, smt[0:P, o[tt] + 13:o[tt] + 14], AF.Ln, [BS[tt]], [BS[tt]], bias=LN_EPS)
        for tt in tiles:
            rs = smt[0:P, o[tt] + 14:o[tt] + 15]
            ACT(rs, rs, AF.Exp, [BS[tt]], [BS[tt]], scale=-0.5)
        for tt in tiles:
            rs = smt[0:P, o[tt] + 14:o[tt] + 15]
            STT(xs_[tt], xs_[tt], smt[0:P, o[tt] + 12:o[tt] + 13], lnt[0:P, 0:D], ALU.subtract, ALU.mult, [BXR, BS[tt], BLN], [BXR])
            STT(xs_[tt], xs_[tt], rs, lnt[0:P, D:2 * D], ALU.mult, ALU.add, [BXR, BS[tt], BLN], [BXR])
        for tt in tiles:
            ACOPY(xbt[0:P, tt % 2, :], xs_[tt], [BXR], [BXB[tt % 2]])
        for tt in tiles:
            bk = 4 + tt % 2
            pv = pbf(bk)
            for kc in range(8):
                TR(pv[:, kc * 128:kc * 128 + P], xbt[0:P, tt % 2, kc * 128:(kc + 1) * 128], ident[0:P, 0:P], [BXB[tt % 2]], [PB[bk]])
        for tt in tiles:
            bk = 4 + tt % 2
            VCOPY(xT[:, :, tt * 128:tt * 128 + P], pbf(bk)[:, :].rearrange("p (k t) -> p k t", k=8)[:, :, 0:P], [PB[bk]], [BXT])

    def resid_from_psum(S, tt, half, pb):
        P = S["P"]
        x = xres[0:P, tt, half * 512:(half + 1) * 512]
        STT(x, x, ALPHA, pbank[pb][0:P, :], ALU.mult, ALU.add, [BXR, PB[pb]], [BXR])

    def fm_chunk(w, bw, c0, M, pb, Tn):
        for kc in range(8):
            MM(pbank[pb][0:M, 0:Tn], w[:, kc, c0:c0 + M], xT[:, kc, 0:Tn], kc == 0, kc == 7, [bw, BXT], [PB[pb]])

    def logsig_inplace(ap, R, W):
        ACT(ap, ap, AF.Exp, R, W, scale=-1.0)
        ACT(ap, ap, AF.Ln, R, W, bias=1.0)
        TS(ap, ap, -1.0, None, ALU.mult, None, R, W)

    def layer(S, l, seg):
        Tn, P, nt, nchk = S["T"], S["P"], S["nt"], S["nchk"]
        samp = S["samp"]
        sidx = S["sidx"]
        tok0 = seg * Tn

        w, bw = getw(l, "wfm")
        for j in range(2):
            pb = 2 + j
            fm_chunk(w, bw, j * 128, 128, pb, Tn)
            ACOPY(gqk[:, j, 0:Tn], pbank[pb][:, 0:Tn], [PB[pb]], [BGQK])
        for j in range(2):
            pb = 2 + j
            fm_chunk(w, bw, 256 + j * 128, 128, pb, Tn)
            ACOPY(suT[:, j, 0:Tn], pbank[pb][:, 0:Tn], [PB[pb]], [BSU])
        VCOPY(sub[:, :, 0:Tn], suT[:, :, 0:Tn], [BSU], [BSU])
        fm_chunk(w, bw, 512, 16, 2, Tn)
        ACOPY(gaT[:, 0:Tn], pbank[2][0:16, 0:Tn], [PB[2]], [BGA])
        fm_chunk(w, bw, 528, 40, 3, Tn)
        lf = tmp[7][0:40, 0:Tn]
        TS(lf, pbank[3][0:40, 0:Tn], spt[0:40, l, SP["fbf"]:SP["fbf"] + 1], None, ALU.add, None, [PB[3], BP], [BT[7]])
        logsig_inplace(lf, [BT[7]], [BT[7]])
        SCAN(ffT[:, 0:Tn], ones32[0:40, 0:1].to_broadcast([40, Tn]), lf, cfc[:, l:l + 1], [BT[7], BC, BCFC[l]], [BFF])
        VCOPY(cfc[:, l:l + 1], ffT[:, Tn - 1:Tn], [BFF], [BCFC[l]])
        VCOPY(augs[0:8, 0:Tn], ffT[0:8, 0:Tn], [BFF], [BAUG])
        VCOPY(hi2[32:40, 0:Tn], ffT[32:40, 0:Tn], [BFF], [BAUG])
        TT(augs[32:40, 0:Tn], ffT[32:40, 0:Tn], hi2[32:40, 0:Tn], ALU.subtract, [BFF, BAUG], [BAUG])
        wdone()
        w, bw = getw(l, "wfq")
        for h in range(8):
            pb = 2 + (h % 4)
            for kc in range(8):
                MM(pbank[pb][0:68, 0:Tn], w[:, kc, h * 68:(h + 1) * 68], xT[:, kc, 0:Tn], kc == 0, False, [bw, BXT], [PB[pb]])
            MM(pbank[pb][0:68, 0:Tn], selb[:, (8 + h) * 68:(9 + h) * 68], augs[:, 0:Tn], False, True, [BC, BAUG], [PB[pb]])
            ACT(qTa[:, h, 0:Tn], pbank[pb][0:68, 0:Tn], AF.Copy, [PB[pb]], [BQT], scale=0.125)
        wdone()
        w, bw = getw(l, "wfk")
        for h in range(8):
            pb = 2 + (h % 4)
            for kc in range(8):
                MM(pbank[pb][0:68, 0:Tn], w[:, kc, h * 68:(h + 1) * 68], xT[:, kc, 0:Tn], kc == 0, False, [bw, BXT], [PB[pb]])
            MM(pbank[pb][0:68, 0:Tn], selb[:, h * 68:(h + 1) * 68], augs[:, 0:Tn], False, True, [BC, BAUG], [PB[pb]])
            ACOPY(kTa[:, h, 0:Tn], pbank[pb][0:68, 0:Tn], [PB[pb]], [BKT])
        wdone()
        kslot = S["kslot"](seg)
        STORE(ktscr[l][kslot].rearrange("p (h t) -> p h t", h=8)[:, :, 0:Tn], kTa[:, :, 0:Tn], [BKT], CKT, W=[KTB[l][kslot]])
        w, bw = getw(l, "wtv")
        for c in range(nchk):
            pb = 2 + (c % 4)
            for kc in range(8):
                MM(pbank[pb][0:64, :], xT[:, kc, c * 64:(c + 1) * 64], w[:, kc, :], kc == 0, kc == 7, [bw, BXT], [PB[pb]])
            ACOPY(vtb[:, c, :], pbank[pb][0:64, 0:256], [PB[pb]], [BVT])
            VCOPY(gtk[:, c, :], pbank[pb][0:64, 256:512], [PB[pb]], [BVT])
        wdone()
        w, bw = getw(l, "wtk")
        for tt in range(nt):
            pb = 2 + (tt % 2)
            for kc in range(8):
                MM(pbank[pb][0:P, :], xT[:, kc, tt * 128:tt * 128 + P], w[:, kc, :], kc == 0, kc == 7, [bw, BXT], [PB[pb]])
            oi = tt % 2
            ACOPY(ost[oi][0:P, 0:512], pbank[pb][0:P, :], [PB[pb]], [BOST[oi]])
            STORE(S["o_fk"](l)[tok0 + tt * 128:tok0 + tt * 128 + P, :], ost[oi][0:P, 0:512], [BOST[oi]], COST[oi])
        wdone()
        w, bw = getw(l, "wtf")
        for tt in range(nt):
            for (c0, cn, pb) in ((0, 512, 2), (512, 8, 3)):
                for kc in range(8):
                    MM(pbank[pb][0:P, 0:cn], xT[:, kc, tt * 128:tt * 128 + P], w[:, kc, c0:c0 + cn], kc == 0, kc == 7,
                       [bw, BXT], [PB[pb]])
            oi = tt % 2
            ACOPY(ost[oi][0:P, 0:512], pbank[2][0:P, :], [PB[2]], [BOST[oi]])
            TT(ost[oi][0:P, 512:520], pbank[3][0:P, 0:8], tpt[0:P, l, TP["fbf"]:TP["fbf"] + 8], ALU.add, [PB[3], BP], [BOST[oi]])
            logsig_inplace(ost[oi][0:P, 512:520], [BOST[oi]], [BOST[oi]])
            v4 = ost[oi][0:P, 0:512].rearrange("p (a b d) -> p a b d", a=4, b=2)
            vo = vown[0:P, tt].rearrange("p (a b) d -> p a b d", a=4)
            VCOPY(vo[:, :, 0, 0:64], v4[:, :, 0, :], [BOST[oi]], [BVO])
            VCOPY(vo[:, :, 1, 64:128], v4[:, :, 1, :], [BOST[oi]], [BVO])
            STORE(S["o_fv"](l)[tok0 + tt * 128:tok0 + tt * 128 + P, :], ost[oi][0:P, 0:512], [BOST[oi]], COST[oi])
            STORE(S["o_fl"](l)[tok0 + tt * 128:tok0 + tt * 128 + P, :], ost[oi][0:P, 512:520], [BOST[oi]], COST[oi])
        wdone()
        STORE(vscr[l][kslot].rearrange("p (t x) -> p t x", t=NT)[0:P, 0:nt, :],
              vown[0:P, 0:nt].rearrange("p t h d -> p t (h d)"), [BVO], CVO, W=[VSB[l][kslot]])

        q = gqk[:, 0, 0:Tn]
        k = gqk[:, 1, 0:Tn]
        cum = tmp[0][:, 0:Tn]
        la = tmp[7][:, 0:Tn]
        MM(pbank[2][:, 0:Tn], wa2b[:, l, :], gaT[:, 0:Tn], True, True, [BP, BGA], [PB[2]])
        TS(la, pbank[2][:, 0:Tn], spt[:, l, SP["gba"]:SP["gba"] + 1], None, ALU.add, None, [PB[2], BP], [BT[7]])
        logsig_inplace(la, [BT[7]], [BT[7]])
        TS(la, la, 1.0 / 16.0, None, ALU.mult, None, [BT[7]], [BT[7]])
        SCAN(cum, chunkmask[:, 0:Tn], la, 0.0, [BT[7], BC], [BT[0]])
        cend = tmp[1][:, 0:nchk]
        VCOPY(cend, cum.rearrange("p (c j) -> p c j", j=64)[:, :, 63], [BT[0]], [BT[1]])
        dec = tmp[1][:, 8:8 + nchk]
        ACT(dec, cend, AF.Exp, [BT[1]], [BT[1]])
        e1 = tmp[2][:, 0:Tn]
        ACT(e1, cum, AF.Exp, [BT[0]], [BT[2]])
        qd = tmpb[0][:, 0, 0:Tn]
        STT(qd, q, 32.0 ** -0.5, e1, ALU.mult, ALU.mult, [BGQK, BT[2]], [BTB[0]])
        ACT(e1, cum, AF.Exp, [BT[0]], [BT[2]], scale=-1.0)
        kinv = tmp[3][:, 0:Tn]
        TT(kinv, k, e1, ALU.mult, [BGQK, BT[2]], [BT[3]])
        kih = tmpb[1]
        for h in range(4):
            TS(kih[:, h, 0:Tn], kinv, spt[:, l, SP["hm"] + h:SP["hm"] + h + 1], None, ALU.mult, None, [BT[3], BP], [BTB[1]])
        kend = tmpb[0][:, 1, 0:Tn]
        for c in range(nchk):
            ACT(e1[:, c * 64:(c + 1) * 64], cum[:, c * 64:(c + 1) * 64], AF.Exp, [BT[0], BT[1]], [BT[2]],
                bias=cend[:, c:c + 1], scale=-1.0)
        TT(kend, k, e1, ALU.mult, [BGQK, BT[2]], [BTB[0]])
        Sf = gS[:, l, :]
        Sb = gSb[:, l, :]
        for c in range(nchk):
            cs = slice(c * 64, (c + 1) * 64)
            for h in range(4):
                MM(pbank[2][0:64, h * 64:(h + 1) * 64], kih[:, h, cs], qd[:, cs], True, True, [BTB[0], BTB[1]], [PB[2]])
            scm = tmpb[0][0:64, 2, 0:256]
            TT(scm, pbank[2][0:64, 0:256], cmask4, ALU.mult, [PB[2], BC], [BTB[0]])
            MM(pbank[3][0:64, 0:256], qd[:, cs], Sb, True, False, [BTB[0], BGS[l]], [PB[3]])
            for h in range(4):
                MM(pbank[3][0:64, h * 64:(h + 1) * 64], scm[:, h * 64:(h + 1) * 64], vtb[:, c, h * 64:(h + 1) * 64], False, h == 3,
                   [BTB[0], BVT], [PB[3]])
            TR(pbf(4)[0:64, 0:128], kend[:, cs], ident, [BTB[0]], [PB[4]])
            keT = tmpb[0][0:64, 3, 0:128]
            VCOPY(keT, pbf(4)[0:64, 0:128], [PB[4]], [BTB[0]])
            MM(pbank[5][:, 0:256], keT, vtb[:, c, :], True, True, [BTB[0], BVT], [PB[5]])
            dsm = tmp[4][:, 0:256]
            TT(dsm, pbank[5][:, 0:256], blockmask, ALU.mult, [PB[5], BC], [BT[4]])
            STT(Sf, Sf, dec[:, c:c + 1], dsm, ALU.mult, ALU.add, [BGS[l], BT[1], BT[4]], [BGS[l]])
            ACOPY(Sb, Sf, [BGS[l]], [BGS[l]])
            osq = tmp[5][0:64, 0:256]
            ACT(osq, pbank[3][0:64, 0:256], AF.Square, [PB[3]], [BT[5]])
            ss = sm[0:64, 16:20]
            add("dve", lambda e, ss=ss, osq=osq: e.reduce_sum(out=ss, in_=osq.rearrange("p (h d) -> p h d", h=4), axis=AX.X),
                reads=[BT[5]], writes=[BSM])
            TS(ss, ss, 1.0 / 64.0, LN_EPS, ALU.mult, ALU.add, [BSM], [BSM])
            ACT(ss, ss, AF.Ln, [BSM], [BSM])
            ACT(ss, ss, AF.Exp, [BSM], [BSM], scale=-0.5)
            on = tmp[5][0:64, 0:256]
            TT(on.rearrange("p (h d) -> p h d", h=4), pbank[3][0:64, 0:256].rearrange("p (h d) -> p h d", h=4),
               ss.unsqueeze(2).to_broadcast([64, 4, 64]), ALU.mult, [PB[3], BSM], [BT[5]])
            TT(on, on, tpt[0:64, l, TP["gng"]:TP["gng"] + 256], ALU.mult, [BT[5], BP], [BT[5]])
            sg = tmp[6][0:64, 0:256]
            ACT(sg, gtk[:, c, :], AF.Exp, [BVT], [BT[6]], scale=-1.0)
            ACT(sg, sg, AF.Ln, [BT[6]], [BT[6]], bias=1.0)
            ACT(sg, sg, AF.Exp, [BT[6]], [BT[6]], scale=-1.0)
            TT(sg, sg, gtk[:, c, :], ALU.mult, [BT[6], BVT], [BT[6]])
            og = tmpb[0][0:64, 2, 0:256]
            TT(og, on, sg, ALU.mult, [BT[5], BT[6]], [BTB[0]])
            for j in range(2):
                TR(pbf(4)[:, 256 + j * 64:256 + (j + 1) * 64], og[:, j * 128:(j + 1) * 128], ident[0:64, 0:64], [BTB[0]], [PB[4]])
            VCOPY(mixT[:, 0:2, cs], pbf(4)[:, 256:384].rearrange("p (j t) -> p j t", j=2), [PB[4]], [BMIX])

        w, bw = getw(l, "s5b")
        xre = big[:, 0:8, 0:Tn]
        xim = big[:, 8:16, 0:Tn]
        cosT = s5tab[:, l, 0]
        sinT = s5tab[:, l, 1]
        nsb = Tn // SB
        v3 = lambda ap: ap.rearrange("p (s j) -> p s j", j=SB)
        btre = s5bt[:, 0, :].rearrange("p (m t) -> p m t", m=8)
        btim = s5bt[:, 1, :].rearrange("p (m t) -> p m t", m=8)
        t0_, t1_, t2_, t3_ = (tmp[i][:, 0:Tn] for i in range(4))
        for m in range(8):
            kc = m // 4
            pa, pi_ = (2, 3) if m % 2 == 0 else (4, 5)
            MM(pbank[pa][:, 0:Tn], w[:, m, 0:128], sub[:, kc, 0:Tn], True, True, [bw, BSU], [PB[pa]])
            MM(pbank[pi_][:, 0:Tn], w[:, m, 128:256], sub[:, kc, 0:Tn], True, True, [bw, BSU], [PB[pi_]])
            cb = cosT[:, m, :].unsqueeze(1).to_broadcast([128, nsb, SB])
            sbb = sinT[:, m, :].unsqueeze(1).to_broadcast([128, nsb, SB])
            pre = pbank[pa][:, 0:Tn]
            pim = pbank[pi_][:, 0:Tn]
            TT(v3(t0_), v3(pre), cb, ALU.mult, [PB[pa], BP], [BT[0]])
            TT(v3(t1_), v3(pim), sbb, ALU.mult, [PB[pi_], BP], [BT[1]])
            TT(v3(t2_), v3(pim), cb, ALU.mult, [PB[pi_], BP], [BT[2]])
            TT(v3(t3_), v3(pre), sbb, ALU.mult, [PB[pa], BP], [BT[3]])
            TT(btre[:, m, 0:Tn], t0_, t1_, ALU.add, [BT[0], BT[1]], [BS5H[0]])
            TT(btim[:, m, 0:Tn], t2_, t3_, ALU.subtract, [BT[2], BT[3]], [BS5H[1]])
        wdone()
        for b in range(4):
            MSET(pbank[b][:, :], 0.0, [PB[b]], eng="dve")
        ksegs = S["ksegs"](seg)

        def kv_load(si):
            slot, nk, diag = ksegs[si]
            r = si % 2
            LOAD(ktr[r][:, :, 0:nk], ktscr[l][slot].rearrange("p (h t) -> p h t", h=8)[:, :, 0:nk], [BKR[r]], CKR[r], R=[KTB[l][slot]])
            nkb = (nk + 127) // 128
            kp = min(128, nk)
            LOAD(vtr[r][0:kp, 0:nkb].rearrange("p t h d -> p t (h d)"), vscr[l][slot].rearrange("p (t x) -> p t x", t=NT)[0:kp, 0:nkb, :],
                 [BVR[r]], CVR[r], R=[VSB[l][slot]])

        items = []
        firsts = {}
        for si, (slot, nk, diag) in enumerate(ksegs):
            nkb = (nk + 127) // 128
            firsts[len(items)] = si
            for h in range(8):
                if diag or nkb * Tn > 512:
                    for kb in range(nkb):
                        items.append((si, h, [kb]))
                else:
                    items.append((si, h, list(range(nkb))))
        kv_load(0)
        if len(ksegs) > 1:
            kv_load(1)
        DPIPE = 3

        def att_s12(it, n):
            si, h, kbs = it
            slot, nk, diag = ksegs[si]
            r = si % 2
            pb = 4 + n % 4
            pt = ptb[n % 4]
            bpt = BPT[n % 4]
            for jj, kb in enumerate(kbs):
                kn = min(128, nk - kb * 128)
                q0 = kb * 128 if diag else 0
                MM(pbank[pb][0:kn, jj * Tn + q0:jj * Tn + Tn], ktr[r][:, h, kb * 128:kb * 128 + kn], qTa[:, h, q0:Tn], True, True,
                   [BKR[r], BQT], [PB[pb]])
            if diag:
                ACT(pt[0:kn, q0:Tn], pbank[pb][0:kn, q0:Tn], AF.Exp, [PB[pb]], [bpt])
                qe = min(Tn, q0 + 128)
                TT(pt[0:kn, q0:qe], pt[0:kn, q0:qe], trib[0:kn, 0:qe - q0], ALU.mult, [bpt, BC], [bpt])
            else:
                ACT(pt[0:kn, 0:len(kbs) * Tn], pbank[pb][0:kn, 0:len(kbs) * Tn], AF.Exp, [PB[pb]], [bpt])

        def att_s3(it, n):
            si, h, kbs = it
            slot, nk, diag = ksegs[si]
            nkb = (nk + 127) // 128
            r = si % 2
            pt = ptb[n % 4]
            bpt = BPT[n % 4]
            ab = h // 2
            acc = pbank[ab][:, (h % 2) * 256:(h % 2) * 256 + Tn]
            for jj, kb in enumerate(kbs):
                kn = min(128, nk - kb * 128)
                q0 = kb * 128 if diag else 0
                last = (si == len(ksegs) - 1) and (kb == nkb - 1)
                MM(acc[:, q0:Tn], vtr[r][0:kn, kb, h, :], pt[0:kn, jj * Tn + q0:jj * Tn + Tn], False, last, [BVR[r], bpt], [PB[ab]], skip=True)

        cS = s5e[:, l, 0, :]
        sS = s5e[:, l, 1, :]
        for sbi in range(nsb):
            js = slice(sbi * SB, (sbi + 1) * SB)
            if sbi == 0:
                wl_re, wl_im, Rw = s5x[:, l, 0, :], s5x[:, l, 1, :], [BS5X[l]]
            else:
                wl_re, wl_im, Rw = btre[:, :, sbi * SB - 1], btim[:, :, sbi * SB - 1], [BS5H[0], BS5H[1]]
            ini = sm[:, 32:48].rearrange("p (a c) -> p a c", a=2)
            ta = sm[:, 48:56]
            tb = sm[:, 56:64]
            TT(ini[:, 0, :], cS, wl_re, ALU.mult, [BP] + Rw, [BSM])
            TT(ta, sS, wl_im, ALU.mult, [BP] + Rw, [BSM])
            TT(ini[:, 0, :], ini[:, 0, :], ta, ALU.subtract, [BSM], [BSM])
            TT(ini[:, 1, :], cS, wl_im, ALU.mult, [BP] + Rw, [BSM])
            TT(tb, sS, wl_re, ALU.mult, [BP] + Rw, [BSM])
            TT(ini[:, 1, :], ini[:, 1, :], tb, ALU.add, [BSM], [BSM])
            for m in range(8):
                rbc = s5r[:, l, m:m + 1].to_broadcast([128, SB])
                SCAN(btre[:, m, js], rbc, btre[:, m, js], ini[:, 0, m:m + 1], [BP, BS5H[0], BSM], [BS5H[0]])
                SCAN(btim[:, m, js], rbc, btim[:, m, js], ini[:, 1, m:m + 1], [BP, BS5H[1], BSM], [BS5H[1]])
        VCOPY(s5x[:, l, 0, :], btre[:, :, Tn - 1], [BS5H[0]], [BS5X[l]])
        VCOPY(s5x[:, l, 1, :], btim[:, :, Tn - 1], [BS5H[1]], [BS5X[l]])
        for m in range(8):
            cb = cosT[:, m, :].unsqueeze(1).to_broadcast([128, nsb, SB])
            sbb = sinT[:, m, :].unsqueeze(1).to_broadcast([128, nsb, SB])
            wre = btre[:, m, 0:Tn]
            wim = btim[:, m, 0:Tn]
            TT(v3(t0_), v3(wre), cb, ALU.mult, [BS5H[0], BP], [BT[0]])
            TT(v3(t1_), v3(wim), sbb, ALU.mult, [BS5H[1], BP], [BT[1]])
            TT(v3(t2_), v3(wim), cb, ALU.mult, [BS5H[1], BP], [BT[2]])
            TT(v3(t3_), v3(wre), sbb, ALU.mult, [BS5H[0], BP], [BT[3]])
            TT(xre[:, m, :], t0_, t1_, ALU.subtract, [BT[0], BT[1]], [BBIG])
            TT(xim[:, m, :], t2_, t3_, ALU.add, [BT[2], BT[3]], [BBIG])
        for i in range(len(items) + DPIPE):
            if i < len(items):
                att_s12(items[i], i)
            k_ = i - DPIPE
            if k_ >= 0:
                att_s3(items[k_], k_)
                if k_ in firsts and firsts[k_] >= 1 and firsts[k_] + 1 < len(ksegs):
                    kv_load(firsts[k_] + 1)
        for g4 in range(2):
            hs = list(range(4 * g4, 4 * g4 + 4))
            info = {}
            for hh, h in enumerate(hs):
                ab = h // 2
                acc = pbank[ab][:, (h % 2) * 256:(h % 2) * 256 + Tn]
                lo, hi_ = (0, 64) if h % 2 == 0 else (64, 128)
                l0 = 64 if h % 2 == 0 else 0
                info[h] = (ab, acc, lo, hi_, l0)
                ACT(tmp[7][l0:l0 + 1, 0:Tn], acc[l0:l0 + 1, :], AF.Ln, [PB[ab]], [BT[7]])
                ACT(rlb[l0:l0 + 1, hh, 0:Tn], tmp[7][l0:l0 + 1, 0:Tn], AF.Exp, [BT[7]], [BRL], scale=-1.0)
            for hh, h in enumerate(hs):
                MM(pbank[4 + hh][:, 0:Tn], pdown if h % 2 == 0 else pup, rlb[:, hh, 0:Tn], True, True, [BC, BRL], [PB[4 + hh]])
            for hh, h in enumerate(hs):
                ab, acc, lo, hi_, l0 = info[h]
                ACOPY(tmp[hh][lo:hi_, 0:Tn], pbank[4 + hh][lo:hi_, 0:Tn], [PB[4 + hh]], [BT[hh]])
            for hh, h in enumerate(hs):
                ab, acc, lo, hi_, l0 = info[h]
                TT(mixT[lo:hi_, 4 + h // 2, 0:Tn], acc[lo:hi_, :], tmp[hh][lo:hi_, 0:Tn], ALU.mult, [PB[ab], BT[hh]], [BMIX])

        w, bw = getw(l, "s5c")
        wg, bwg = getw(l, "s5g")
        zT = tmpb[1][:, 0:2, 0:Tn]
        zf = [tmp[0][:, 0:Tn], tmp[1][:, 0:Tn]]
        for oc in range(2):
            pb = 2 + oc
            n = 0
            for c in range(4 * oc, 4 * oc + 4):
                MM(pbank[pb][:, 0:Tn], w[:, c, 0:128], xre[:, c, :], n == 0, False, [bw, BBIG], [PB[pb]])
                MM(pbank[pb][:, 0:Tn], w[:, c, 128:256], xim[:, c, :], False, n == 3, [bw, BBIG], [PB[pb]])
                n += 1
            y = tmp[2][:, 0:Tn]
            STT(y, suT[:, oc, 0:Tn], spt[:, l, SP["s5d"] + oc:SP["s5d"] + oc + 1], pbank[pb][:, 0:Tn], ALU.mult, ALU.add,
                [BSU, BP, PB[pb]], [BT[2]])
            gelu(y, zf[oc], [BT[2]], [BT[oc]], 3, Tn)
            ACOPY(zT[:, oc, :], zf[oc], [BT[oc]], [BTB[1]])
        for oc in range(2):
            pb = 2 + oc
            for kc in range(2):
                MM(pbank[pb][:, 0:Tn], wg[:, kc, oc * 128:(oc + 1) * 128], zT[:, kc, :], kc == 0, kc == 1, [bwg, BTB[1]], [PB[pb]])
            gt = tmp[2][:, 0:Tn]
            ACT(gt, pbank[pb][:, 0:Tn], AF.Exp, [PB[pb], BP], [BT[2]], bias=spt[:, l, SP["hre"] + oc:SP["hre"] + oc + 1], scale=-1.0)
            ACT(gt, gt, AF.Ln, [BT[2]], [BT[2]], bias=1.0)
            ACT(gt, gt, AF.Exp, [BT[2]], [BT[2]], scale=-1.0)
            TT(mixT[:, 2 + oc, 0:Tn], zf[oc], gt, ALU.mult, [BT[oc], BT[2]], [BMIX])
        wdone()

        ln_load(1 + 3 * l)
        for hf in range(2):
            w, bw = getw(l, "wmx%d" % hf)
            for tt in range(nt):
                pb = 2 + tt % 2
                for kc in range(8):
                    MM(pbank[pb][0:P, :], mixT[:, kc, tt * 128:tt * 128 + P], w[:, kc, :], kc == 0, kc == 7, [bw, BMIX], [PB[pb]])
                resid_from_psum(S, tt, hf, pb)
            wdone()
        layer_norm(S, list(range(nt)))

        mk = S["mkT"](l)
        mv_ = S["mv"](l)
        bmk = S["bmk"](l)
        for hf in range(2):
            w, bw = getw(l, "wq%d" % hf)
            for c4 in range(4):
                pb = 2 + c4 % 2
                fm_chunk(w, bw, c4 * 128, 128, pb, Tn)
                ACOPY(qmT[:, hf * 4 + c4, 0:Tn], pbank[pb][:, 0:Tn], [PB[pb]], [BQM])
            wdone()
        def mem_a(h):
            pt = ptb[h]
            for mc in range(2):
                pb = 2 + (h % 2) * 2 + mc
                for dc in range(2):
                    MM(pbank[pb][:, 0:Tn], mk[:, 2 * h + dc, mc * 128:(mc + 1) * 128], qmT[:, 2 * h + dc, 0:Tn], dc == 0, dc == 1,
                       [bmk, BQM], [PB[pb]])
                ACT(pt[:, mc * 256:mc * 256 + Tn], pbank[pb][:, 0:Tn], AF.Exp, [PB[pb]], [BPT[h]], scale=1.0 / 16.0)

        def mem_b(h):
            pt = ptb[h]
            for mc in range(2):
                MM(pbank[6][0:1, 0:Tn], onesb[:, 0:1], pt[:, mc * 256:mc * 256 + Tn], mc == 0, mc == 1, [BC, BPT[h]], [PB[6]])
            ACT(tmp[7][0:1, 0:Tn], pbank[6][0:1, 0:Tn], AF.Ln, [PB[6]], [BT[7]])
            ACT(rl1[0:1, 0:Tn], tmp[7][0:1, 0:Tn], AF.Exp, [BT[7]], [BRL1], scale=-1.0)
            MM(pbank[6][:, 256:256 + Tn], onesb[0:1, :], rl1[0:1, 0:Tn], True, True, [BC, BRL1], [PB[6]])
            rl = tmp[h % 2][:, 0:Tn]
            ACOPY(rl, pbank[6][:, 256:256 + Tn], [PB[6]], [BT[h % 2]])
            for dc in range(2):
                pb = (7, 0)[dc]
                for mc in range(2):
                    MM(pbank[pb][:, 0:Tn], mv_[:, mc, (2 * h + dc) * 128:(2 * h + dc + 1) * 128], pt[:, mc * 256:mc * 256 + Tn],
                       mc == 0, mc == 1, [bmk, BPT[h]], [PB[pb]])
                TT(mixT[:, 2 * h + dc, 0:Tn], pbank[pb][:, 0:Tn], rl, ALU.mult, [PB[pb], BT[h % 2]], [BMIX])

        mem_a(0)
        for h in range(1, 4):
            mem_a(h)
            mem_b(h - 1)
        mem_b(3)
        ln_load(2 + 3 * l)
        for hf in range(2):
            w, bw = getw(l, "wo%d" % hf)
            for tt in range(nt):
                pb = 2 + tt % 2
                for kc in range(8):
                    MM(pbank[pb][0:P, :], mixT[:, kc, tt * 128:tt * 128 + P], w[:, kc, :], kc == 0, kc == 7, [bw, BMIX], [PB[pb]])
                resid_from_psum(S, tt, hf, pb)
            wdone()
        layer_norm(S, list(range(nt)))

        hl = halo[:, l]
        if not samp:
            VCOPY(xTp[:, :, 0:2], xhc[:, l], [BXH[l]], [BXT])
        for j in range(11):
            w, bw = getw(l, "wup%d" % j)
            bk = [(j % 2) * 4 + r4 for r4 in range(4)]
            ys = [tmp[(j % 2) * 4 + r4][:, 0:Tn] for r4 in range(4)]
            bys = [BT[(j % 2) * 4 + r4] for r4 in range(4)]
            cwp = lambda tap, ci: spt[:, l, SP["cw%d" % tap] + ci:SP["cw%d" % tap] + ci + 1]
            c0 = 0 if samp else 2
            for r4 in range(4):
                pb = bk[r4]
                for kc in range(8):
                    if samp:
                        MM(pbank[pb][:, 0:Tn], w[:, kc, r4 * 128:(r4 + 1) * 128], xT[:, kc, 0:Tn], kc == 0, kc == 7, [bw, BXT], [PB[pb]])
                    else:
                        MM(pbank[pb][:, 0:Tn + 2], w[:, kc, r4 * 128:(r4 + 1) * 128], xTp[:, kc, 0:Tn + 2], kc == 0, kc == 7,
                           [bw, BXT], [PB[pb]])
            for r4 in range(4):
                ci = 4 * j + r4
                ACT(ys[r4], pbank[bk[r4]][:, c0:c0 + Tn], AF.Identity, [PB[bk[r4]], BP], [bys[r4]],
                    bias=spt[:, l, SP["cb"] + ci:SP["cb"] + ci + 1], scale=cwp(2, ci))
            for r4 in range(4):
                ci = 4 * j + r4
                pb = bk[r4]
                ps = pbank[pb]
                y = ys[r4]
                by = bys[r4]
                bh = BHALO[l][ci]
                if samp:
                    STT(y[:, 1:Tn], ps[:, 0:Tn - 1], cwp(1, ci), y[:, 1:Tn], ALU.mult, ALU.add, [PB[pb], BP, by], [by])
                    STT(y[:, 0:1], hl[:, ci, 1:2], cwp(1, ci), y[:, 0:1], ALU.mult, ALU.add, [bh, BP, by], [by])
                    STT(y[:, 2:Tn], ps[:, 0:Tn - 2], cwp(0, ci), y[:, 2:Tn], ALU.mult, ALU.add, [PB[pb], BP, by], [by])
                    STT(y[:, 0:2], hl[:, ci, 0:2], cwp(0, ci), y[:, 0:2], ALU.mult, ALU.add, [bh, BP, by], [by])
                    ACOPY(hl[:, ci, :], ps[:, Tn - 2:Tn], [PB[pb]], [bh])
                else:
                    STT(y, ps[:, 1:Tn + 1], cwp(1, ci), y, ALU.mult, ALU.add, [PB[pb], BP, by], [by])
                    STT(y, ps[:, 0:Tn], cwp(0, ci), y, ALU.mult, ALU.add, [PB[pb], BP, by], [by])
                    if seg == NSEG - 1:
                        ACOPY(hl[:, ci, :], ps[:, Tn:Tn + 2], [PB[pb]], [bh])
                if r4 < 2:
                    ACT(y, y, AF.Gelu_apprx_tanh, [by], [by])
            for r4 in range(2):
                TT(big[:, 2 * j + r4, 0:Tn], ys[r4], ys[2 + r4], ALU.mult, [bys[r4], bys[2 + r4]], [BBIG])
            wdone()
        if not samp:
            VCOPY(xhc[:, l], xTp[:, :, Tn:Tn + 2], [BXT], [BXH[l]])
        ln_load(3 + 3 * l)
        for hf in range(2):
            for g in range(3):
                w, bw = getw(l, "wdn%d%d" % (hf, g))
                nk = (8, 8, 6)[g]
                for tt in range(nt):
                    pb = 2 + tt
                    for kk in range(nk):
                        kc = g * 8 + kk
                        MM(pbank[pb][0:P, :], big[:, kc, tt * 128:tt * 128 + P], w[:, kk, :], kc == 0, kc == 21, [bw, BBIG], [PB[pb]])
                wdone()
            for tt in range(nt):
                resid_from_psum(S, tt, hf, 2 + tt)
        layer_norm(S, list(range(nt)))

    def gelu(y, out, R, W, ti, Tn):
        t = tmp[ti][:, 0:Tn]
        bt = BT[ti]
        ACT(t, y, AF.Square, R, [bt])
        TS(t, t, 0.044715, 1.0, ALU.mult, ALU.add, [bt], [bt])
        TT(t, t, y, ALU.mult, [bt] + R, [bt])
        ACT(t, t, AF.Sigmoid, [bt], [bt], scale=1.5957691216057308)
        TT(out, y, t, ALU.mult, R + [bt], W)

    def cmul(o_re, o_im, a_re, a_im, b_re, b_im, conj_a, R, W):
        ta = sm[:, 48:56]
        tb = sm[:, 56:64]
        TT(ta, a_re, b_re, ALU.mult, R, [BSM])
        TT(tb, a_im, b_im, ALU.mult, R, [BSM])
        TT(o_re, ta, tb, ALU.add if conj_a else ALU.subtract, [BSM], W)
        TT(ta, a_re, b_im, ALU.mult, R, [BSM])
        TT(tb, a_im, b_re, ALU.mult, R, [BSM])
        TT(o_im, ta, tb, ALU.subtract if conj_a else ALU.add, [BSM], W)

    def s5_state_out(l, dst):
        xq = sm[:, 0:16].rearrange("p (a c) -> p a c", a=2)
        so = tmp[0][:, 0:16].rearrange("p (a c) -> p a c", a=2)
        cmul(xq[:, 0, :], xq[:, 1, :], s5tab[:, l, 0, :, SB - 1], s5tab[:, l, 1, :, SB - 1], s5x[:, l, 0, :], s5x[:, l, 1, :], False,
             [BP, BS5X[l], BSM], [BSM])
        cmul(so[:, 0, :], so[:, 1, :], s5f[:, l, 0, :], s5f[:, l, 1, :], xq[:, 0, :], xq[:, 1, :], False, [BP, BSM], [BT[0]])
        STORE(dst.rearrange("a p c -> p a c"), so, [BT[0]], CS5X)

    def s5_state_in(l, src):
        h0 = tmp[0][:, 0:16].rearrange("p (a c) -> p a c", a=2)
        LOAD(h0, src.rearrange("a p c -> p a c"), [BT[0]], CS5XL)
        xq = sm[:, 0:16].rearrange("p (a c) -> p a c", a=2)
        cmul(xq[:, 0, :], xq[:, 1, :], s5f[:, l, 2, :], s5f[:, l, 3, :], h0[:, 0, :], h0[:, 1, :], False, [BP, BT[0], BSM], [BSM])
        cmul(s5x[:, l, 0, :], s5x[:, l, 1, :], s5tab[:, l, 0, :, SB - 1], s5tab[:, l, 1, :, SB - 1], xq[:, 0, :], xq[:, 1, :], True,
             [BP, BSM], [BS5X[l]])

    plan_stream(NSEG + 2)

    SP_ = dict(T=T, P=128, nt=NT, nchk=NCHK, samp=False, sidx=0,
               kslot=lambda seg: seg,
               ksegs=lambda seg: [(s, T, False) for s in range(seg)] + [(seg, T, True)],
               o_fk=lambda l: o_fkp[l], o_fv=lambda l: o_fvp[l], o_fl=lambda l: o_flp[l],
               mkT=lambda l: mkT[:, l], mv=lambda l: mvb[:, l], bmk=lambda l: BMK[l])
    for l in range(L):
        MSET(gS[:, l, :], 0.0, [BGS[l]])
        MSET(gSb[:, l, :], 0.0, [BGS[l]])
        MSET(s5x[:, l], 0.0, [BS5X[l]])
        MSET(halo[:, l], 0.0, BHALO[l])
        MSET(cfc[:, l:l + 1], 0.0, [BCFC[l]])
        MSET(xhc[:, l], 0.0, [BXH[l]])
    for seg in range(NSEG):
        ln_load(0)
        for tt in range(NT):
            LOAD(xres[:, tt, :], i_xp[seg * T + tt * 128:seg * T + (tt + 1) * 128, :], [BXR], CXIN)
        layer_norm(SP_, list(range(NT)))
        for l in range(L):
            layer(SP_, l, seg)
        for tt in range(NT):
            STORE(o_yp[seg * T + tt * 128:seg * T + (tt + 1) * 128, :], xres[:, tt, :], [BXR], COUT)
    for l in range(L):
        for h in range(4):
            STORE(o_glap[l][32 * h:32 * h + 32, :], gS[32 * h:32 * h + 32, l, 64 * h:64 * h + 64], [BGS[l]], CGS[l])
        s5_state_out(l, o_s5p[l])
        STORE(o_cvp[l], halo[:, l].rearrange("p c j -> p (c j)"), BHALO[l], CHALO[l])

    smk = mkT
    for s in range(2):
        SS = dict(T=64, P=64, nt=1, nchk=1, samp=True, sidx=s,
                  kslot=lambda seg: NPS,
                  ksegs=lambda seg: [(ps, T, False) for ps in range(NPS)] + [(NPS, 64, True)],
                  o_fk=lambda l, s=s: o_fks[l][s], o_fv=lambda l, s=s: o_fvs[l][s], o_fl=lambda l, s=s: o_fls[l][s],
                  mkT=lambda l: mkT[:, l], mv=lambda l: mvb[:, l], bmk=lambda l: BMK[l])
        for l in range(L):
            MSET(gS[:, l, :], 0.0, [BGS[l]])
            for h in range(4):
                LOAD(gS[32 * h:32 * h + 32, l, 64 * h:64 * h + 64], i_stgla[l][s][32 * h:32 * h + 32, :], [BGS[l]], CGSL[l])
            ACOPY(gSb[:, l, :], gS[:, l, :], [BGS[l]], [BGS[l]])
            s5_state_in(l, i_sts5[l][s])
            LOAD(halo[:, l].rearrange("p c j -> p (c j)"), i_cvst[l][s], BHALO[l], CHALOL[l])
            for i in range(2):
                LOAD(stg[i][:, :], i_cmkT[l][s][:, i * 1024:(i + 1) * 1024], [BSTG[i]], CSTG[i])
                VCOPY(mkT[:, l, 4 * i:4 * i + 4, :], stg[i][:, :].rearrange("p (k m) -> p k m", k=4), [BSTG[i]], [BMK[l]])
            for i in range(2):
                LOAD(stg[i][:, :], i_cmv[l][s][:, i * 1024:(i + 1) * 1024], [BSTG[i]], CSTG[i])
                VCOPY(mvb[:, l, i, :], stg[i][:, :], [BSTG[i]], [BMK[l]])
            cfp = tmp[6][0:40, 0:T]
            BXIN = BT[6]
            MSET(cfc[:, l:l + 1], 0.0, [BCFC[l]])
            for ps in range(NPS):
                MSET(tmp[6][0:40, 0:T], 0.0, [BXIN])
                LOAD(tmp[6][0:8, 0:T], i_cflf[l][s][:, ps * T:(ps + 1) * T], [BXIN], CXIN2)
                LOAD(tmp[6][32:40, 0:T], i_cflf[l][s][:, ps * T:(ps + 1) * T], [BXIN], CXIN2)
                cfo = tmp[7][0:40, 0:T]
                SCAN(cfo, ones32[0:40, 0:1].to_broadcast([40, T]), cfp, cfc[:, l:l + 1], [BXIN, BC, BCFC[l]], [BT[7]])
                VCOPY(cfc[:, l:l + 1], cfo[:, T - 1:T], [BT[7]], [BCFC[l]])
                VCOPY(augs[0:8, 0:T], cfo[0:8, :], [BT[7]], [BAUG])
                VCOPY(hi2[32:40, 0:T], cfo[32:40, :], [BT[7]], [BAUG])
                TT(augs[32:40, 0:T], cfo[32:40, :], hi2[32:40, 0:T], ALU.subtract, [BT[7], BAUG], [BAUG])
                kc_src = i_cfk[l][s].rearrange("p (h t) -> p h t", h=8)
                for h in range(8):
                    i = h % 2
                    LOAD(stg[i][0:64, 0:T], kc_src[:, h, ps * T:(ps + 1) * T], [BSTG[i]], CSTG[i])
                    VCOPY(stgb[i][0:64, 0:T], stg[i][0:64, 0:T], [BSTG[i]], [BSTGB[i]])
                    pb = 2 + i
                    MM(pbank[pb][0:68, 0:T], i64pad, stgb[i][0:64, 0:T], True, False, [BC, BSTGB[i]], [PB[pb]])
                    MM(pbank[pb][0:68, 0:T], selb[:, h * 68:(h + 1) * 68], augs[:, 0:T], False, True, [BC, BAUG], [PB[pb]])
                    ACOPY(kTa[:, h, 0:T], pbank[pb][0:68, 0:T], [PB[pb]], [BKT])
                STORE(ktscr[l][ps].rearrange("p (h t) -> p h t", h=8), kTa[:, :, 0:T], [BKT], CKT, W=[KTB[l][ps]])
                vsrc = i_cfv[l][s].rearrange("p (b x) -> p b x", x=512)
                for tt in range(NT):
                    i = tt % 2
                    LOAD(stg[i][:, 0:512], vsrc[:, ps * NT + tt, :], [BSTG[i]], CSTG[i])
                    v4 = stg[i][:, 0:512].rearrange("p (a b d) -> p a b d", a=4, b=2)
                    vo = vown[:, tt].rearrange("p (a b) d -> p a b d", a=4)
                    VCOPY(vo[:, :, 0, 0:64], v4[:, :, 0, :], [BSTG[i]], [BVO])
                    VCOPY(vo[:, :, 1, 64:128], v4[:, :, 1, :], [BSTG[i]], [BVO])
                STORE(vscr[l][ps].rearrange("p (t x) -> p t x", t=NT), vown[:, 0:NT].rearrange("p t h d -> p t (h d)"),
                      [BVO], CVO, W=[VSB[l][ps]])
        ln_load(0)
        LOAD(xres[0:64, 0, :], i_xs[s], [BXR], CXIN)
        layer_norm(SS, [0])
        for l in range(L):
            layer(SS, l, 0)
        STORE(o_ys[s], xres[0:64, 0, :], [BXR], COUT)
        for l in range(L):
            for h in range(4):
                STORE(o_glas[l][s][32 * h:32 * h + 32, :], gS[32 * h:32 * h + 32, l, 64 * h:64 * h + 64], [BGS[l]], CGS[l])
            s5_state_out(l, o_s5s[l][s])
            STORE(o_cvs[l][s], halo[:, l].rearrange("p c j -> p (c j)"), BHALO[l], CHALO[l])

    with nc.allow_low_precision(reason="bf16 matmul operands, fp32 accumulation"):
        sch.emit()
    st.close()
    return nc


def _ktile(w, nk=None):
    K, C = w.shape
    nk = K // 128
    return np.ascontiguousarray(w.reshape(nk, 128, C).transpose(1, 0, 2)).reshape(128, nk * C)


def make_consts(cfg):
    T = cfg.T
    c = np.zeros((128, 2048), np.float32)
    c[:, 0:128] = np.eye(128)
    c[64, 128:128 + 64] = 1.0
    c[0, 256 + 64:256 + 128] = 1.0
    k = np.arange(128)[:, None]
    q = np.arange(128)[None, :]
    c[:, 384:512] = (k <= q)
    c[:, 512:640] = 1.0
    s = np.arange(64)[:, None]
    l_ = np.arange(64)[None, :]
    c[0:64, 640:896] = np.tile((s <= l_).astype(np.float32), (1, 4))
    for h in range(4):
        c[32 * h:32 * h + 32, 896 + 64 * h:896 + 64 * h + 64] = 1.0
    c[0:64, 1152:1152 + 64] = np.eye(64)
    cm = np.ones(T, np.float32)
    cm[0::64] = 0.0
    c[:, 1280:1280 + T] = cm[None, :]
    sel = np.zeros((65, 2, 8, 68), np.float32)
    for h in range(8):
        sel[h, 0, h, 64] = -1.0
        sel[32 + h, 0, h, 65] = -1.0
        sel[64, 0, h, 66] = 1.0
        sel[64, 0, h, 67] = 1.0
        sel[64, 1, h, 64] = 8.0
        sel[64, 1, h, 65] = 8.0
        sel[h, 1, h, 66] = 8.0
        sel[32 + h, 1, h, 67] = 8.0
    return c, sel.reshape(65, 2 * 8 * 68)


def pack_weights(inp, l):
    w_in = inp["w_in"][l]
    Z = np.zeros
    t = {}
    ff = w_in[:, OFF["ff"]:OFF["ff"] + 8]
    ff40 = Z((D, 40), np.float32)
    ff40[:, 0:8] = ff
    ff40[:, 32:40] = ff
    t["wfm"] = np.concatenate([w_in[:, 0:256], w_in[:, OFF["su"]:OFF["su"] + 256], w_in[:, OFF["ga"]:OFF["ga"] + 16], ff40], 1)
    for nm, o in (("wfq", OFF["fq"]), ("wfk", OFF["fk"])):
        a = Z((D, 8, 68), np.float32)
        a[:, :, 0:64] = w_in[:, o:o + 512].reshape(D, 8, 64)
        t[nm] = a.reshape(D, 544)
    t["wtv"] = w_in[:, OFF["gv"]:OFF["gv"] + 512]
    t["wtk"] = w_in[:, OFF["fk"]:OFF["fk"] + 512]
    t["wtf"] = np.concatenate([w_in[:, OFF["fv"]:OFF["fv"] + 512], ff], 1)
    bre, bim = inp["s5_b_re"][l], inp["s5_b_im"][l]
    sb_ = Z((128, 8, 256), np.float32)
    for m in range(8):
        for gg in range(2):
            g = 2 * m + gg
            r0 = (g % 8) * 16
            sb_[r0:r0 + 16, m, gg * 64:(gg + 1) * 64] = bre[g].T
            sb_[r0:r0 + 16, m, 128 + gg * 64:128 + (gg + 1) * 64] = bim[g].T
    t["s5b"] = sb_.reshape(128, 8 * 256)
    cre, cim = inp["s5_c_re"][l], inp["s5_c_im"][l]
    sc_ = Z((128, 8, 256), np.float32)
    for c in range(8):
        for gg in range(2):
            g = 2 * c + gg
            c0 = (g % 8) * 16
            sc_[gg * 64:(gg + 1) * 64, c, c0:c0 + 16] = cre[g].T
            sc_[gg * 64:(gg + 1) * 64, c, 128 + c0:128 + c0 + 16] = cim[g].T
    t["s5cr"] = sc_.reshape(128, 8 * 256)
    t["s5g"] = inp["s5_w_glu"][l]
    mx = inp["w_mix_out"][l]
    t["wmx0"], t["wmx1"] = mx[:, 0:512], mx[:, 512:1024]
    for nm, key in (("wq", "mem_w_q"), ("wo", "mem_w_o"), ("wk", "mem_w_k"), ("wv", "mem_w_v")):
        t[nm + "0"], t[nm + "1"] = inp[key][l][:, 0:512], inp[key][l][:, 512:1024]
    up = inp["ffn_w_up"][l]
    for j in range(11):
        cols = np.concatenate([np.arange(ffn_feat(4 * j + r), ffn_feat(4 * j + r) + 128) for r in range(4)])
        t["wup%d" % j] = up[:, cols]
    dn = inp["ffn_w_down"][l]
    for hf in range(2):
        for g in range(3):
            nk = (8, 8, 6)[g]
            t["wdn%d%d" % (hf, g)] = dn[g * 1024:g * 1024 + nk * 128, hf * 512:(hf + 1) * 512]
    out = np.zeros((128, W_TOT), np.float32)
    for n, k, c in W_PACK:
        o = W_OFF[n][0]
        a = t[n]
        if n in ("s5b", "s5cr"):
            out[:, o:o + k * c] = a
        else:
            out[:, o:o + k * c] = _ktile(np.ascontiguousarray(a))
    return out


def pack_small(inp, l):
    sp = np.zeros((128, NSP), np.float32)
    sp[:, SP["gba"]] = inp["gla_b_a"][l]
    sp[0:8, SP["fbf"]] = inp["fox_b_f"][l]
    sp[32:40, SP["fbf"]] = inp["fox_b_f"][l]
    sp[:, SP["s5d"]:SP["s5d"] + 2] = inp["s5_d"][l].reshape(2, 128).T
    sp[:, SP["s5bg"]:SP["s5bg"] + 2] = inp["s5_b_glu"][l].reshape(2, 128).T
    idx = np.stack([ffn_feat(ci) + np.arange(128) for ci in range(NCH_FF)], 1)
    for tap in range(3):
        sp[:, SP["cw%d" % tap]:SP["cw%d" % tap] + NCH_FF] = inp["ffn_conv_w"][l][tap][idx]
    sp[:, SP["cb"]:SP["cb"] + NCH_FF] = inp["ffn_conv_b"][l][idx]
    for nm, key in (("lre", "s5_lam_re"), ("lim", "s5_lam_im"), ("ldt", "s5_log_dt")):
        sp[:, SP[nm]:SP[nm] + 8] = inp[key][l].reshape(8, 128).T
    for h in range(4):
        sp[32 * h:32 * h + 32, SP["hm"] + h] = 1.0
    tp = np.zeros((128, NTP), np.float32)
    tp[:, TP["gng"]:TP["gng"] + 256] = np.tile(inp["gla_norm_g"][l], 4)[None, :]
    tp[:, TP["fbf"]:TP["fbf"] + 8] = inp["fox_b_f"][l][None, :]
    return sp, tp


def prep_core(inp, c, cfg, shared):
    b = c % 2
    PAST, T = cfg.PAST, cfg.T
    ss = [2 * c, 2 * c + 1]
    m = {}
    m["xp"] = np.ascontiguousarray(inp["x_prompt"][b])
    m["xs"] = np.ascontiguousarray(inp["x_sample"][ss])
    m["memT"] = _ktile(np.ascontiguousarray(inp["mem_prompt"][b].T))
    m["stgla"] = np.ascontiguousarray(inp["state_gla"][:, ss].reshape(L, 2, 128, 64))
    s5 = np.stack([inp["state_s5_re"][:, ss], inp["state_s5_im"][:, ss]], 2)
    m["sts5"] = np.ascontiguousarray(s5.reshape(L, 2, 2, 8, 128).transpose(0, 1, 2, 4, 3))
    ck = inp["cache_fox_k"][:, ss]
    m["cfk"] = np.ascontiguousarray(ck.transpose(0, 1, 4, 3, 2)).reshape(L, 2, 64, 8 * PAST)
    cv = inp["cache_fox_v"][:, ss].reshape(L, 2, PAST // 128, 128, 512)
    m["cfv"] = np.ascontiguousarray(cv.transpose(0, 1, 3, 2, 4)).reshape(L, 2, 128, (PAST // 128) * 512)
    m["cflf"] = np.ascontiguousarray(inp["cache_fox_logf"][:, ss].transpose(0, 1, 3, 2))
    mk = inp["cache_mem_k"][:, ss].reshape(L, 2, 256, 1024)
    m["cmkT"] = np.ascontiguousarray(mk.transpose(0, 1, 3, 2).reshape(L, 2, 8, 128, 256).transpose(0, 1, 3, 2, 4)).reshape(L, 2, 128, 2048)
    mv = inp["cache_mem_v"][:, ss].reshape(L, 2, 2, 128, 1024)
    m["cmv"] = np.ascontiguousarray(mv.transpose(0, 1, 3, 2, 4)).reshape(L, 2, 128, 2048)
    idx = np.stack([ffn_feat(ci) + np.arange(128) for ci in range(NCH_FF)], 1)
    cs = inp["state_ffn_conv"][:, ss]
    m["cvst"] = np.ascontiguousarray(cs[:, :, :, idx].transpose(0, 1, 3, 4, 2)).reshape(L, 2, 128, NCH_FF * 2)
    m.update(shared)
    return m


def kernel(_cfg=None, **inp):
    cfg = _cfg or Cfg()
    SEQ, PAST, T = cfg.SEQ, cfg.PAST, cfg.T
    inp = {k: np.asarray(v) for k, v in inp.items()}
    shared = {}
    shared["wpk"] = np.stack([pack_weights(inp, l) for l in range(L)])
    lnp = np.zeros((1 + 3 * L, 128, 2 * D), np.float32)
    lnp[0, :, 0:D] = inp["ln_in_g"][None]
    lnp[0, :, D:] = inp["ln_in_b"][None]
    for l in range(L):
        for j, nm in enumerate(("ln1", "ln2", "ln3")):
            lnp[1 + 3 * l + j, :, 0:D] = inp[nm + "_g"][l][None]
            lnp[1 + 3 * l + j, :, D:] = inp[nm + "_b"][l][None]
    shared["lnp"] = lnp
    sps, tps = zip(*[pack_small(inp, l) for l in range(L)])
    shared["sp"] = np.stack(sps)
    shared["tp"] = np.stack(tps)
    shared["wa2"] = np.ascontiguousarray(inp["gla_w_a2"])
    shared["cst"], shared["sel"] = make_consts(cfg)
    nc = build(cfg)
    in_maps = [prep_core(inp, c, cfg, shared) for c in range(NCORES)]
    res = run_bass_kernel_spmd(nc, in_maps, core_ids=list(range(NCORES)))
    R = res.results
    B = 2
    f32 = np.float32
    yp = np.stack([R[b]["yp"] for b in range(B)])
    ys = np.concatenate([R[c]["ys"] for c in range(NCORES)], 0)

    def stackp(key, shape):
        return np.stack([R[b][key] for b in range(B)], 1).reshape(shape)

    gla_p = stackp("glap", (L, B, 4, 32, 64))
    s5p = np.stack([R[b]["s5p"] for b in range(B)], 1)
    s5p = s5p.transpose(0, 1, 2, 4, 3).reshape(L, B, 2, 16, 64)
    fk_p = stackp("fkp", (L, B, SEQ, 8, 64))
    fv_p = stackp("fvp", (L, B, SEQ, 8, 64))
    fl_p = stackp("flp", (L, B, SEQ, 8))
    idx = np.stack([ffn_feat(ci) + np.arange(128) for ci in range(NCH_FF)], 1)

    def conv_out(a):
        a = a.reshape(a.shape[:-1] + (NCH_FF, 2))
        o = np.zeros(a.shape[:-3] + (2, 2 * DFF), f32)
        o[..., :, idx] = np.moveaxis(a, -1, -3)
        return o

    cv_p = conv_out(np.stack([R[b]["cvp"] for b in range(B)], 1))
    mk_p = stackp("mkp", (L, B, 256, 4, 256))
    mv_p = stackp("mvp", (L, B, 256, 4, 256))

    def cats(key):
        return np.concatenate([R[c][key] for c in range(NCORES)], 1)

    gla_s = cats("glas").reshape(L, 16, 4, 32, 64)
    s5s = cats("s5s").transpose(0, 1, 2, 4, 3).reshape(L, 16, 2, 16, 64)
    fk_s = cats("fks").reshape(L, 16, 64, 8, 64)
    fv_s = cats("fvs").reshape(L, 16, 64, 8, 64)
    fl_s = cats("fls").reshape(L, 16, 64, 8)
    cv_s = conv_out(cats("cvs"))
    outs = (yp, ys, gla_p, s5p[:, :, 0], s5p[:, :, 1], fk_p, fv_p, fl_p, cv_p, mk_p, mv_p,
            gla_s, s5s[:, :, 0], s5s[:, :, 1], fk_s, fv_s, fl_s, cv_s)
    return tuple(np.ascontiguousarray(o, dtype=f32) for o in outs)
```

```python
import math
from contextlib import ExitStack

import numpy as np
import concourse.bass as bass
import concourse.mybir as mybir
from concourse.bass_utils import run_bass_kernel_spmd

F32 = mybir.dt.float32
BF16 = mybir.dt.bfloat16
ALU = mybir.AluOpType
AF = mybir.ActivationFunctionType
AX = mybir.AxisListType

D = 1024
L = 2
DFF = 2816
NCH_FF = 44
LN_EPS = 1e-5
ALPHA = float((2 * L) ** 0.25)
NCORES = 8


class Buf:
    __slots__ = ("name", "excl", "last_w", "readers")

    def __init__(self, name, excl=False):
        self.name = name
        self.excl = excl
        self.last_w = None
        self.readers = []


class Chan:
    __slots__ = ("sem", "count")

    def __init__(self, sem):
        self.sem = sem
        self.count = 0


class Op:
    __slots__ = ("engine", "fn", "deps", "signal", "sigidx", "chan", "chanval")

    def __init__(self, engine, fn, chan):
        self.engine = engine
        self.fn = fn
        self.deps = []
        self.signal = False
        self.sigidx = 0
        self.chan = chan
        self.chanval = 0


class Sched:
    ENGS = ("pe", "act", "dve", "pool", "sp")

    def __init__(self, nc, stack):
        self.nc = nc
        self.stack = stack
        self.ops = {e: [] for e in self.ENGS}
        self.esem = {e: stack.enter_context(nc.semaphore("es_" + e)) for e in ("pe", "act", "dve", "pool")}
        self.chans = []
        self.nops = 0

    def chan(self, name):
        c = Chan(self.stack.enter_context(self.nc.semaphore("ch_" + name)))
        self.chans.append(c)
        return c

    def add(self, eng, fn, reads=(), writes=(), chan=None):
        op = Op(eng, fn, chan)
        if chan is not None:
            chan.count += 16
            op.chanval = chan.count
        deps = {}
        for b in reads:
            if b.last_w is not None:
                deps[id(b.last_w)] = b.last_w
            if b.excl:
                for r in b.readers:
                    deps[id(r)] = r
        for b in writes:
            if b.last_w is not None:
                deps[id(b.last_w)] = b.last_w
            for r in b.readers:
                deps[id(r)] = r
        for d in deps.values():
            if d is op:
                continue
            if d.chan is None:
                if d.engine == eng and eng == "pe":
                    continue
                d.signal = True
            op.deps.append(d)
        for b in reads:
            if b.excl:
                b.last_w = op
                b.readers = []
            else:
                b.readers.append(op)
        for b in writes:
            b.last_w = op
            b.readers = []
        self.ops[eng].append(op)
        self.nops += 1
        return op

    def emit(self):
        nc = self.nc
        for e in ("pe", "act", "dve", "pool"):
            n = 0
            for op in self.ops[e]:
                if op.signal:
                    n += 1
                    op.sigidx = n
        chans = self.chans
        esem = self.esem

        def run(ename, e):
            waited = {}
            for op in self.ops[ename]:
                need = {}
                for d in op.deps:
                    if d.chan is not None:
                        sem, val = d.chan.sem, d.chanval
                    else:
                        sem, val = esem[d.engine], d.sigidx
                    k = id(sem)
                    if k not in need or need[k][1] < val:
                        need[k] = (sem, val)
                for k, (sem, val) in need.items():
                    if waited.get(k, 0) >= val:
                        continue
                    waited[k] = val
                    e.wait_ge(sem, val)
                inst = op.fn(e)
                if op.chan is not None:
                    inst.then_inc(op.chan.sem, 16)
                elif op.signal:
                    inst.then_inc(esem[ename], 1)
            if ename == "sp":
                for c in chans:
                    if c.count > 0 and waited.get(id(c.sem), 0) < c.count:
                        e.wait_ge(c.sem, c.count)

        with nc.Block() as block:
            @block.tensor
            def _(e):
                run("pe", e)

            @block.scalar
            def _(e):
                run("act", e)

            @block.vector
            def _(e):
                run("dve", e)

            @block.gpsimd
            def _(e):
                run("pool", e)

            @block.sync
            def _(e):
                run("sp", e)


OFF = dict(gq=0, gk=128, gv=256, go=512, ga=768, su=784, fq=1040, fk=1552, fv=2064, ff=2576)

W_STREAM = [("wfm", 8, 568), ("wfq", 8, 544), ("wfk", 8, 544), ("wtv", 8, 512), ("wtk", 8, 512), ("wtf", 8, 520),
            ("s5b", 8, 256), ("s5c", 8, 256), ("s5g", 2, 256),
            ("wmx0", 8, 512), ("wmx1", 8, 512), ("wq0", 8, 512), ("wq1", 8, 512), ("wo0", 8, 512), ("wo1", 8, 512)]
W_STREAM += [("wup%d" % j, 8, 512) for j in range(11)]
W_STREAM += [("wdn%d%d" % (h, g), (8, 8, 6)[g], 512) for h in range(2) for g in range(3)]
W_PACK = [t for t in W_STREAM if t[0] != "s5c"] + [("wk0", 8, 512), ("wk1", 8, 512), ("wv0", 8, 512), ("wv1", 8, 512),
                                                    ("s5cr", 8, 256)]
W_OFF = {}
_o = 0
for _n, _k, _c in W_PACK:
    W_OFF[_n] = (_o, _k, _c)
    _o += _k * _c
W_TOT = _o
SLOT_COLS = 8 * 568

SP = dict(gba=0, fbf=1, s5d=2, s5bg=4, cw0=6, cw1=50, cw2=94, cb=138, lre=182, lim=190, ldt=198, hm=206, hre=210, him=218)
NSP = 226
TP = dict(gng=0, fbf=256)
NTP = 264


def ffn_feat(ci):
    j, r = divmod(ci, 4)
    if r < 2:
        return (2 * j + r) * 128
    return DFF + (2 * j + r - 2) * 128


class Cfg:
    def __init__(self, SEQ=8192, PAST=2048, T=256):
        self.SEQ, self.PAST, self.T = SEQ, PAST, T
        self.NSEG = SEQ // T
        self.NPS = PAST // T
        assert SEQ % T == 0 and PAST % T == 0 and T % 128 == 0


def build(cfg):
    SEQ, PAST, T = cfg.SEQ, cfg.PAST, cfg.T
    NSEG, NPS = cfg.NSEG, cfg.NPS
    SB = 64
    nc = bass.Bass("TRN2", target_bir_lowering=False)
    st = ExitStack()
    sch = Sched(nc, st)
    add = sch.add

    def din(name, shape):
        return nc.dram_tensor(name, list(shape), F32, kind="ExternalInput").ap()

    def dout(name, shape):
        return nc.dram_tensor(name, list(shape), F32, kind="ExternalOutput").ap()

    def dscr(name, shape, dt=BF16):
        return nc.dram_tensor(name, list(shape), dt, kind="Internal").ap()

    def sb(name, shape, dt=F32):
        return st.enter_context(nc.sbuf_tensor("s_" + name, list(shape), dt))

    i_xp = din("xp", [SEQ, D])
    i_xs = din("xs", [2, 64, D])
    i_memT = din("memT", [128, 8 * 256])
    i_stgla = din("stgla", [L, 2, 128, 64])
    i_sts5 = din("sts5", [L, 2, 2, 128, 8])
    i_cfk = din("cfk", [L, 2, 64, 8 * PAST])
    i_cfv = din("cfv", [L, 2, 128, (PAST // 128) * 512])
    i_cflf = din("cflf", [L, 2, 8, PAST])
    i_cmkT = din("cmkT", [L, 2, 128, 8 * 256])
    i_cmv = din("cmv", [L, 2, 128, 2 * 1024])
    i_cvst = din("cvst", [L, 2, 128, NCH_FF * 2])
    i_wpk = din("wpk", [L, 128, W_TOT])
    i_lnp = din("lnp", [1 + 3 * L, 128, 2 * D])
    i_sp = din("sp", [L, 128, NSP])
    i_tp = din("tp", [L, 128, NTP])
    i_wa2 = din("wa2", [L, 16, 128])
    i_cst = din("cst", [128, 2048])
    i_sel = din("sel", [65, 2 * 8 * 68])

    o_yp = dout("yp", [SEQ, D])
    o_ys = dout("ys", [2, 64, D])
    o_glap = dout("glap", [L, 128, 64])
    o_s5p = dout("s5p", [L, 2, 128, 8])
    o_fkp = dout("fkp", [L, SEQ, 512])
    o_fvp = dout("fvp", [L, SEQ, 512])
    o_flp = dout("flp", [L, SEQ, 8])
    o_cvp = dout("cvp", [L, 128, NCH_FF * 2])
    o_mkp = dout("mkp", [L, 256, D])
    o_mvp = dout("mvp", [L, 256, D])
    o_glas = dout("glas", [L, 2, 128, 64])
    o_s5s = dout("s5s", [L, 2, 2, 128, 8])
    o_fks = dout("fks", [L, 2, 64, 512])
    o_fvs = dout("fvs", [L, 2, 64, 512])
    o_fls = dout("fls", [L, 2, 64, 8])
    o_cvs = dout("cvs", [L, 2, 128, NCH_FF * 2])

    wscr = {}
    for l in range(L):
        for n, k, c in W_PACK + [("s5c", 8, 256)]:
            if n == "s5cr":
                continue
            wscr[(l, n)] = dscr("w_%d_%s" % (l, n), [128, k * c])
    NKS = max(NSEG, NPS + 1)
    ktscr = [dscr("kts%d" % l, [NKS, 68, 8 * T]) for l in range(L)]
    vscr = [dscr("vs%d" % l, [NKS, 128, (T // 128) * 8 * 128]) for l in range(L)]
    KTB = [[Buf("kts%d_%d" % (l, s)) for s in range(NKS)] for l in range(L)]
    VSB = [[Buf("vs%d_%d" % (l, s)) for s in range(NKS)] for l in range(L)]
    WSB = {k: Buf("wscr") for k in wscr}

    pbank = [st.enter_context(nc.psum_tensor("pb%d" % i, [128, 512], F32)) for i in range(8)]
    PB = [Buf("pb%d" % i, excl=True) for i in range(8)]

    def pbf(i):
        return pbank[i][:].bitcast(BF16)

    NT = T // 128
    NCHK = T // 64
    cst = sb("cst", [128, 2048])
    cstb = sb("cstb", [128, 1024], BF16)
    selb = sb("selb", [65, 2 * 8 * 68], BF16)
    self32 = sb("self32", [65, 2 * 8 * 68])
    C_ID, C_PD, C_PU, C_TRI, C_ONE = 0, 128, 256, 384, 512
    C_CM4, C_BM, C_I64, C_CHM = 640, 896, 1152, 1280
    ident = cstb[:, 0:128]
    pdown = cstb[:, 128:256]
    pup = cstb[:, 256:384]
    trib = cstb[:, 384:512]
    onesb = cstb[:, 512:640]
    i64pad = cstb[0:64, 640:708]
    cmask4 = cst[0:64, C_CM4:C_CM4 + 256]
    blockmask = cst[:, C_BM:C_BM + 256]
    chunkmask = cst[:, C_CHM:C_CHM + T]
    ones32 = cst[:, C_ONE:C_ONE + 128]
    BC = Buf("consts")

    spt = sb("spt", [128, L, NSP])
    tpt = sb("tpt", [128, L, NTP])
    wa2b = sb("wa2b", [16, L, 128], BF16)
    wa2f = sb("wa2f", [16, L, 128])
    s5tab = sb("s5tab", [128, L, 2, 8, SB])
    s5r = sb("s5r", [128, L, 8])
    s5f = sb("s5f", [128, L, 4, 8])
    BP = BC

    xres = sb("xres", [128, NT, D])
    BXR = Buf("xres")
    xbt = sb("xbt", [128, 2, D], BF16)
    BXB = [Buf("xbt%d" % i) for i in range(2)]
    xTp = sb("xTp", [128, 8, T + 2], BF16)
    xT = xTp[:, :, 2:T + 2]
    BXT = Buf("xT")
    xhc = sb("xhc", [128, L, 8, 2], BF16)
    BXH = [Buf("xhc%d" % l) for l in range(L)]
    NWS = 3
    wring = [sb("wring%d" % i, [128, SLOT_COLS], BF16) for i in range(NWS)]
    BW = [Buf("wring%d" % i) for i in range(NWS)]
    CW = [sch.chan("w%d" % i) for i in range(NWS)]
    lnt = sb("lnt", [128, 2 * D])
    BLN = Buf("lnt")
    CLN = sch.chan("ln")
    s5bt = sb("s5bt", [128, 2, 8 * T])
    BS5H = [Buf("s5bt%d" % i) for i in range(2)]
    stg = [s5bt[:, i, 0:1024] for i in range(2)]
    BSTG = BS5H
    CSTG = [sch.chan("stg%d" % i) for i in range(2)]
    stgb = [sb("stgb%d" % i, [128, 1024], BF16) for i in range(2)]
    BSTGB = [Buf("stgb%d" % i) for i in range(2)]
    CSTGB = [sch.chan("stgb%d" % i) for i in range(2)]
    mkT = sb("mkT", [128, L, 8, 256], BF16)
    mvb = sb("mvb", [128, L, 2, D], BF16)
    BMK = [Buf("mk%d" % l) for l in range(L)]
    gS = sb("gS", [128, L, 256])
    gSb = sb("gSb", [128, L, 256], BF16)
    BGS = [Buf("gS%d" % l) for l in range(L)]
    CGS = [sch.chan("gS%d" % l) for l in range(L)]
    CGSL = [sch.chan("gSl%d" % l) for l in range(L)]
    s5x = sb("s5x", [128, L, 2, 8])
    BS5X = [Buf("s5x%d" % l) for l in range(L)]
    s5e = sb("s5e", [128, L, 2, 8])
    CS5X = sch.chan("s5x")
    CS5XL = sch.chan("s5xl")
    halo = sb("halo", [128, L, NCH_FF, 2])
    BHALO = [[Buf("halo%d_%d" % (l, ci)) for ci in range(NCH_FF)] for l in range(L)]
    CHALO = [sch.chan("halo%d" % l) for l in range(L)]
    CHALOL = [sch.chan("halol%d" % l) for l in range(L)]
    cfc = sb("cfc", [40, L])
    BCFC = [Buf("cfc%d" % l) for l in range(L)]

    gqk = sb("gqk", [128, 2, T])
    BGQK = Buf("gqk")
    suT = sb("suT", [128, 2, T])
    BSU = Buf("suT")
    sub = sb("sub", [128, 2, T], BF16)
    gaT = sb("gaT", [16, T], BF16)
    BGA = Buf("gaT")
    ffT = sb("ffT", [40, T])
    BFF = Buf("ffT")
    qTa = sb("qTa", [68, 8, T], BF16)
    BQT = Buf("qTa")
    kTa = sb("kTa", [68, 8, T], BF16)
    BKT = Buf("kTa")
    CKT = sch.chan("kTa")
    vtb = sb("vtb", [64, NCHK, 256], BF16)
    gtk = sb("gtk", [64, NCHK, 256])
    BVT = Buf("vtok")
    ost = [sb("ost%d" % i, [128, 520]) for i in range(2)]
    BOST = [Buf("ost%d" % i) for i in range(2)]
    COST = [sch.chan("ost%d" % i) for i in range(2)]
    vown = sb("vown", [128, NT, 8, 128], BF16)
    BVO = Buf("vown")
    CVO = sch.chan("vown")
    NTMP = 8
    tmp = [sb("tmp%d" % i, [128, T + 2]) for i in range(NTMP)]
    BT = [Buf("tmp%d" % i) for i in range(NTMP)]
    tmpb = [sb("tmpb%d" % i, [128, 4, T], BF16) for i in range(2)]
    BTB = [Buf("tmpb%d" % i) for i in range(2)]
    s5t = sb("s5t", [128, 10, 8])
    BS5T = Buf("s5t")
    sm = sb("sm", [128, 64])
    smt = sb("smt", [128, 32])
    BSMT = [Buf("smt%d" % i) for i in range(2)]
    BSM = Buf("sm")
    augs = sb("augs", [65, T], BF16)
    BAUG = Buf("augs")
    hi2 = sb("hi2", [40, T], BF16)
    big = sb("big", [128, 22, T], BF16)
    BBIG = Buf("big")
    mixT = sb("mixT", [128, 8, T], BF16)
    BMIX = Buf("mixT")
    qmT = sb("qmT", [128, 8, T], BF16)
    BQM = Buf("qmT")
    ktr = [sb("ktr%d" % i, [68, 8, T], BF16) for i in range(2)]
    BKR = [Buf("ktr%d" % i) for i in range(2)]
    CKR = [sch.chan("ktr%d" % i) for i in range(2)]
    vtr = [sb("vtr%d" % i, [128, NT, 8, 128], BF16) for i in range(2)]
    BVR = [Buf("vtr%d" % i) for i in range(2)]
    CVR = [sch.chan("vtr%d" % i) for i in range(2)]
    ptb = [sb("ptb%d" % i, [128, 512], BF16) for i in range(4)]
    BPT = [Buf("ptb%d" % i) for i in range(4)]
    rlb = sb("rlb", [128, 4, T], BF16)
    BRL = Buf("rlb")
    rl1 = sb("rl1", [1, T], BF16)
    BRL1 = Buf("rl1")
    CXIN = sch.chan("xin")
    CXIN2 = sch.chan("xin2")
    COUT = sch.chan("out")
    CPAR = sch.chan("par")

    def MM(out, lhsT, rhs, start, stop, R, W, skip=False):
        if skip:
            add("pe", lambda e: e.matmul(out, lhsT, rhs, start=start, stop=stop, skip_group_check=True), reads=R, writes=W)
        else:
            add("pe", lambda e: e.matmul(out, lhsT, rhs, start=start, stop=stop), reads=R, writes=W)

    def TR(out, in_, idn, R, W):
        add("pe", lambda e: e.transpose(out, in_, idn), reads=R + [BC], writes=W)

    def ACT(out, in_, func, R, W, bias=None, scale=1.0, accum=None):
        kw = {}
        if bias is not None:
            kw["bias"] = bias
        if accum is not None:
            kw["accum_out"] = accum
        add("act", lambda e: e.activation(out=out, in_=in_, func=func, scale=scale, **kw), reads=R, writes=W)

    def ACOPY(out, in_, R, W):
        add("act", lambda e: e.copy(out=out, in_=in_), reads=R, writes=W)

    def VCOPY(out, in_, R, W, eng="dve"):
        add(eng, lambda e: e.tensor_copy(out=out, in_=in_), reads=R, writes=W)

    def TT(out, a, b, op, R, W, eng="dve"):
        add(eng, lambda e: e.tensor_tensor(out=out, in0=a, in1=b, op=op), reads=R, writes=W)

    def TS(out, a, s1, s2, op0, op1, R, W, eng="dve"):
        if op1 is None:
            add(eng, lambda e: e.tensor_scalar(out=out, in0=a, scalar1=s1, scalar2=None, op0=op0), reads=R, writes=W)
        else:
            add(eng, lambda e: e.tensor_scalar(out=out, in0=a, scalar1=s1, scalar2=s2, op0=op0, op1=op1), reads=R, writes=W)

    def STT(out, a, s, b, op0, op1, R, W):
        add("dve", lambda e: e.scalar_tensor_tensor(out=out, in0=a, scalar=s, in1=b, op0=op0, op1=op1), reads=R, writes=W)

    def SCAN(out, d0, d1, init, R, W):
        add("dve", lambda e: e.tensor_tensor_scan(out=out, data0=d0, data1=d1, initial=init, op0=ALU.mult, op1=ALU.add),
            reads=R, writes=W)

    def RECIP(out, in_, R, W):
        add("dve", lambda e: e.reciprocal(out=out, in_=in_), reads=R, writes=W)

    def MSET(ap, v, W, eng="pool"):
        add(eng, lambda e: e.memset(ap, v), writes=W)

    def LOAD(out, in_, W, chan, R=()):
        add("sp", lambda e: e.dma_start(out=out, in_=in_), reads=list(R), writes=W, chan=chan)

    def STORE(out, in_, R, chan, W=()):
        add("pool", lambda e: e.dma_start(out=out, in_=in_), reads=R, writes=list(W), chan=chan)

    wstate = dict(next_load=0, next_use=0)
    wseq = []

    def plan_stream(nstreams):
        for _ in range(nstreams):
            for l in range(L):
                for n, k, c in W_STREAM:
                    wseq.append((l, n, k, c))

    def issue_wload():
        i = wstate["next_load"]
        if i >= len(wseq):
            return
        l, n, k, c = wseq[i]
        slot = i % NWS
        LOAD(wring[slot][:, 0:k * c], wscr[(l, n)], [BW[slot]], CW[slot], R=[WSB[(l, n)]])
        wstate["next_load"] = i + 1

    def getw(l, name):
        i = wstate["next_use"]
        assert wseq[i][0] == l and wseq[i][1] == name, (wseq[i], l, name)
        while wstate["next_load"] <= i:
            issue_wload()
        k, c = wseq[i][2], wseq[i][3]
        slot = i % NWS
        wstate["next_use"] = i + 1
        return wring[slot][:, 0:k * c].rearrange("p (k c) -> p k c", k=k), BW[slot]

    def wdone():
        while wstate["next_load"] < min(len(wseq), wstate["next_use"] + NWS - 1):
            issue_wload()

    LOAD(cst[:], i_cst, [BC], CPAR)
    LOAD(self32[:], i_sel, [BC], CPAR)
    LOAD(spt[:], i_sp.rearrange("l p n -> p l n"), [BP], CPAR)
    LOAD(tpt[:], i_tp.rearrange("l p n -> p l n"), [BP], CPAR)
    LOAD(wa2f[:], i_wa2.rearrange("l p n -> p l n"), [BP], CPAR)
    VCOPY(cstb[:, 0:640], cst[:, 0:640], [BC], [BC])
    VCOPY(cstb[0:64, 640:708], cst[0:64, C_I64:C_I64 + 68], [BC], [BC])
    VCOPY(selb[:], self32[:], [BC], [BC])
    VCOPY(wa2b[:], wa2f[:], [BP], [BP])
    MSET(rlb[:], 0.0, [BRL])
    MSET(augs[:], 0.0, [BAUG])
    MSET(augs[64:65, :], 1.0, [BAUG])
    MSET(hi2[:], 0.0, [BAUG])
    MSET(ffT[:], 0.0, [BFF])
    for i in range(2):
        MSET(vtr[i][:], 1.0, [BVR[i]])
    MSET(vown[:], 1.0, [BVO])

    castn = [0]

    def cast_pieces(src_ap, dst_ap, ncols, dstbuf):
        c0 = 0
        while c0 < ncols:
            cw = min(1024, ncols - c0)
            i = castn[0] % 2
            castn[0] += 1
            LOAD(stg[i][:, 0:cw], src_ap[:, c0:c0 + cw], [BSTG[i]], CSTG[i])
            eng = ("dve", "act", "pool")[castn[0] % 3]
            if eng == "act":
                ACOPY(stgb[i][:, 0:cw], stg[i][:, 0:cw], [BSTG[i]], [BSTGB[i]])
            else:
                VCOPY(stgb[i][:, 0:cw], stg[i][:, 0:cw], [BSTG[i]], [BSTGB[i]], eng=eng)
            STORE(dst_ap[:, c0:c0 + cw], stgb[i][:, 0:cw], [BSTGB[i]], CSTGB[i], W=[dstbuf])
            c0 += cw

    for l in range(L):
        for n, k, c in W_PACK:
            if n == "s5cr":
                continue
            o = W_OFF[n][0]
            cast_pieces(i_wpk[l][:, o:o + k * c], wscr[(l, n)], k * c, WSB[(l, n)])

    def s5_prep(l):
        lre = spt[:, l, SP["lre"]:SP["lre"] + 8]
        lim = spt[:, l, SP["lim"]:SP["lim"] + 8]
        ldt = spt[:, l, SP["ldt"]:SP["ldt"] + 8]
        t = lambda i: s5t[:, i, :]
        R = [BP]
        Wt = [BS5T]
        ACT(t(0), ldt, AF.Exp, R, Wt)
        TT(t(1), lre, t(0), ALU.mult, R + Wt, Wt)
        TT(t(2), lim, t(0), ALU.mult, R + Wt, Wt)
        ACT(s5r[:, l, :], t(1), AF.Exp, Wt, [BP])
        ACT(t(3), t(2), AF.Sin, Wt, Wt, scale=1.0 / 16)
        TS(t(4), t(2), 1.0 / 16, math.pi / 2, ALU.mult, ALU.add, Wt, Wt)
        ACT(t(4), t(4), AF.Sin, Wt, Wt)
        for _ in range(4):
            TT(t(5), t(4), t(4), ALU.mult, Wt, Wt)
            TT(t(6), t(3), t(3), ALU.mult, Wt, Wt)
            TT(t(3), t(3), t(4), ALU.mult, Wt, Wt)
            TS(t(3), t(3), 2.0, None, ALU.mult, None, Wt, Wt)
            TT(t(4), t(5), t(6), ALU.subtract, Wt, Wt)
        cosT = s5tab[:, l, 0]
        sinT = s5tab[:, l, 1]
        MSET(cosT[:, :, 0:1], 1.0, [BP])
        MSET(sinT[:, :, 0:1], 0.0, [BP])
        VCOPY(t(5), t(4), Wt, Wt)
        VCOPY(t(6), t(3), Wt, Wt)
        n = 1
        while n < SB:
            dc = t(5).unsqueeze(2).to_broadcast([128, 8, n])
            ds = t(6).unsqueeze(2).to_broadcast([128, 8, n])
            a = tmp[1][:, 0:8 * n].rearrange("p (c j) -> p c j", c=8)
            b = tmp[2][:, 0:8 * n].rearrange("p (c j) -> p c j", c=8)
            RW = Wt + [BP, BT[1], BT[2]]
            TT(a, cosT[:, :, 0:n], dc, ALU.mult, RW, [BT[1]])
            TT(b, sinT[:, :, 0:n], ds, ALU.mult, RW, [BT[2]])
            TT(cosT[:, :, n:2 * n], a, b, ALU.subtract, RW, [BP])
            TT(a, cosT[:, :, 0:n], ds, ALU.mult, RW, [BT[1]])
            TT(b, sinT[:, :, 0:n], dc, ALU.mult, RW, [BT[2]])
            TT(sinT[:, :, n:2 * n], a, b, ALU.add, RW, [BP])
            TT(t(7), t(5), t(5), ALU.mult, Wt, Wt)
            TT(t(8), t(6), t(6), ALU.mult, Wt, Wt)
            TT(t(6), t(5), t(6), ALU.mult, Wt, Wt)
            TS(t(6), t(6), 2.0, None, ALU.mult, None, Wt, Wt)
            TT(t(5), t(7), t(8), ALU.subtract, Wt, Wt)
            n *= 2
        TS(spt[:, l, SP["hre"]:SP["hre"] + 2], spt[:, l, SP["s5bg"]:SP["s5bg"] + 2], -1.0, None, ALU.mult, None, [BP], [BP])
        VCOPY(s5e[:, l, 0, :], t(5), Wt, [BP])
        VCOPY(s5e[:, l, 1, :], t(6), Wt, [BP])
        mag = s5r[:, l, :]
        u = lambda i: tmp[4][:, 8 * i:8 * i + 8]
        W4 = [BT[4]]
        RR = R + Wt + W4
        TT(u(0), mag, t(4), ALU.mult, RR, W4)
        TS(u(0), u(0), -1.0, None, ALU.add, None, RR, W4)
        TT(u(1), mag, t(3), ALU.mult, RR, W4)
        TT(u(2), lre, lre, ALU.mult, RR, W4)
        TT(u(3), lim, lim, ALU.mult, RR, W4)
        TT(u(2), u(2), u(3), ALU.add, RR, W4)
        RECIP(u(2), u(2), RR, W4)
        TT(u(3), u(0), lre, ALU.mult, RR, W4)
        TT(u(4), u(1), lim, ALU.mult, RR, W4)
        TT(u(3), u(3), u(4), ALU.add, RR, W4)
        TT(s5f[:, l, 0, :], u(3), u(2), ALU.mult, RR, [BP])
        TT(u(3), u(1), lre, ALU.mult, RR, W4)
        TT(u(4), u(0), lim, ALU.mult, RR, W4)
        TT(u(3), u(3), u(4), ALU.subtract, RR, W4)
        TT(s5f[:, l, 1, :], u(3), u(2), ALU.mult, RR, [BP])
        TT(u(5), s5f[:, l, 0, :], s5f[:, l, 0, :], ALU.mult, RR, W4)
        TT(u(6), s5f[:, l, 1, :], s5f[:, l, 1, :], ALU.mult, RR, W4)
        TT(u(5), u(5), u(6), ALU.add, RR, W4)
        RECIP(u(5), u(5), RR, W4)
        TT(s5f[:, l, 2, :], s5f[:, l, 0, :], u(5), ALU.mult, RR, [BP])
        TT(s5f[:, l, 3, :], s5f[:, l, 1, :], u(5), ALU.mult, RR, [BP])
        TS(s5f[:, l, 3, :], s5f[:, l, 3, :], -1.0, None, ALU.mult, None, RR, [BP])
        o = W_OFF["s5cr"][0]
        LOAD(stg[0][:, :], i_wpk[l][:, o:o + 1024], [BSTG[0]], CSTG[0])
        LOAD(stg[1][:, :], i_wpk[l][:, o + 1024:o + 2048], [BSTG[1]], CSTG[1])
        for c in range(8):
            sg = stg[c // 4]
            bs = BSTG[c // 4]
            cre = sg[:, (c % 4) * 256:(c % 4) * 256 + 128]
            cim = sg[:, (c % 4) * 256 + 128:(c % 4) * 256 + 256]
            fre = s5f[:, l, 0, c:c + 1]
            fim = s5f[:, l, 1, c:c + 1]
            a = tmp[5][:, 0:128]
            b = tmp[6][:, 0:128]
            TS(a, cre, fre, None, ALU.mult, None, [bs, BP], [BT[5]])
            TS(b, cim, fim, None, ALU.mult, None, [bs, BP], [BT[6]])
            TT(stgb[0][:, c * 128:c * 128 + 128], a, b, ALU.subtract, [BT[5], BT[6]], [BSTGB[0]])
            TS(a, cre, fim, None, ALU.mult, None, [bs, BP], [BT[5]])
            TS(b, cim, fre, None, ALU.mult, None, [bs, BP], [BT[6]])
            TT(a, a, b, ALU.add, [BT[5], BT[6]], [BT[5]])
            TS(stgb[1][:, c * 128:c * 128 + 128], a, -1.0, None, ALU.mult, None, [BT[5]], [BSTGB[1]])
        dst = wscr[(l, "s5c")].rearrange("p (c x) -> p c x", c=8)
        STORE(dst[:, :, 0:128], stgb[0][:, 0:1024].rearrange("p (c x) -> p c x", c=8), [BSTGB[0]], CSTGB[0], W=[WSB[(l, "s5c")]])
        STORE(dst[:, :, 128:256], stgb[1][:, 0:1024].rearrange("p (c x) -> p c x", c=8), [BSTGB[1]], CSTGB[1], W=[WSB[(l, "s5c")]])

    for l in range(L):
        s5_prep(l)

    def mem_prep_prompt(l):
        mt = big[:, 0:8, 0:256]
        for i in range(2):
            LOAD(stg[i][:, :], i_memT[:, i * 1024:(i + 1) * 1024], [BSTG[i]], CSTG[i])
            VCOPY(mt[:, 4 * i:4 * i + 4, :], stg[i][:, :].rearrange("p (k m) -> p k m", k=4), [BSTG[i]], [BBIG])
        for which, name, od in ((0, "wk", o_mkp), (1, "wv", o_mvp)):
            for hf in range(2):
                i = hf % 2
                LOAD(wring[i][:, 0:4096], wscr[(l, "%s%d" % (name, hf))], [BW[i]], CW[i], R=[WSB[(l, "%s%d" % (name, hf))]])
                w = wring[i][:, 0:4096].rearrange("p (k c) -> p k c", k=8)
                if which == 0:
                    for c4 in range(4):
                        cidx = hf * 4 + c4
                        pb = 2 + (cidx % 2)
                        for kc in range(8):
                            MM(pbank[pb][:, 0:256], w[:, kc, c4 * 128:(c4 + 1) * 128], mt[:, kc, :], kc == 0, kc == 7,
                               [BW[i], BBIG], [PB[pb]])
                        ACOPY(mkT[:, l, cidx, :], pbank[pb][:, 0:256], [PB[pb]], [BMK[l]])
                for m2 in range(2):
                    pb = 4 + m2
                    for kc in range(8):
                        MM(pbank[pb][:, :], mt[:, kc, m2 * 128:(m2 + 1) * 128], w[:, kc, :], kc == 0, kc == 7,
                           [BW[i], BBIG], [PB[pb]])
                    oi = (hf * 2 + m2) % 2
                    ACOPY(ost[oi][:, 0:512], pbank[pb][:, :], [PB[pb]], [BOST[oi]])
                    if which == 1:
                        VCOPY(mvb[:, l, m2, hf * 512:(hf + 1) * 512], ost[oi][:, 0:512], [BOST[oi]], [BMK[l]])
                    STORE(od[l][m2 * 128:(m2 + 1) * 128, hf * 512:(hf + 1) * 512], ost[oi][:, 0:512], [BOST[oi]], COST[oi])

    for l in range(L):
        mem_prep_prompt(l)

    lnstate = dict(cur=None)

    def ln_load(idx):
        LOAD(lnt[:], i_lnp[idx], [BLN], CLN)

    def layer_norm(S, tiles):
        P = S["P"]
        xs_ = {tt: xres[0:P, tt, :] for tt in tiles}
        o = {tt: 16 * (tt % 2) for tt in tiles}
        BS = {tt: BSMT[tt % 2] for tt in tiles}
        st6 = {tt: smt[0:P, o[tt]:o[tt] + 12].rearrange("p (a b) -> p a b", a=2) for tt in tiles}
        for tt in tiles:
            x = xs_[tt]
            add("dve", lambda e, x=x, a=st6[tt]: e.bn_stats(out=a[:, 0, :], in_=x[:, 0:512]), reads=[BXR], writes=[BS[tt]])
            add("dve", lambda e, x=x, a=st6[tt]: e.bn_stats(out=a[:, 1, :], in_=x[:, 512:1024]), reads=[BXR], writes=[BS[tt]])
        for tt in tiles:
            mv = smt[0:P, o[tt] + 12:o[tt] + 14]
            add("dve", lambda e, mv=mv, a=st6[tt]: e.bn_aggr(out=mv, in_=a), reads=[BS[tt]], writes=[BS[tt]])
        for tt in tiles:
            rs = smt[0:P, o[tt] + 14:o[tt] + 15]
            ACT(rs, smt[0:P, o[tt] + 13:o[tt] + 14], AF.Ln, [BS[tt]], [BS[tt]], bias=LN_EPS)
        for tt in tiles:
            rs = smt[0:P, o[tt] + 14:o[tt] + 15]
            ACT(rs, rs, AF.Exp, [BS[tt]], [BS[tt]], scale=-0.5)
        for tt in tiles:
            rs = smt[0:P, o[tt] + 14:o[tt] + 15]
            STT(xs_[tt], xs_[tt], smt[0:P, o[tt] + 12:o[tt] + 13], lnt[0:P, 0:D], ALU.subtract, ALU.mult, [BXR, BS[tt], BLN], [BXR])
            STT(xs_[tt], xs_[tt], rs, lnt[0:P, D:2 * D], ALU.mult, ALU.add, [BXR, BS[tt], BLN], [BXR])
        for tt in tiles:
            ACOPY(xbt[0:P, tt % 2, :], xs_[tt], [BXR], [BXB[tt % 2]])
        for tt in tiles:
            bk = 4 + tt % 2
            pv = pbf(bk)
            for kc in range(8):
                TR(pv[:, kc * 128:kc * 128 + P], xbt[0:P, tt % 2, kc * 128:(kc + 1) * 128], ident[0:P, 0:P], [BXB[tt % 2]], [PB[bk]])
        for tt in tiles:
            bk = 4 + tt % 2
            VCOPY(xT[:, :, tt * 128:tt * 128 + P], pbf(bk)[:, :].rearrange("p (k t) -> p k t", k=8)[:, :, 0:P], [PB[bk]], [BXT])

    def resid_from_psum(S, tt, half, pb):
        P = S["P"]
        x = xres[0:P, tt, half * 512:(half + 1) * 512]
        STT(x, x, ALPHA, pbank[pb][0:P, :], ALU.mult, ALU.add, [BXR, PB[pb]], [BXR])

    def fm_chunk(w, bw, c0, M, pb, Tn):
        for kc in range(8):
            MM(pbank[pb][0:M, 0:Tn], w[:, kc, c0:c0 + M], xT[:, kc, 0:Tn], kc == 0, kc == 7, [bw, BXT], [PB[pb]])

    def logsig_inplace(ap, R, W):
        ACT(ap, ap, AF.Exp, R, W, scale=-1.0)
        ACT(ap, ap, AF.Ln, R, W, bias=1.0)
        TS(ap, ap, -1.0, None, ALU.mult, None, R, W)

    def layer(S, l, seg):
        Tn, P, nt, nchk = S["T"], S["P"], S["nt"], S["nchk"]
        samp = S["samp"]
        sidx = S["sidx"]
        tok0 = seg * Tn

        w, bw = getw(l, "wfm")
        for j in range(2):
            pb = 2 + j
            fm_chunk(w, bw, j * 128, 128, pb, Tn)
            ACOPY(gqk[:, j, 0:Tn], pbank[pb][:, 0:Tn], [PB[pb]], [BGQK])
        for j in range(2):
            pb = 2 + j
            fm_chunk(w, bw, 256 + j * 128, 128, pb, Tn)
            ACOPY(suT[:, j, 0:Tn], pbank[pb][:, 0:Tn], [PB[pb]], [BSU])
        VCOPY(sub[:, :, 0:Tn], suT[:, :, 0:Tn], [BSU], [BSU])
        fm_chunk(w, bw, 512, 16, 2, Tn)
        ACOPY(gaT[:, 0:Tn], pbank[2][0:16, 0:Tn], [PB[2]], [BGA])
        fm_chunk(w, bw, 528, 40, 3, Tn)
        lf = tmp[7][0:40, 0:Tn]
        TS(lf, pbank[3][0:40, 0:Tn], spt[0:40, l, SP["fbf"]:SP["fbf"] + 1], None, ALU.add, None, [PB[3], BP], [BT[7]])
        logsig_inplace(lf, [BT[7]], [BT[7]])
        SCAN(ffT[:, 0:Tn], ones32[0:40, 0:1].to_broadcast([40, Tn]), lf, cfc[:, l:l + 1], [BT[7], BC, BCFC[l]], [BFF])
        VCOPY(cfc[:, l:l + 1], ffT[:, Tn - 1:Tn], [BFF], [BCFC[l]])
        VCOPY(augs[0:8, 0:Tn], ffT[0:8, 0:Tn], [BFF], [BAUG])
        VCOPY(hi2[32:40, 0:Tn], ffT[32:40, 0:Tn], [BFF], [BAUG])
        TT(augs[32:40, 0:Tn], ffT[32:40, 0:Tn], hi2[32:40, 0:Tn], ALU.subtract, [BFF, BAUG], [BAUG])
        wdone()
        w, bw = getw(l, "wfq")
        for h in range(8):
            pb = 2 + (h % 4)
            for kc in range(8):
                MM(pbank[pb][0:68, 0:Tn], w[:, kc, h * 68:(h + 1) * 68], xT[:, kc, 0:Tn], kc == 0, False, [bw, BXT], [PB[pb]])
            MM(pbank[pb][0:68, 0:Tn], selb[:, (8 + h) * 68:(9 + h) * 68], augs[:, 0:Tn], False, True, [BC, BAUG], [PB[pb]])
            ACT(qTa[:, h, 0:Tn], pbank[pb][0:68, 0:Tn], AF.Copy, [PB[pb]], [BQT], scale=0.125)
        wdone()
        w, bw = getw(l, "wfk")
        for h in range(8):
            pb = 2 + (h % 4)
            for kc in range(8):
                MM(pbank[pb][0:68, 0:Tn], w[:, kc, h * 68:(h + 1) * 68], xT[:, kc, 0:Tn], kc == 0, False, [bw, BXT], [PB[pb]])
            MM(pbank[pb][0:68, 0:Tn], selb[:, h * 68:(h + 1) * 68], augs[:, 0:Tn], False, True, [BC, BAUG], [PB[pb]])
            ACOPY(kTa[:, h, 0:Tn], pbank[pb][0:68, 0:Tn], [PB[pb]], [BKT])
        wdone()
        kslot = S["kslot"](seg)
        STORE(ktscr[l][kslot].rearrange("p (h t) -> p h t", h=8)[:, :, 0:Tn], kTa[:, :, 0:Tn], [BKT], CKT, W=[KTB[l][kslot]])
        w, bw = getw(l, "wtv")
        for c in range(nchk):
            pb = 2 + (c % 4)
            for kc in range(8):
                MM(pbank[pb][0:64, :], xT[:, kc, c * 64:(c + 1) * 64], w[:, kc, :], kc == 0, kc == 7, [bw, BXT], [PB[pb]])
            ACOPY(vtb[:, c, :], pbank[pb][0:64, 0:256], [PB[pb]], [BVT])
            VCOPY(gtk[:, c, :], pbank[pb][0:64, 256:512], [PB[pb]], [BVT])
        wdone()
        w, bw = getw(l, "wtk")
        for tt in range(nt):
            pb = 2 + (tt % 2)
            for kc in range(8):
                MM(pbank[pb][0:P, :], xT[:, kc, tt * 128:tt * 128 + P], w[:, kc, :], kc == 0, kc == 7, [bw, BXT], [PB[pb]])
            oi = tt % 2
            ACOPY(ost[oi][0:P, 0:512], pbank[pb][0:P, :], [PB[pb]], [BOST[oi]])
            STORE(S["o_fk"](l)[tok0 + tt * 128:tok0 + tt * 128 + P, :], ost[oi][0:P, 0:512], [BOST[oi]], COST[oi])
        wdone()
        w, bw = getw(l, "wtf")
        for tt in range(nt):
            for (c0, cn, pb) in ((0, 512, 2), (512, 8, 3)):
                for kc in range(8):
                    MM(pbank[pb][0:P, 0:cn], xT[:, kc, tt * 128:tt * 128 + P], w[:, kc, c0:c0 + cn], kc == 0, kc == 7,
                       [bw, BXT], [PB[pb]])
            oi = tt % 2
            ACOPY(ost[oi][0:P, 0:512], pbank[2][0:P, :], [PB[2]], [BOST[oi]])
            TT(ost[oi][0:P, 512:520], pbank[3][0:P, 0:8], tpt[0:P, l, TP["fbf"]:TP["fbf"] + 8], ALU.add, [PB[3], BP], [BOST[oi]])
            logsig_inplace(ost[oi][0:P, 512:520], [BOST[oi]], [BOST[oi]])
            v4 = ost[oi][0:P, 0:512].rearrange("p (a b d) -> p a b d", a=4, b=2)
            vo = vown[0:P, tt].rearrange("p (a b) d -> p a b d", a=4)
            VCOPY(vo[:, :, 0, 0:64], v4[:, :, 0, :], [BOST[oi]], [BVO])
            VCOPY(vo[:, :, 1, 64:128], v4[:, :, 1, :], [BOST[oi]], [BVO])
            STORE(S["o_fv"](l)[tok0 + tt * 128:tok0 + tt * 128 + P, :], ost[oi][0:P, 0:512], [BOST[oi]], COST[oi])
            STORE(S["o_fl"](l)[tok0 + tt * 128:tok0 + tt * 128 + P, :], ost[oi][0:P, 512:520], [BOST[oi]], COST[oi])
        wdone()
        STORE(vscr[l][kslot].rearrange("p (t x) -> p t x", t=NT)[0:P, 0:nt, :],
              vown[0:P, 0:nt].rearrange("p t h d -> p t (h d)"), [BVO], CVO, W=[VSB[l][kslot]])

        ksegs = S["ksegs"](seg)

        def kv_load(si):
            slot, nk, diag = ksegs[si]
            r = si % 2
            LOAD(ktr[r][:, :, 0:nk], ktscr[l][slot].rearrange("p (h t) -> p h t", h=8)[:, :, 0:nk], [BKR[r]], CKR[r], R=[KTB[l][slot]])
            nkb = (nk + 127) // 128
            kp = min(128, nk)
            LOAD(vtr[r][0:kp, 0:nkb].rearrange("p t h d -> p t (h d)"), vscr[l][slot].rearrange("p (t x) -> p t x", t=NT)[0:kp, 0:nkb, :],
                 [BVR[r]], CVR[r], R=[VSB[l][slot]])

        kv_load(0)
        if len(ksegs) > 1:
            kv_load(1)

        q = gqk[:, 0, 0:Tn]
        k = gqk[:, 1, 0:Tn]
        cum = tmp[0][:, 0:Tn]
        la = tmp[7][:, 0:Tn]
        MM(pbank[2][:, 0:Tn], wa2b[:, l, :], gaT[:, 0:Tn], True, True, [BP, BGA], [PB[2]])
        TS(la, pbank[2][:, 0:Tn], spt[:, l, SP["gba"]:SP["gba"] + 1], None, ALU.add, None, [PB[2], BP], [BT[7]])
        logsig_inplace(la, [BT[7]], [BT[7]])
        TS(la, la, 1.0 / 16.0, None, ALU.mult, None, [BT[7]], [BT[7]])
        SCAN(cum, chunkmask[:, 0:Tn], la, 0.0, [BT[7], BC], [BT[0]])
        cend = tmp[1][:, 0:nchk]
        VCOPY(cend, cum.rearrange("p (c j) -> p c j", j=64)[:, :, 63], [BT[0]], [BT[1]])
        dec = tmp[1][:, 8:8 + nchk]
        ACT(dec, cend, AF.Exp, [BT[1]], [BT[1]])
        e1 = tmp[2][:, 0:Tn]
        ACT(e1, cum, AF.Exp, [BT[0]], [BT[2]])
        qd = tmpb[0][:, 0, 0:Tn]
        STT(qd, q, 32.0 ** -0.5, e1, ALU.mult, ALU.mult, [BGQK, BT[2]], [BTB[0]])
        ACT(e1, cum, AF.Exp, [BT[0]], [BT[2]], scale=-1.0)
        kinv = tmp[3][:, 0:Tn]
        TT(kinv, k, e1, ALU.mult, [BGQK, BT[2]], [BT[3]])
        kih = tmpb[1]
        for h in range(4):
            TS(kih[:, h, 0:Tn], kinv, spt[:, l, SP["hm"] + h:SP["hm"] + h + 1], None, ALU.mult, None, [BT[3], BP], [BTB[1]])
        kend = tmpb[0][:, 1, 0:Tn]
        for c in range(nchk):
            ACT(e1[:, c * 64:(c + 1) * 64], cum[:, c * 64:(c + 1) * 64], AF.Exp, [BT[0], BT[1]], [BT[2]],
                bias=cend[:, c:c + 1], scale=-1.0)
        TT(kend, k, e1, ALU.mult, [BGQK, BT[2]], [BTB[0]])
        Sf = gS[:, l, :]
        Sb = gSb[:, l, :]
        for c in range(nchk):
            cs = slice(c * 64, (c + 1) * 64)
            for h in range(4):
                MM(pbank[2][0:64, h * 64:(h + 1) * 64], kih[:, h, cs], qd[:, cs], True, True, [BTB[0], BTB[1]], [PB[2]])
            scm = tmpb[0][0:64, 2, 0:256]
            TT(scm, pbank[2][0:64, 0:256], cmask4, ALU.mult, [PB[2], BC], [BTB[0]])
            MM(pbank[3][0:64, 0:256], qd[:, cs], Sb, True, False, [BTB[0], BGS[l]], [PB[3]])
            for h in range(4):
                MM(pbank[3][0:64, h * 64:(h + 1) * 64], scm[:, h * 64:(h + 1) * 64], vtb[:, c, h * 64:(h + 1) * 64], False, h == 3,
                   [BTB[0], BVT], [PB[3]])
            TR(pbf(4)[0:64, 0:128], kend[:, cs], ident, [BTB[0]], [PB[4]])
            keT = tmpb[0][0:64, 3, 0:128]
            VCOPY(keT, pbf(4)[0:64, 0:128], [PB[4]], [BTB[0]])
            MM(pbank[5][:, 0:256], keT, vtb[:, c, :], True, True, [BTB[0], BVT], [PB[5]])
            dsm = tmp[4][:, 0:256]
            TT(dsm, pbank[5][:, 0:256], blockmask, ALU.mult, [PB[5], BC], [BT[4]])
            STT(Sf, Sf, dec[:, c:c + 1], dsm, ALU.mult, ALU.add, [BGS[l], BT[1], BT[4]], [BGS[l]])
            ACOPY(Sb, Sf, [BGS[l]], [BGS[l]])
            osq = tmp[5][0:64, 0:256]
            ACT(osq, pbank[3][0:64, 0:256], AF.Square, [PB[3]], [BT[5]])
            ss = sm[0:64, 16:20]
            add("dve", lambda e, ss=ss, osq=osq: e.reduce_sum(out=ss, in_=osq.rearrange("p (h d) -> p h d", h=4), axis=AX.X),
                reads=[BT[5]], writes=[BSM])
            TS(ss, ss, 1.0 / 64.0, LN_EPS, ALU.mult, ALU.add, [BSM], [BSM])
            ACT(ss, ss, AF.Ln, [BSM], [BSM])
            ACT(ss, ss, AF.Exp, [BSM], [BSM], scale=-0.5)
            on = tmp[5][0:64, 0:256]
            TT(on.rearrange("p (h d) -> p h d", h=4), pbank[3][0:64, 0:256].rearrange("p (h d) -> p h d", h=4),
               ss.unsqueeze(2).to_broadcast([64, 4, 64]), ALU.mult, [PB[3], BSM], [BT[5]])
            TT(on, on, tpt[0:64, l, TP["gng"]:TP["gng"] + 256], ALU.mult, [BT[5], BP], [BT[5]])
            sg = tmp[6][0:64, 0:256]
            ACT(sg, gtk[:, c, :], AF.Exp, [BVT], [BT[6]], scale=-1.0)
            ACT(sg, sg, AF.Ln, [BT[6]], [BT[6]], bias=1.0)
            ACT(sg, sg, AF.Exp, [BT[6]], [BT[6]], scale=-1.0)
            TT(sg, sg, gtk[:, c, :], ALU.mult, [BT[6], BVT], [BT[6]])
            og = tmpb[0][0:64, 2, 0:256]
            TT(og, on, sg, ALU.mult, [BT[5], BT[6]], [BTB[0]])
            for j in range(2):
                TR(pbf(4)[:, 256 + j * 64:256 + (j + 1) * 64], og[:, j * 128:(j + 1) * 128], ident[0:64, 0:64], [BTB[0]], [PB[4]])
            VCOPY(mixT[:, 0:2, cs], pbf(4)[:, 256:384].rearrange("p (j t) -> p j t", j=2), [PB[4]], [BMIX])

        w, bw = getw(l, "s5b")
        xre = big[:, 0:8, 0:Tn]
        xim = big[:, 8:16, 0:Tn]
        cosT = s5tab[:, l, 0]
        sinT = s5tab[:, l, 1]
        nsb = Tn // SB
        v3 = lambda ap: ap.rearrange("p (s j) -> p s j", j=SB)
        btre = s5bt[:, 0, :].rearrange("p (m t) -> p m t", m=8)
        btim = s5bt[:, 1, :].rearrange("p (m t) -> p m t", m=8)
        t0_, t1_, t2_, t3_ = (tmp[i][:, 0:Tn] for i in range(4))
        for m in range(8):
            kc = m // 4
            pa, pi_ = (2, 3) if m % 2 == 0 else (4, 5)
            MM(pbank[pa][:, 0:Tn], w[:, m, 0:128], sub[:, kc, 0:Tn], True, True, [bw, BSU], [PB[pa]])
            MM(pbank[pi_][:, 0:Tn], w[:, m, 128:256], sub[:, kc, 0:Tn], True, True, [bw, BSU], [PB[pi_]])
            cb = cosT[:, m, :].unsqueeze(1).to_broadcast([128, nsb, SB])
            sbb = sinT[:, m, :].unsqueeze(1).to_broadcast([128, nsb, SB])
            pre = pbank[pa][:, 0:Tn]
            pim = pbank[pi_][:, 0:Tn]
            TT(v3(t0_), v3(pre), cb, ALU.mult, [PB[pa], BP], [BT[0]])
            TT(v3(t1_), v3(pim), sbb, ALU.mult, [PB[pi_], BP], [BT[1]])
            TT(v3(t2_), v3(pim), cb, ALU.mult, [PB[pi_], BP], [BT[2]])
            TT(v3(t3_), v3(pre), sbb, ALU.mult, [PB[pa], BP], [BT[3]])
            TT(btre[:, m, 0:Tn], t0_, t1_, ALU.add, [BT[0], BT[1]], [BS5H[0]])
            TT(btim[:, m, 0:Tn], t2_, t3_, ALU.subtract, [BT[2], BT[3]], [BS5H[1]])
        wdone()
        for b in range(4):
            MSET(pbank[b][:, :], 0.0, [PB[b]], eng="dve")
        items = []
        firsts = {}
        for si, (slot, nk, diag) in enumerate(ksegs):
            nkb = (nk + 127) // 128
            firsts[len(items)] = si
            for h in range(8):
                if diag or nkb * Tn > 512:
                    for kb in range(nkb):
                        items.append((si, h, [kb]))
                else:
                    items.append((si, h, list(range(nkb))))
        DPIPE = 3

        def att_s12(it, n):
            si, h, kbs = it
            slot, nk, diag = ksegs[si]
            r = si % 2
            pb = 4 + n % 4
            pt = ptb[n % 4]
            bpt = BPT[n % 4]
            for jj, kb in enumerate(kbs):
                kn = min(128, nk - kb * 128)
                q0 = kb * 128 if diag else 0
                MM(pbank[pb][0:kn, jj * Tn + q0:jj * Tn + Tn], ktr[r][:, h, kb * 128:kb * 128 + kn], qTa[:, h, q0:Tn], True, True,
                   [BKR[r], BQT], [PB[pb]])
            if diag:
                ACT(pt[0:kn, q0:Tn], pbank[pb][0:kn, q0:Tn], AF.Exp, [PB[pb]], [bpt])
                qe = min(Tn, q0 + 128)
                TT(pt[0:kn, q0:qe], pt[0:kn, q0:qe], trib[0:kn, 0:qe - q0], ALU.mult, [bpt, BC], [bpt])
            else:
                ACT(pt[0:kn, 0:len(kbs) * Tn], pbank[pb][0:kn, 0:len(kbs) * Tn], AF.Exp, [PB[pb]], [bpt])

        def att_s3(it, n):
            si, h, kbs = it
            slot, nk, diag = ksegs[si]
            nkb = (nk + 127) // 128
            r = si % 2
            pt = ptb[n % 4]
            bpt = BPT[n % 4]
            ab = h // 2
            acc = pbank[ab][:, (h % 2) * 256:(h % 2) * 256 + Tn]
            for jj, kb in enumerate(kbs):
                kn = min(128, nk - kb * 128)
                q0 = kb * 128 if diag else 0
                last = (si == len(ksegs) - 1) and (kb == nkb - 1)
                MM(acc[:, q0:Tn], vtr[r][0:kn, kb, h, :], pt[0:kn, jj * Tn + q0:jj * Tn + Tn], False, last, [BVR[r], bpt], [PB[ab]], skip=True)

        cS = s5e[:, l, 0, :]
        sS = s5e[:, l, 1, :]
        for sbi in range(nsb):
            js = slice(sbi * SB, (sbi + 1) * SB)
            if sbi == 0:
                wl_re, wl_im, Rw = s5x[:, l, 0, :], s5x[:, l, 1, :], [BS5X[l]]
            else:
                wl_re, wl_im, Rw = btre[:, :, sbi * SB - 1], btim[:, :, sbi * SB - 1], [BS5H[0], BS5H[1]]
            ini = sm[:, 32:48].rearrange("p (a c) -> p a c", a=2)
            ta = sm[:, 48:56]
            tb = sm[:, 56:64]
            TT(ini[:, 0, :], cS, wl_re, ALU.mult, [BP] + Rw, [BSM])
            TT(ta, sS, wl_im, ALU.mult, [BP] + Rw, [BSM])
            TT(ini[:, 0, :], ini[:, 0, :], ta, ALU.subtract, [BSM], [BSM])
            TT(ini[:, 1, :], cS, wl_im, ALU.mult, [BP] + Rw, [BSM])
            TT(tb, sS, wl_re, ALU.mult, [BP] + Rw, [BSM])
            TT(ini[:, 1, :], ini[:, 1, :], tb, ALU.add, [BSM], [BSM])
            for m in range(8):
                rbc = s5r[:, l, m:m + 1].to_broadcast([128, SB])
                SCAN(btre[:, m, js], rbc, btre[:, m, js], ini[:, 0, m:m + 1], [BP, BS5H[0], BSM], [BS5H[0]])
                SCAN(btim[:, m, js], rbc, btim[:, m, js], ini[:, 1, m:m + 1], [BP, BS5H[1], BSM], [BS5H[1]])
        VCOPY(s5x[:, l, 0, :], btre[:, :, Tn - 1], [BS5H[0]], [BS5X[l]])
        VCOPY(s5x[:, l, 1, :], btim[:, :, Tn - 1], [BS5H[1]], [BS5X[l]])
        for m in range(8):
            cb = cosT[:, m, :].unsqueeze(1).to_broadcast([128, nsb, SB])
            sbb = sinT[:, m, :].unsqueeze(1).to_broadcast([128, nsb, SB])
            wre = btre[:, m, 0:Tn]
            wim = btim[:, m, 0:Tn]
            TT(v3(t0_), v3(wre), cb, ALU.mult, [BS5H[0], BP], [BT[0]])
            TT(v3(t1_), v3(wim), sbb, ALU.mult, [BS5H[1], BP], [BT[1]])
            TT(v3(t2_), v3(wim), cb, ALU.mult, [BS5H[1], BP], [BT[2]])
            TT(v3(t3_), v3(wre), sbb, ALU.mult, [BS5H[0], BP], [BT[3]])
            TT(xre[:, m, :], t0_, t1_, ALU.subtract, [BT[0], BT[1]], [BBIG])
            TT(xim[:, m, :], t2_, t3_, ALU.add, [BT[2], BT[3]], [BBIG])
        for i in range(len(items) + DPIPE):
            if i < len(items):
                att_s12(items[i], i)
            k_ = i - DPIPE
            if k_ >= 0:
                att_s3(items[k_], k_)
                if k_ in firsts and firsts[k_] >= 1 and firsts[k_] + 1 < len(ksegs):
                    kv_load(firsts[k_] + 1)
        for g4 in range(2):
            hs = list(range(4 * g4, 4 * g4 + 4))
            info = {}
            for hh, h in enumerate(hs):
                ab = h // 2
                acc = pbank[ab][:, (h % 2) * 256:(h % 2) * 256 + Tn]
                lo, hi_ = (0, 64) if h % 2 == 0 else (64, 128)
                l0 = 64 if h % 2 == 0 else 0
                info[h] = (ab, acc, lo, hi_, l0)
                ACT(tmp[7][l0:l0 + 1, 0:Tn], acc[l0:l0 + 1, :], AF.Ln, [PB[ab]], [BT[7]])
                ACT(rlb[l0:l0 + 1, hh, 0:Tn], tmp[7][l0:l0 + 1, 0:Tn], AF.Exp, [BT[7]], [BRL], scale=-1.0)
            for hh, h in enumerate(hs):
                MM(pbank[4 + hh][:, 0:Tn], pdown if h % 2 == 0 else pup, rlb[:, hh, 0:Tn], True, True, [BC, BRL], [PB[4 + hh]])
            for hh, h in enumerate(hs):
                ab, acc, lo, hi_, l0 = info[h]
                ACOPY(tmp[hh][lo:hi_, 0:Tn], pbank[4 + hh][lo:hi_, 0:Tn], [PB[4 + hh]], [BT[hh]])
            for hh, h in enumerate(hs):
                ab, acc, lo, hi_, l0 = info[h]
                TT(mixT[lo:hi_, 4 + h // 2, 0:Tn], acc[lo:hi_, :], tmp[hh][lo:hi_, 0:Tn], ALU.mult, [PB[ab], BT[hh]], [BMIX])

        w, bw = getw(l, "s5c")
        wg, bwg = getw(l, "s5g")
        zT = tmpb[1][:, 0:2, 0:Tn]
        zf = [tmp[0][:, 0:Tn], tmp[1][:, 0:Tn]]
        for oc in range(2):
            pb = 2 + oc
            n = 0
            for c in range(4 * oc, 4 * oc + 4):
                MM(pbank[pb][:, 0:Tn], w[:, c, 0:128], xre[:, c, :], n == 0, False, [bw, BBIG], [PB[pb]])
                MM(pbank[pb][:, 0:Tn], w[:, c, 128:256], xim[:, c, :], False, n == 3, [bw, BBIG], [PB[pb]])
                n += 1
            y = tmp[2][:, 0:Tn]
            STT(y, suT[:, oc, 0:Tn], spt[:, l, SP["s5d"] + oc:SP["s5d"] + oc + 1], pbank[pb][:, 0:Tn], ALU.mult, ALU.add,
                [BSU, BP, PB[pb]], [BT[2]])
            gelu(y, zf[oc], [BT[2]], [BT[oc]], 3, Tn)
            ACOPY(zT[:, oc, :], zf[oc], [BT[oc]], [BTB[1]])
        for oc in range(2):
            pb = 2 + oc
            for kc in range(2):
                MM(pbank[pb][:, 0:Tn], wg[:, kc, oc * 128:(oc + 1) * 128], zT[:, kc, :], kc == 0, kc == 1, [bwg, BTB[1]], [PB[pb]])
            gt = tmp[2][:, 0:Tn]
            ACT(gt, pbank[pb][:, 0:Tn], AF.Exp, [PB[pb], BP], [BT[2]], bias=spt[:, l, SP["hre"] + oc:SP["hre"] + oc + 1], scale=-1.0)
            ACT(gt, gt, AF.Ln, [BT[2]], [BT[2]], bias=1.0)
            ACT(gt, gt, AF.Exp, [BT[2]], [BT[2]], scale=-1.0)
            TT(mixT[:, 2 + oc, 0:Tn], zf[oc], gt, ALU.mult, [BT[oc], BT[2]], [BMIX])
        wdone()

        ln_load(1 + 3 * l)
        for hf in range(2):
            w, bw = getw(l, "wmx%d" % hf)
            for tt in range(nt):
                pb = 2 + tt % 2
                for kc in range(8):
                    MM(pbank[pb][0:P, :], mixT[:, kc, tt * 128:tt * 128 + P], w[:, kc, :], kc == 0, kc == 7, [bw, BMIX], [PB[pb]])
                resid_from_psum(S, tt, hf, pb)
            wdone()
        layer_norm(S, list(range(nt)))

        mk = S["mkT"](l)
        mv_ = S["mv"](l)
        bmk = S["bmk"](l)
        for hf in range(2):
            w, bw = getw(l, "wq%d" % hf)
            for c4 in range(4):
                pb = 2 + c4 % 2
                fm_chunk(w, bw, c4 * 128, 128, pb, Tn)
                ACOPY(qmT[:, hf * 4 + c4, 0:Tn], pbank[pb][:, 0:Tn], [PB[pb]], [BQM])
            wdone()
        def mem_a(h):
            pt = ptb[h]
            for mc in range(2):
                pb = 2 + (h % 2) * 2 + mc
                for dc in range(2):
                    MM(pbank[pb][:, 0:Tn], mk[:, 2 * h + dc, mc * 128:(mc + 1) * 128], qmT[:, 2 * h + dc, 0:Tn], dc == 0, dc == 1,
                       [bmk, BQM], [PB[pb]])
                ACT(pt[:, mc * 256:mc * 256 + Tn], pbank[pb][:, 0:Tn], AF.Exp, [PB[pb]], [BPT[h]], scale=1.0 / 16.0)

        def mem_b(h):
            pt = ptb[h]
            for mc in range(2):
                MM(pbank[6][0:1, 0:Tn], onesb[:, 0:1], pt[:, mc * 256:mc * 256 + Tn], mc == 0, mc == 1, [BC, BPT[h]], [PB[6]])
            ACT(tmp[7][0:1, 0:Tn], pbank[6][0:1, 0:Tn], AF.Ln, [PB[6]], [BT[7]])
            ACT(rl1[0:1, 0:Tn], tmp[7][0:1, 0:Tn], AF.Exp, [BT[7]], [BRL1], scale=-1.0)
            MM(pbank[6][:, 256:256 + Tn], onesb[0:1, :], rl1[0:1, 0:Tn], True, True, [BC, BRL1], [PB[6]])
            rl = tmp[h % 2][:, 0:Tn]
            ACOPY(rl, pbank[6][:, 256:256 + Tn], [PB[6]], [BT[h % 2]])
            for dc in range(2):
                pb = (7, 0)[dc]
                for mc in range(2):
                    MM(pbank[pb][:, 0:Tn], mv_[:, mc, (2 * h + dc) * 128:(2 * h + dc + 1) * 128], pt[:, mc * 256:mc * 256 + Tn],
                       mc == 0, mc == 1, [bmk, BPT[h]], [PB[pb]])
                TT(mixT[:, 2 * h + dc, 0:Tn], pbank[pb][:, 0:Tn], rl, ALU.mult, [PB[pb], BT[h % 2]], [BMIX])

        mem_a(0)
        for h in range(1, 4):
            mem_a(h)
            mem_b(h - 1)
        mem_b(3)
        ln_load(2 + 3 * l)
        for hf in range(2):
            w, bw = getw(l, "wo%d" % hf)
            for tt in range(nt):
                pb = 2 + tt % 2
                for kc in range(8):
                    MM(pbank[pb][0:P, :], mixT[:, kc, tt * 128:tt * 128 + P], w[:, kc, :], kc == 0, kc == 7, [bw, BMIX], [PB[pb]])
                resid_from_psum(S, tt, hf, pb)
            wdone()
        layer_norm(S, list(range(nt)))

        hl = halo[:, l]
        if not samp:
            VCOPY(xTp[:, :, 0:2], xhc[:, l], [BXH[l]], [BXT])
        for j in range(11):
            w, bw = getw(l, "wup%d" % j)
            bk = [(j % 2) * 4 + r4 for r4 in range(4)]
            ys = [tmp[(j % 2) * 4 + r4][:, 0:Tn] for r4 in range(4)]
            bys = [BT[(j % 2) * 4 + r4] for r4 in range(4)]
            cwp = lambda tap, ci: spt[:, l, SP["cw%d" % tap] + ci:SP["cw%d" % tap] + ci + 1]
            c0 = 0 if samp else 2
            for r4 in range(4):
                pb = bk[r4]
                for kc in range(8):
                    if samp:
                        MM(pbank[pb][:, 0:Tn], w[:, kc, r4 * 128:(r4 + 1) * 128], xT[:, kc, 0:Tn], kc == 0, kc == 7, [bw, BXT], [PB[pb]])
                    else:
                        MM(pbank[pb][:, 0:Tn + 2], w[:, kc, r4 * 128:(r4 + 1) * 128], xTp[:, kc, 0:Tn + 2], kc == 0, kc == 7,
                           [bw, BXT], [PB[pb]])
            for r4 in range(4):
                ci = 4 * j + r4
                ACT(ys[r4], pbank[bk[r4]][:, c0:c0 + Tn], AF.Identity, [PB[bk[r4]], BP], [bys[r4]],
                    bias=spt[:, l, SP["cb"] + ci:SP["cb"] + ci + 1], scale=cwp(2, ci))
            for r4 in range(4):
                ci = 4 * j + r4
                pb = bk[r4]
                ps = pbank[pb]
                y = ys[r4]
                by = bys[r4]
                bh = BHALO[l][ci]
                if samp:
                    STT(y[:, 1:Tn], ps[:, 0:Tn - 1], cwp(1, ci), y[:, 1:Tn], ALU.mult, ALU.add, [PB[pb], BP, by], [by])
                    STT(y[:, 0:1], hl[:, ci, 1:2], cwp(1, ci), y[:, 0:1], ALU.mult, ALU.add, [bh, BP, by], [by])
                    STT(y[:, 2:Tn], ps[:, 0:Tn - 2], cwp(0, ci), y[:, 2:Tn], ALU.mult, ALU.add, [PB[pb], BP, by], [by])
                    STT(y[:, 0:2], hl[:, ci, 0:2], cwp(0, ci), y[:, 0:2], ALU.mult, ALU.add, [bh, BP, by], [by])
                    ACOPY(hl[:, ci, :], ps[:, Tn - 2:Tn], [PB[pb]], [bh])
                else:
                    STT(y, ps[:, 1:Tn + 1], cwp(1, ci), y, ALU.mult, ALU.add, [PB[pb], BP, by], [by])
                    STT(y, ps[:, 0:Tn], cwp(0, ci), y, ALU.mult, ALU.add, [PB[pb], BP, by], [by])
                    if seg == NSEG - 1:
                        ACOPY(hl[:, ci, :], ps[:, Tn:Tn + 2], [PB[pb]], [bh])
                if r4 < 2:
                    ACT(y, y, AF.Gelu_apprx_tanh, [by], [by])
            for r4 in range(2):
                TT(big[:, 2 * j + r4, 0:Tn], ys[r4], ys[2 + r4], ALU.mult, [bys[r4], bys[2 + r4]], [BBIG])
            wdone()
        if not samp:
            VCOPY(xhc[:, l], xTp[:, :, Tn:Tn + 2], [BXT], [BXH[l]])
        ln_load(3 + 3 * l)
        for hf in range(2):
            for g in range(3):
                w, bw = getw(l, "wdn%d%d" % (hf, g))
                nk = (8, 8, 6)[g]
                for tt in range(nt):
                    pb = 2 + tt
                    for kk in range(nk):
                        kc = g * 8 + kk
                        MM(pbank[pb][0:P, :], big[:, kc, tt * 128:tt * 128 + P], w[:, kk, :], kc == 0, kc == 21, [bw, BBIG], [PB[pb]])
                wdone()
            for tt in range(nt):
                resid_from_psum(S, tt, hf, 2 + tt)
        layer_norm(S, list(range(nt)))

    def gelu(y, out, R, W, ti, Tn):
        t = tmp[ti][:, 0:Tn]
        bt = BT[ti]
        ACT(t, y, AF.Square, R, [bt])
        TS(t, t, 0.044715, 1.0, ALU.mult, ALU.add, [bt], [bt])
        TT(t, t, y, ALU.mult, [bt] + R, [bt])
        ACT(t, t, AF.Sigmoid, [bt], [bt], scale=1.5957691216057308)
        TT(out, y, t, ALU.mult, R + [bt], W)

    def cmul(o_re, o_im, a_re, a_im, b_re, b_im, conj_a, R, W):
        ta = sm[:, 48:56]
        tb = sm[:, 56:64]
        TT(ta, a_re, b_re, ALU.mult, R, [BSM])
        TT(tb, a_im, b_im, ALU.mult, R, [BSM])
        TT(o_re, ta, tb, ALU.add if conj_a else ALU.subtract, [BSM], W)
        TT(ta, a_re, b_im, ALU.mult, R, [BSM])
        TT(tb, a_im, b_re, ALU.mult, R, [BSM])
        TT(o_im, ta, tb, ALU.subtract if conj_a else ALU.add, [BSM], W)

    def s5_state_out(l, dst):
        xq = sm[:, 0:16].rearrange("p (a c) -> p a c", a=2)
        so = tmp[0][:, 0:16].rearrange("p (a c) -> p a c", a=2)
        cmul(xq[:, 0, :], xq[:, 1, :], s5tab[:, l, 0, :, SB - 1], s5tab[:, l, 1, :, SB - 1], s5x[:, l, 0, :], s5x[:, l, 1, :], False,
             [BP, BS5X[l], BSM], [BSM])
        cmul(so[:, 0, :], so[:, 1, :], s5f[:, l, 0, :], s5f[:, l, 1, :], xq[:, 0, :], xq[:, 1, :], False, [BP, BSM], [BT[0]])
        STORE(dst.rearrange("a p c -> p a c"), so, [BT[0]], CS5X)

    def s5_state_in(l, src):
        h0 = tmp[0][:, 0:16].rearrange("p (a c) -> p a c", a=2)
        LOAD(h0, src.rearrange("a p c -> p a c"), [BT[0]], CS5XL)
        xq = sm[:, 0:16].rearrange("p (a c) -> p a c", a=2)
        cmul(xq[:, 0, :], xq[:, 1, :], s5f[:, l, 2, :], s5f[:, l, 3, :], h0[:, 0, :], h0[:, 1, :], False, [BP, BT[0], BSM], [BSM])
        cmul(s5x[:, l, 0, :], s5x[:, l, 1, :], s5tab[:, l, 0, :, SB - 1], s5tab[:, l, 1, :, SB - 1], xq[:, 0, :], xq[:, 1, :], True,
             [BP, BSM], [BS5X[l]])

    plan_stream(NSEG + 2)

    SP_ = dict(T=T, P=128, nt=NT, nchk=NCHK, samp=False, sidx=0,
               kslot=lambda seg: seg,
               ksegs=lambda seg: [(s, T, False) for s in range(seg)] + [(seg, T, True)],
               o_fk=lambda l: o_fkp[l], o_fv=lambda l: o_fvp[l], o_fl=lambda l: o_flp[l],
               mkT=lambda l: mkT[:, l], mv=lambda l: mvb[:, l], bmk=lambda l: BMK[l])
    for l in range(L):
        MSET(gS[:, l, :], 0.0, [BGS[l]])
        MSET(gSb[:, l, :], 0.0, [BGS[l]])
        MSET(s5x[:, l], 0.0, [BS5X[l]])
        MSET(halo[:, l], 0.0, BHALO[l])
        MSET(cfc[:, l:l + 1], 0.0, [BCFC[l]])
        MSET(xhc[:, l], 0.0, [BXH[l]])
    for seg in range(NSEG):
        ln_load(0)
        for tt in range(NT):
            LOAD(xres[:, tt, :], i_xp[seg * T + tt * 128:seg * T + (tt + 1) * 128, :], [BXR], CXIN)
        layer_norm(SP_, list(range(NT)))
        for l in range(L):
            layer(SP_, l, seg)
        for tt in range(NT):
            STORE(o_yp[seg * T + tt * 128:seg * T + (tt + 1) * 128, :], xres[:, tt, :], [BXR], COUT)
    for l in range(L):
        for h in range(4):
            STORE(o_glap[l][32 * h:32 * h + 32, :], gS[32 * h:32 * h + 32, l, 64 * h:64 * h + 64], [BGS[l]], CGS[l])
        s5_state_out(l, o_s5p[l])
        STORE(o_cvp[l], halo[:, l].rearrange("p c j -> p (c j)"), BHALO[l], CHALO[l])

    smk = mkT
    for s in range(2):
        SS = dict(T=64, P=64, nt=1, nchk=1, samp=True, sidx=s,
                  kslot=lambda seg: NPS,
                  ksegs=lambda seg: [(ps, T, False) for ps in range(NPS)] + [(NPS, 64, True)],
                  o_fk=lambda l, s=s: o_fks[l][s], o_fv=lambda l, s=s: o_fvs[l][s], o_fl=lambda l, s=s: o_fls[l][s],
                  mkT=lambda l: mkT[:, l], mv=lambda l: mvb[:, l], bmk=lambda l: BMK[l])
        for l in range(L):
            MSET(gS[:, l, :], 0.0, [BGS[l]])
            for h in range(4):
                LOAD(gS[32 * h:32 * h + 32, l, 64 * h:64 * h + 64], i_stgla[l][s][32 * h:32 * h + 32, :], [BGS[l]], CGSL[l])
            ACOPY(gSb[:, l, :], gS[:, l, :], [BGS[l]], [BGS[l]])
            s5_state_in(l, i_sts5[l][s])
            LOAD(halo[:, l].rearrange("p c j -> p (c j)"), i_cvst[l][s], BHALO[l], CHALOL[l])
            for i in range(2):
                LOAD(stg[i][:, :], i_cmkT[l][s][:, i * 1024:(i + 1) * 1024], [BSTG[i]], CSTG[i])
                VCOPY(mkT[:, l, 4 * i:4 * i + 4, :], stg[i][:, :].rearrange("p (k m) -> p k m", k=4), [BSTG[i]], [BMK[l]])
            for i in range(2):
                LOAD(stg[i][:, :], i_cmv[l][s][:, i * 1024:(i + 1) * 1024], [BSTG[i]], CSTG[i])
                VCOPY(mvb[:, l, i, :], stg[i][:, :], [BSTG[i]], [BMK[l]])
            cfp = tmp[6][0:40, 0:T]
            BXIN = BT[6]
            MSET(cfc[:, l:l + 1], 0.0, [BCFC[l]])
            for ps in range(NPS):
                MSET(tmp[6][0:40, 0:T], 0.0, [BXIN])
                LOAD(tmp[6][0:8, 0:T], i_cflf[l][s][:, ps * T:(ps + 1) * T], [BXIN], CXIN2)
                LOAD(tmp[6][32:40, 0:T], i_cflf[l][s][:, ps * T:(ps + 1) * T], [BXIN], CXIN2)
                cfo = tmp[7][0:40, 0:T]
                SCAN(cfo, ones32[0:40, 0:1].to_broadcast([40, T]), cfp, cfc[:, l:l + 1], [BXIN, BC, BCFC[l]], [BT[7]])
                VCOPY(cfc[:, l:l + 1], cfo[:, T - 1:T], [BT[7]], [BCFC[l]])
                VCOPY(augs[0:8, 0:T], cfo[0:8, :], [BT[7]], [BAUG])
                VCOPY(hi2[32:40, 0:T], cfo[32:40, :], [BT[7]], [BAUG])
                TT(augs[32:40, 0:T], cfo[32:40, :], hi2[32:40, 0:T], ALU.subtract, [BT[7], BAUG], [BAUG])
                kc_src = i_cfk[l][s].rearrange("p (h t) -> p h t", h=8)
                for h in range(8):
                    i = h % 2
                    LOAD(stg[i][0:64, 0:T], kc_src[:, h, ps * T:(ps + 1) * T], [BSTG[i]], CSTG[i])
                    VCOPY(stgb[i][0:64, 0:T], stg[i][0:64, 0:T], [BSTG[i]], [BSTGB[i]])
                    pb = 2 + i
                    MM(pbank[pb][0:68, 0:T], i64pad, stgb[i][0:64, 0:T], True, False, [BC, BSTGB[i]], [PB[pb]])
                    MM(pbank[pb][0:68, 0:T], selb[:, h * 68:(h + 1) * 68], augs[:, 0:T], False, True, [BC, BAUG], [PB[pb]])
                    ACOPY(kTa[:, h, 0:T], pbank[pb][0:68, 0:T], [PB[pb]], [BKT])
                STORE(ktscr[l][ps].rearrange("p (h t) -> p h t", h=8), kTa[:, :, 0:T], [BKT], CKT, W=[KTB[l][ps]])
                vsrc = i_cfv[l][s].rearrange("p (b x) -> p b x", x=512)
                for tt in range(NT):
                    i = tt % 2
                    LOAD(stg[i][:, 0:512], vsrc[:, ps * NT + tt, :], [BSTG[i]], CSTG[i])
                    v4 = stg[i][:, 0:512].rearrange("p (a b d) -> p a b d", a=4, b=2)
                    vo = vown[:, tt].rearrange("p (a b) d -> p a b d", a=4)
                    VCOPY(vo[:, :, 0, 0:64], v4[:, :, 0, :], [BSTG[i]], [BVO])
                    VCOPY(vo[:, :, 1, 64:128], v4[:, :, 1, :], [BSTG[i]], [BVO])
                STORE(vscr[l][ps].rearrange("p (t x) -> p t x", t=NT), vown[:, 0:NT].rearrange("p t h d -> p t (h d)"),
                      [BVO], CVO, W=[VSB[l][ps]])
        ln_load(0)
        LOAD(xres[0:64, 0, :], i_xs[s], [BXR], CXIN)
        layer_norm(SS, [0])
        for l in range(L):
            layer(SS, l, 0)
        STORE(o_ys[s], xres[0:64, 0, :], [BXR], COUT)
        for l in range(L):
            for h in range(4):
                STORE(o_glas[l][s][32 * h:32 * h + 32, :], gS[32 * h:32 * h + 32, l, 64 * h:64 * h + 64], [BGS[l]], CGS[l])
            s5_state_out(l, o_s5s[l][s])
            STORE(o_cvs[l][s], halo[:, l].rearrange("p c j -> p (c j)"), BHALO[l], CHALO[l])

    with nc.allow_low_precision(reason="bf16 matmul operands, fp32 accumulation"):
        sch.emit()
    st.close()
    return nc


def _ktile(w, nk=None):
    K, C = w.shape
    nk = K // 128
    return np.ascontiguousarray(w.reshape(nk, 128, C).transpose(1, 0, 2)).reshape(128, nk * C)


def make_consts(cfg):
    T = cfg.T
    c = np.zeros((128, 2048), np.float32)
    c[:, 0:128] = np.eye(128)
    c[64, 128:128 + 64] = 1.0
    c[0, 256 + 64:256 + 128] = 1.0
    k = np.arange(128)[:, None]
    q = np.arange(128)[None, :]
    c[:, 384:512] = (k <= q)
    c[:, 512:640] = 1.0
    s = np.arange(64)[:, None]
    l_ = np.arange(64)[None, :]
    c[0:64, 640:896] = np.tile((s <= l_).astype(np.float32), (1, 4))
    for h in range(4):
        c[32 * h:32 * h + 32, 896 + 64 * h:896 + 64 * h + 64] = 1.0
    c[0:64, 1152:1152 + 64] = np.eye(64)
    cm = np.ones(T, np.float32)
    cm[0::64] = 0.0
    c[:, 1280:1280 + T] = cm[None, :]
    sel = np.zeros((65, 2, 8, 68), np.float32)
    for h in range(8):
        sel[h, 0, h, 64] = -1.0
        sel[32 + h, 0, h, 65] = -1.0
        sel[64, 0, h, 66] = 1.0
        sel[64, 0, h, 67] = 1.0
        sel[64, 1, h, 64] = 8.0
        sel[64, 1, h, 65] = 8.0
        sel[h, 1, h, 66] = 8.0
        sel[32 + h, 1, h, 67] = 8.0
    return c, sel.reshape(65, 2 * 8 * 68)


def pack_weights(inp, l):
    w_in = inp["w_in"][l]
    Z = np.zeros
    t = {}
    ff = w_in[:, OFF["ff"]:OFF["ff"] + 8]
    ff40 = Z((D, 40), np.float32)
    ff40[:, 0:8] = ff
    ff40[:, 32:40] = ff
    t["wfm"] = np.concatenate([w_in[:, 0:256], w_in[:, OFF["su"]:OFF["su"] + 256], w_in[:, OFF["ga"]:OFF["ga"] + 16], ff40], 1)
    for nm, o in (("wfq", OFF["fq"]), ("wfk", OFF["fk"])):
        a = Z((D, 8, 68), np.float32)
        a[:, :, 0:64] = w_in[:, o:o + 512].reshape(D, 8, 64)
        t[nm] = a.reshape(D, 544)
    t["wtv"] = w_in[:, OFF["gv"]:OFF["gv"] + 512]
    t["wtk"] = w_in[:, OFF["fk"]:OFF["fk"] + 512]
    t["wtf"] = np.concatenate([w_in[:, OFF["fv"]:OFF["fv"] + 512], ff], 1)
    bre, bim = inp["s5_b_re"][l], inp["s5_b_im"][l]
    sb_ = Z((128, 8, 256), np.float32)
    for m in range(8):
        for gg in range(2):
            g = 2 * m + gg
            r0 = (g % 8) * 16
            sb_[r0:r0 + 16, m, gg * 64:(gg + 1) * 64] = bre[g].T
            sb_[r0:r0 + 16, m, 128 + gg * 64:128 + (gg + 1) * 64] = bim[g].T
    t["s5b"] = sb_.reshape(128, 8 * 256)
    cre, cim = inp["s5_c_re"][l], inp["s5_c_im"][l]
    sc_ = Z((128, 8, 256), np.float32)
    for c in range(8):
        for gg in range(2):
            g = 2 * c + gg
            c0 = (g % 8) * 16
            sc_[gg * 64:(gg + 1) * 64, c, c0:c0 + 16] = cre[g].T
            sc_[gg * 64:(gg + 1) * 64, c, 128 + c0:128 + c0 + 16] = cim[g].T
    t["s5cr"] = sc_.reshape(128, 8 * 256)
    t["s5g"] = inp["s5_w_glu"][l]
    mx = inp["w_mix_out"][l]
    t["wmx0"], t["wmx1"] = mx[:, 0:512], mx[:, 512:1024]
    for nm, key in (("wq", "mem_w_q"), ("wo", "mem_w_o"), ("wk", "mem_w_k"), ("wv", "mem_w_v")):
        t[nm + "0"], t[nm + "1"] = inp[key][l][:, 0:512], inp[key][l][:, 512:1024]
    up = inp["ffn_w_up"][l]
    for j in range(11):
        cols = np.concatenate([np.arange(ffn_feat(4 * j + r), ffn_feat(4 * j + r) + 128) for r in range(4)])
        t["wup%d" % j] = up[:, cols]
    dn = inp["ffn_w_down"][l]
    for hf in range(2):
        for g in range(3):
            nk = (8, 8, 6)[g]
            t["wdn%d%d" % (hf, g)] = dn[g * 1024:g * 1024 + nk * 128, hf * 512:(hf + 1) * 512]
    out = np.zeros((128, W_TOT), np.float32)
    for n, k, c in W_PACK:
        o = W_OFF[n][0]
        a = t[n]
        if n in ("s5b", "s5cr"):
            out[:, o:o + k * c] = a
        else:
            out[:, o:o + k * c] = _ktile(np.ascontiguousarray(a))
    return out


def pack_small(inp, l):
    sp = np.zeros((128, NSP), np.float32)
    sp[:, SP["gba"]] = inp["gla_b_a"][l]
    sp[0:8, SP["fbf"]] = inp["fox_b_f"][l]
    sp[32:40, SP["fbf"]] = inp["fox_b_f"][l]
    sp[:, SP["s5d"]:SP["s5d"] + 2] = inp["s5_d"][l].reshape(2, 128).T
    sp[:, SP["s5bg"]:SP["s5bg"] + 2] = inp["s5_b_glu"][l].reshape(2, 128).T
    idx = np.stack([ffn_feat(ci) + np.arange(128) for ci in range(NCH_FF)], 1)
    for tap in range(3):
        sp[:, SP["cw%d" % tap]:SP["cw%d" % tap] + NCH_FF] = inp["ffn_conv_w"][l][tap][idx]
    sp[:, SP["cb"]:SP["cb"] + NCH_FF] = inp["ffn_conv_b"][l][idx]
    for nm, key in (("lre", "s5_lam_re"), ("lim", "s5_lam_im"), ("ldt", "s5_log_dt")):
        sp[:, SP[nm]:SP[nm] + 8] = inp[key][l].reshape(8, 128).T
    for h in range(4):
        sp[32 * h:32 * h + 32, SP["hm"] + h] = 1.0
    tp = np.zeros((128, NTP), np.float32)
    tp[:, TP["gng"]:TP["gng"] + 256] = np.tile(inp["gla_norm_g"][l], 4)[None, :]
    tp[:, TP["fbf"]:TP["fbf"] + 8] = inp["fox_b_f"][l][None, :]
    return sp, tp


def prep_core(inp, c, cfg, shared):
    b = c % 2
    PAST, T = cfg.PAST, cfg.T
    ss = [2 * c, 2 * c + 1]
    m = {}
    m["xp"] = np.ascontiguousarray(inp["x_prompt"][b])
    m["xs"] = np.ascontiguousarray(inp["x_sample"][ss])
    m["memT"] = _ktile(np.ascontiguousarray(inp["mem_prompt"][b].T))
    m["stgla"] = np.ascontiguousarray(inp["state_gla"][:, ss].reshape(L, 2, 128, 64))
    s5 = np.stack([inp["state_s5_re"][:, ss], inp["state_s5_im"][:, ss]], 2)
    m["sts5"] = np.ascontiguousarray(s5.reshape(L, 2, 2, 8, 128).transpose(0, 1, 2, 4, 3))
    ck = inp["cache_fox_k"][:, ss]
    m["cfk"] = np.ascontiguousarray(ck.transpose(0, 1, 4, 3, 2)).reshape(L, 2, 64, 8 * PAST)
    cv = inp["cache_fox_v"][:, ss].reshape(L, 2, PAST // 128, 128, 512)
    m["cfv"] = np.ascontiguousarray(cv.transpose(0, 1, 3, 2, 4)).reshape(L, 2, 128, (PAST // 128) * 512)
    m["cflf"] = np.ascontiguousarray(inp["cache_fox_logf"][:, ss].transpose(0, 1, 3, 2))
    mk = inp["cache_mem_k"][:, ss].reshape(L, 2, 256, 1024)
    m["cmkT"] = np.ascontiguousarray(mk.transpose(0, 1, 3, 2).reshape(L, 2, 8, 128, 256).transpose(0, 1, 3, 2, 4)).reshape(L, 2, 128, 2048)
    mv = inp["cache_mem_v"][:, ss].reshape(L, 2, 2, 128, 1024)
    m["cmv"] = np.ascontiguousarray(mv.transpose(0, 1, 3, 2, 4)).reshape(L, 2, 128, 2048)
    idx = np.stack([ffn_feat(ci) + np.arange(128) for ci in range(NCH_FF)], 1)
    cs = inp["state_ffn_conv"][:, ss]
    m["cvst"] = np.ascontiguousarray(cs[:, :, :, idx].transpose(0, 1, 3, 4, 2)).reshape(L, 2, 128, NCH_FF * 2)
    m.update(shared)
    return m


def kernel(_cfg=None, **inp):
    cfg = _cfg or Cfg()
    SEQ, PAST, T = cfg.SEQ, cfg.PAST, cfg.T
    inp = {k: np.asarray(v) for k, v in inp.items()}
    shared = {}
    shared["wpk"] = np.stack([pack_weights(inp, l) for l in range(L)])
    lnp = np.zeros((1 + 3 * L, 128, 2 * D), np.float32)
    lnp[0, :, 0:D] = inp["ln_in_g"][None]
    lnp[0, :, D:] = inp["ln_in_b"][None]
    for l in range(L):
        for j, nm in enumerate(("ln1", "ln2", "ln3")):
            lnp[1 + 3 * l + j, :, 0:D] = inp[nm + "_g"][l][None]
            lnp[1 + 3 * l + j, :, D:] = inp[nm + "_b"][l][None]
    shared["lnp"] = lnp
    sps, tps = zip(*[pack_small(inp, l) for l in range(L)])
    shared["sp"] = np.stack(sps)
    shared["tp"] = np.stack(tps)
    shared["wa2"] = np.ascontiguousarray(inp["gla_w_a2"])
    shared["cst"], shared["sel"] = make_consts(cfg)
    nc = build(cfg)
    in_maps = [prep_core(inp, c, cfg, shared) for c in range(NCORES)]
    res = run_bass_kernel_spmd(nc, in_maps, core_ids=list(range(NCORES)))
    R = res.results
    B = 2
    f32 = np.float32
    yp = np.stack([R[b]["yp"] for b in range(B)])
    ys = np.concatenate([R[c]["ys"] for c in range(NCORES)], 0)

    def stackp(key, shape):
        return np.stack([R[b][key] for b in range(B)], 1).reshape(shape)

    gla_p = stackp("glap", (L, B, 4, 32, 64))
    s5p = np.stack([R[b]["s5p"] for b in range(B)], 1)
    s5p = s5p.transpose(0, 1, 2, 4, 3).reshape(L, B, 2, 16, 64)
    fk_p = stackp("fkp", (L, B, SEQ, 8, 64))
    fv_p = stackp("fvp", (L, B, SEQ, 8, 64))
    fl_p = stackp("flp", (L, B, SEQ, 8))
    idx = np.stack([ffn_feat(ci) + np.arange(128) for ci in range(NCH_FF)], 1)

    def conv_out(a):
        a = a.reshape(a.shape[:-1] + (NCH_FF, 2))
        o = np.zeros(a.shape[:-3] + (2, 2 * DFF), f32)
        o[..., :, idx] = np.moveaxis(a, -1, -3)
        return o

    cv_p = conv_out(np.stack([R[b]["cvp"] for b in range(B)], 1))
    mk_p = stackp("mkp", (L, B, 256, 4, 256))
    mv_p = stackp("mvp", (L, B, 256, 4, 256))

    def cats(key):
        return np.concatenate([R[c][key] for c in range(NCORES)], 1)

    gla_s = cats("glas").reshape(L, 16, 4, 32, 64)
    s5s = cats("s5s").transpose(0, 1, 2, 4, 3).reshape(L, 16, 2, 16, 64)
    fk_s = cats("fks").reshape(L, 16, 64, 8, 64)
    fv_s = cats("fvs").reshape(L, 16, 64, 8, 64)
    fl_s = cats("fls").reshape(L, 16, 64, 8)
    cv_s = conv_out(cats("cvs"))
    outs = (yp, ys, gla_p, s5p[:, :, 0], s5p[:, :, 1], fk_p, fv_p, fl_p, cv_p, mk_p, mv_p,
            gla_s, s5s[:, :, 0], s5s[:, :, 1], fk_s, fv_s, fl_s, cv_s)
    return tuple(np.ascontiguousarray(o, dtype=f32) for o in outs)
```
